# Optimizing a Trainium2 kernel written in Bass

```python
import math
import jax, jax.numpy as jnp
from jax import lax
import numpy as np

D_MODEL = 1024
BATCH = 4
SEQ = 8192
DEPTH = 2
DEC_BATCH = 16
DEC_SEQ = 16
PAST_LEN = 1024

CHUNK = 64
D_MIX = D_MODEL
LRU_WIDTH = D_MIX // 4
LRU_HEADS = 4
LRU_HEAD_DIM = LRU_WIDTH // LRU_HEADS
LRU_CONV = 4
LRU_C = 8.0
RWKV_WIDTH = D_MIX // 2
RWKV_HEAD = 64
RWKV_HEADS = RWKV_WIDTH // RWKV_HEAD
RWKV_DECAY_LORA = 64
RWKV_A_LORA = 64
RWKV_GATE_LORA = 128
RWKV_COLS = 3 * RWKV_WIDTH + RWKV_DECAY_LORA + RWKV_A_LORA + RWKV_GATE_LORA
RWKV_SPLITS = (RWKV_WIDTH, 2 * RWKV_WIDTH, 3 * RWKV_WIDTH,
               3 * RWKV_WIDTH + RWKV_DECAY_LORA,
               3 * RWKV_WIDTH + RWKV_DECAY_LORA + RWKV_A_LORA)
S5_WIDTH = D_MIX - LRU_WIDTH - RWKV_WIDTH
S5_GROUP_DIM = 16
S5_GROUPS = S5_WIDTH // S5_GROUP_DIM
S5_STATE = 64
IN_COLS = 2 * LRU_WIDTH + RWKV_COLS + S5_WIDTH
IN_SPLITS = (LRU_WIDTH, 2 * LRU_WIDTH, 2 * LRU_WIDTH + RWKV_COLS)
D_FF = 2816
FFN_CONV = 3
NORM_EPS = 1e-6
RWKV_GN_EPS = 64e-5

kernel_name = 'hymba_rglru_rwkv7_s5_stream_step'


def _rmsnorm(x, g):
    xf = x.astype(jnp.float32)
    y = xf * lax.rsqrt(jnp.mean(xf * xf, axis=-1, keepdims=True) + NORM_EPS)
    return (y * g.astype(jnp.float32)).astype(x.dtype)


def _causal_dwconv(x, buf, w, b):
    width = w.shape[0]
    t = x.shape[1]
    xp = jnp.concatenate([buf.astype(x.dtype), x], axis=1)
    y = b + xp[:, 0:t] * w[0]
    for k in range(1, width):
        y = y + xp[:, k:k + t] * w[k]
    return y, xp[:, t:]


def _lin_combine(e1, e2):
    a1, b1 = e1
    a2, b2 = e2
    return a1 * a2, a2 * b1 + b2


def _complex_combine(e1, e2):
    ar1, ai1, br1, bi1 = e1
    ar2, ai2, br2, bi2 = e2
    return (ar2 * ar1 - ai2 * ai1,
            ar2 * ai1 + ai2 * ar1,
            ar2 * br1 - ai2 * bi1 + br2,
            ar2 * bi1 + ai2 * br1 + bi2)


def _rglru(u_gate, u_x, buf, h0, conv_w, conv_b, wa, ba, wx, bx, lam):
    bsz, t, _ = u_x.shape
    xc, new_buf = _causal_dwconv(u_x, buf, conv_w, conv_b)
    xf = xc.astype(jnp.float32)
    xh = xf.reshape(bsz, t, LRU_HEADS, LRU_HEAD_DIM)
    r = jax.nn.sigmoid(jnp.einsum('bthi,hij->bthj', xh, wa).reshape(bsz, t, LRU_WIDTH) + ba)
    i = jax.nn.sigmoid(jnp.einsum('bthi,hij->bthj', xh, wx).reshape(bsz, t, LRU_WIDTH) + bx)
    log_a = -LRU_C * r * jax.nn.softplus(-lam.astype(jnp.float32))
    a = jnp.exp(log_a)
    gain = jnp.sqrt(jnp.maximum(-jnp.expm1(2.0 * log_a), 0.0))
    a_cum, b_cum = lax.associative_scan(_lin_combine, (a, gain * i * xf), axis=1)
    h = a_cum * h0.astype(jnp.float32)[:, None] + b_cum
    y = h * jax.nn.gelu(u_gate.astype(jnp.float32))
    return y.astype(u_x.dtype), new_buf, h[:, -1]


def _rwkv7(p_b, shift0, s0, mu, w0, w2, a0, a2, g2, k_k, k_a, r_k, ln_w, ln_b):
    bsz, t, _ = p_b.shape
    pf = p_b.astype(jnp.float32)
    prev = jnp.concatenate([shift0.astype(jnp.float32)[:, None], pf[:, :-1]], axis=1)
    xm = pf + (prev - pf) * mu
    r, k, v, w_lo, a_lo, g_lo = jnp.split(xm, RWKV_SPLITS, axis=-1)
    log_w = -jax.nn.softplus(-(w0 + jnp.tanh(w_lo) @ w2)) - 0.5
    decay = jnp.exp(-jnp.exp(log_w))
    a = jax.nn.sigmoid(a0 + a_lo @ a2)
    g = jax.nn.sigmoid(g_lo) @ g2

    def heads(z):
        return z.reshape(bsz, t, RWKV_HEADS, RWKV_HEAD)

    kk = heads(k * k_k)
    kk = kk * lax.rsqrt(jnp.maximum(jnp.sum(kk * kk, axis=-1, keepdims=True), 1e-24))
    k = k * (1.0 + (a - 1.0) * k_a)
    rh, wh, kh, vh, ah = heads(r), heads(decay), heads(k), heads(v), heads(a)

    def step(s, inp):
        r_t, w_t, k_t, v_t, kk_t, a_t = inp
        sa = jnp.einsum('bhvk,bhk->bhv', s, kk_t)
        s = (s * w_t[:, :, None, :]
             - sa[..., None] * (kk_t * a_t)[:, :, None, :]
             + v_t[..., None] * k_t[:, :, None, :])
        return s, jnp.einsum('bhvk,bhk->bhv', s, r_t)

    xs = tuple(jnp.moveaxis(z, 1, 0) for z in (rh, wh, kh, vh, kk, ah))
    s_last, y = lax.scan(step, s0.astype(jnp.float32), xs)
    y = jnp.moveaxis(y, 0, 1)
    mean = jnp.mean(y, axis=-1, keepdims=True)
    var = jnp.mean(jnp.square(y - mean), axis=-1, keepdims=True)
    yn = ((y - mean) * lax.rsqrt(var + RWKV_GN_EPS)).reshape(bsz, t, RWKV_WIDTH) * ln_w + ln_b
    bonus = (jnp.sum(rh * kh * r_k, axis=-1, keepdims=True) * vh).reshape(bsz, t, RWKV_WIDTH)
    out = (yn + bonus) * g
    return out.astype(p_b.dtype), p_b[:, -1], s_last


def _s5(u, s_re0, s_im0, a_re, a_im, b_re, b_im, c_re, c_im, d, log_dt, glu_w, glu_b):
    bsz, t, _ = u.shape
    uf = u.astype(jnp.float32)
    ug = uf.reshape(bsz, t, S5_GROUPS, S5_GROUP_DIM)
    a_re = a_re.astype(jnp.float32)
    a_im = a_im.astype(jnp.float32)
    dt = jnp.exp(log_dt.astype(jnp.float32))[:, None]
    mag = jnp.exp(dt * a_re)
    abar_re = mag * jnp.cos(dt * a_im)
    abar_im = mag * jnp.sin(dt * a_im)
    den = a_re * a_re + a_im * a_im
    fr = ((abar_re - 1.0) * a_re + abar_im * a_im) / den
    fi = (abar_im * a_re - (abar_re - 1.0) * a_im) / den
    bbar_re = fr[..., None] * b_re - fi[..., None] * b_im
    bbar_im = fr[..., None] * b_im + fi[..., None] * b_re
    bu_re = jnp.einsum('btgi,gpi->btgp', ug, bbar_re)
    bu_im = jnp.einsum('btgi,gpi->btgp', ug, bbar_im)
    ar = jnp.broadcast_to(abar_re, bu_re.shape)
    ai = jnp.broadcast_to(abar_im, bu_im.shape)
    acr, aci, xr, xi = lax.associative_scan(_complex_combine, (ar, ai, bu_re, bu_im), axis=1)
    h0r = s_re0.astype(jnp.float32)[:, None]
    h0i = s_im0.astype(jnp.float32)[:, None]
    h_re = acr * h0r - aci * h0i + xr
    h_im = acr * h0i + aci * h0r + xi
    y = jnp.einsum('btgp,gop->btgo', h_re, c_re) - jnp.einsum('btgp,gop->btgo', h_im, c_im)
    y = y.reshape(bsz, t, S5_WIDTH) + d * uf
    z = jax.nn.gelu(y)
    out = z * jax.nn.sigmoid(z @ glu_w + glu_b)
    return out.astype(u.dtype), h_re[:, -1], h_im[:, -1]


def _conv_ffn(h, buf, w_up, conv_w, conv_b, w_down):
    up = h @ w_up
    up, new_buf = _causal_dwconv(up, buf, conv_w, conv_b)
    val, gate = jnp.split(up, 2, axis=-1)
    return (val * jax.nn.silu(gate)) @ w_down, new_buf


def _layer(x, c, st, p, l):
    lru_buf, lru_h, rw_shift, rw_s, s5_re, s5_im, ffn_buf = st
    mod = jax.nn.silu(c) @ p['w_ada'][l] + p['b_ada'][l]
    sh1, sc1, g1, sh2, sc2, g2 = jnp.split(mod[:, None], 6, axis=-1)
    h = _rmsnorm(x, p['norm_mix'][l]) * (1.0 + sc1) + sh1
    proj = h @ p['w_in'][l]
    u_gate, u_lru, p_rwkv, u_s5 = jnp.split(proj, IN_SPLITS, axis=-1)
    y_a, lru_buf, lru_h = _rglru(u_gate, u_lru, lru_buf, lru_h, p['lru_conv_w'][l], p['lru_conv_b'][l],
                                 p['lru_wa'][l], p['lru_ba'][l], p['lru_wx'][l], p['lru_bx'][l],
                                 p['lru_lambda'][l])
    y_b, rw_shift, rw_s = _rwkv7(p_rwkv, rw_shift, rw_s, p['rwkv_mu'][l], p['rwkv_w0'][l], p['rwkv_w2'][l],
                                 p['rwkv_a0'][l], p['rwkv_a2'][l], p['rwkv_g2'][l], p['rwkv_k_k'][l],
                                 p['rwkv_k_a'][l], p['rwkv_r_k'][l], p['rwkv_ln_w'][l], p['rwkv_ln_b'][l])
    y_c, s5_re, s5_im = _s5(u_s5, s5_re, s5_im, p['s5_a_re'][l], p['s5_a_im'][l], p['s5_b_re'][l],
                            p['s5_b_im'][l], p['s5_c_re'][l], p['s5_c_im'][l], p['s5_d'][l],
                            p['s5_log_dt'][l], p['s5_glu_w'][l], p['s5_glu_b'][l])
    mix = jnp.concatenate([y_a, y_b, y_c], axis=-1) @ p['w_out'][l]
    x = x + g1 * mix
    h2 = _rmsnorm(x, p['norm_ffn'][l]) * (1.0 + sc2) + sh2
    f, ffn_buf = _conv_ffn(h2, ffn_buf, p['ffn_up'][l], p['ffn_conv_w'][l], p['ffn_conv_b'][l], p['ffn_down'][l])
    x = x + g2 * f
    return x, (lru_buf, lru_h, rw_shift, rw_s, s5_re, s5_im, ffn_buf)


def _trunk(x, c, states, p):
    new = []
    for l in range(DEPTH):
        x, st = _layer(x, c, tuple(s[l] for s in states), p, l)
        new.append(st)
    stacked = tuple(jnp.stack([n[i] for n in new], axis=0) for i in range(len(states)))
    return _rmsnorm(x, p['norm_final']), stacked


def _zero_states(bsz, dtype):
    return (jnp.zeros((DEPTH, bsz, LRU_CONV - 1, LRU_WIDTH), dtype),
            jnp.zeros((DEPTH, bsz, LRU_WIDTH), dtype),
            jnp.zeros((DEPTH, bsz, RWKV_COLS), dtype),
            jnp.zeros((DEPTH, bsz, RWKV_HEADS, RWKV_HEAD, RWKV_HEAD), dtype),
            jnp.zeros((DEPTH, bsz, S5_GROUPS, S5_STATE), dtype),
            jnp.zeros((DEPTH, bsz, S5_GROUPS, S5_STATE), dtype),
            jnp.zeros((DEPTH, bsz, FFN_CONV - 1, 2 * D_FF), dtype))


def setup_inputs(seed: int = 0) -> dict:
    key = jax.random.key(seed)
    keys = jax.random.split(key, 64)
    counter = [0]

    def nxt():
        counter[0] += 1
        return keys[counter[0] - 1]

    def nrm(shape, scale):
        return scale * jax.random.normal(nxt(), shape, jnp.float32)

    def uni(shape, lo, hi):
        return jax.random.uniform(nxt(), shape, jnp.float32, lo, hi)

    L = DEPTH
    lru_u = uni((L, LRU_WIDTH), 0.9, 0.999)
    lru_a = jnp.exp(jnp.log(lru_u) / LRU_C)
    lru_lambda = jnp.log(lru_a) - jnp.log1p(-lru_a)
    n_idx = jnp.arange(S5_STATE, dtype=jnp.float32)
    return {
        'x_prompt': nrm((BATCH, SEQ, D_MODEL), 1.0),
        'x_sample': nrm((DEC_BATCH, DEC_SEQ, D_MODEL), 1.0),
        'c_prompt': nrm((BATCH, D_MODEL), 1.0),
        'c_sample': nrm((DEC_BATCH, D_MODEL), 1.0),
        'state_lru_conv': nrm((L, DEC_BATCH, LRU_CONV - 1, LRU_WIDTH), 1.0),
        'state_lru_h': nrm((L, DEC_BATCH, LRU_WIDTH), 0.5),
        'state_rwkv_shift': nrm((L, DEC_BATCH, RWKV_COLS), 1.0),
        'state_rwkv_S': nrm((L, DEC_BATCH, RWKV_HEADS, RWKV_HEAD, RWKV_HEAD), 0.5),
        'state_s5_re': nrm((L, DEC_BATCH, S5_GROUPS, S5_STATE), 0.5),
        'state_s5_im': nrm((L, DEC_BATCH, S5_GROUPS, S5_STATE), 0.5),
        'state_ffn_conv': nrm((L, DEC_BATCH, FFN_CONV - 1, 2 * D_FF), 1.0),
        'w_ada': nrm((L, D_MODEL, 6 * D_MODEL), 0.5 * D_MODEL ** -0.5),
        'b_ada': nrm((L, 6 * D_MODEL), 0.05),
        'norm_mix': 1.0 + nrm((L, D_MODEL), 0.1),
        'norm_ffn': 1.0 + nrm((L, D_MODEL), 0.1),
        'w_in': nrm((L, D_MODEL, IN_COLS), D_MODEL ** -0.5),
        'w_out': nrm((L, D_MIX, D_MODEL), D_MIX ** -0.5),
        'lru_conv_w': nrm((L, LRU_CONV, LRU_WIDTH), 0.5),
        'lru_conv_b': nrm((L, LRU_WIDTH), 0.05),
        'lru_wa': nrm((L, LRU_HEADS, LRU_HEAD_DIM, LRU_HEAD_DIM), LRU_HEAD_DIM ** -0.5),
        'lru_ba': nrm((L, LRU_WIDTH), 0.1),
        'lru_wx': nrm((L, LRU_HEADS, LRU_HEAD_DIM, LRU_HEAD_DIM), LRU_HEAD_DIM ** -0.5),
        'lru_bx': nrm((L, LRU_WIDTH), 0.1),
        'lru_lambda': lru_lambda,
        'rwkv_mu': uni((L, RWKV_COLS), 0.0, 1.0),
        'rwkv_w0': uni((L, RWKV_WIDTH), -6.5, -1.5),
        'rwkv_w2': nrm((L, RWKV_DECAY_LORA, RWKV_WIDTH), 0.1),
        'rwkv_a0': nrm((L, RWKV_WIDTH), 0.1),
        'rwkv_a2': nrm((L, RWKV_A_LORA, RWKV_WIDTH), 0.5 * RWKV_A_LORA ** -0.5),
        'rwkv_g2': nrm((L, RWKV_GATE_LORA, RWKV_WIDTH), RWKV_GATE_LORA ** -0.5),
        'rwkv_k_k': 0.85 + nrm((L, RWKV_WIDTH), 0.1),
        'rwkv_k_a': 1.0 + nrm((L, RWKV_WIDTH), 0.1),
        'rwkv_r_k': nrm((L, RWKV_HEADS, RWKV_HEAD), 0.1),
        'rwkv_ln_w': 1.0 + nrm((L, RWKV_WIDTH), 0.1),
        'rwkv_ln_b': nrm((L, RWKV_WIDTH), 0.05),
        's5_a_re': -0.5 + nrm((L, S5_GROUPS, S5_STATE), 0.01),
        's5_a_im': jnp.pi * n_idx + nrm((L, S5_GROUPS, S5_STATE), 0.01),
        's5_b_re': nrm((L, S5_GROUPS, S5_STATE, S5_GROUP_DIM), (2.0 * S5_GROUP_DIM) ** -0.5),
        's5_b_im': nrm((L, S5_GROUPS, S5_STATE, S5_GROUP_DIM), (2.0 * S5_GROUP_DIM) ** -0.5),
        's5_c_re': nrm((L, S5_GROUPS, S5_GROUP_DIM, S5_STATE), (2.0 * S5_STATE) ** -0.5),
        's5_c_im': nrm((L, S5_GROUPS, S5_GROUP_DIM, S5_STATE), (2.0 * S5_STATE) ** -0.5),
        's5_d': nrm((L, S5_WIDTH), 0.5),
        's5_log_dt': uni((L, S5_GROUPS), math.log(0.001), math.log(0.1)),
        's5_glu_w': nrm((L, S5_WIDTH, S5_WIDTH), S5_WIDTH ** -0.5),
        's5_glu_b': nrm((L, S5_WIDTH), 0.05),
        'ffn_up': nrm((L, D_MODEL, 2 * D_FF), D_MODEL ** -0.5),
        'ffn_conv_w': nrm((L, FFN_CONV, 2 * D_FF), 0.5),
        'ffn_conv_b': nrm((L, 2 * D_FF), 0.05),
        'ffn_down': nrm((L, D_FF, D_MODEL), D_FF ** -0.5),
        'norm_final': 1.0 + nrm((D_MODEL,), 0.1),
    }


def reference(x_prompt, x_sample, c_prompt, c_sample,
              state_lru_conv, state_lru_h, state_rwkv_shift, state_rwkv_S,
              state_s5_re, state_s5_im, state_ffn_conv,
              w_ada, b_ada, norm_mix, norm_ffn, w_in, w_out,
              lru_conv_w, lru_conv_b, lru_wa, lru_ba, lru_wx, lru_bx, lru_lambda,
              rwkv_mu, rwkv_w0, rwkv_w2, rwkv_a0, rwkv_a2, rwkv_g2, rwkv_k_k, rwkv_k_a, rwkv_r_k,
              rwkv_ln_w, rwkv_ln_b,
              s5_a_re, s5_a_im, s5_b_re, s5_b_im, s5_c_re, s5_c_im, s5_d, s5_log_dt, s5_glu_w, s5_glu_b,
              ffn_up, ffn_conv_w, ffn_conv_b, ffn_down, norm_final):
    p = dict(w_ada=w_ada, b_ada=b_ada, norm_mix=norm_mix, norm_ffn=norm_ffn, w_in=w_in, w_out=w_out,
             lru_conv_w=lru_conv_w, lru_conv_b=lru_conv_b, lru_wa=lru_wa, lru_ba=lru_ba,
             lru_wx=lru_wx, lru_bx=lru_bx, lru_lambda=lru_lambda,
             rwkv_mu=rwkv_mu, rwkv_w0=rwkv_w0, rwkv_w2=rwkv_w2, rwkv_a0=rwkv_a0, rwkv_a2=rwkv_a2,
             rwkv_g2=rwkv_g2, rwkv_k_k=rwkv_k_k, rwkv_k_a=rwkv_k_a, rwkv_r_k=rwkv_r_k,
             rwkv_ln_w=rwkv_ln_w, rwkv_ln_b=rwkv_ln_b,
             s5_a_re=s5_a_re, s5_a_im=s5_a_im, s5_b_re=s5_b_re, s5_b_im=s5_b_im,
             s5_c_re=s5_c_re, s5_c_im=s5_c_im, s5_d=s5_d, s5_log_dt=s5_log_dt,
             s5_glu_w=s5_glu_w, s5_glu_b=s5_glu_b,
             ffn_up=ffn_up, ffn_conv_w=ffn_conv_w, ffn_conv_b=ffn_conv_b, ffn_down=ffn_down,
             norm_final=norm_final)
    y_prompt, p_states = _trunk(x_prompt, c_prompt, _zero_states(x_prompt.shape[0], x_prompt.dtype), p)
    s_in = (state_lru_conv, state_lru_h, state_rwkv_shift, state_rwkv_S, state_s5_re, state_s5_im, state_ffn_conv)
    y_sample, s_states = _trunk(x_sample, c_sample, s_in, p)
    p_lru_conv, p_lru_h, p_rwkv_shift, p_rwkv_S, p_s5_re, p_s5_im, p_ffn_conv = p_states
    s_lru_conv, s_lru_h, s_rwkv_shift, s_rwkv_S, s_s5_re, s_s5_im, s_ffn_conv = s_states
    return (y_prompt, y_sample,
            p_lru_conv, p_lru_h, p_rwkv_shift, p_rwkv_S, p_s5_re, p_s5_im, p_ffn_conv,
            s_lru_conv, s_lru_h, s_rwkv_shift, s_rwkv_S, s_s5_re, s_s5_im, s_ffn_conv)
```

```python
import bisect
import contextlib
import math
import numpy as np
import concourse.bass as bass
import concourse.mybir as mybir
from concourse.bass_utils import run_bass_kernel_spmd

F32 = mybir.dt.float32
F32R = mybir.dt.float32r
USE_R = True


def RR(ap):
    return ap.bitcast(F32R) if USE_R else ap
ALU = mybir.AluOpType
AF = mybir.ActivationFunctionType

ENG_ATTR = {'pe': 'tensor', 'act': 'scalar', 'dve': 'vector', 'pool': 'gpsimd', 'sp': 'sync'}
ERA = 30000
NSLOT = 8


class _Rec:
    __slots__ = ('w', 'r')

    def __init__(self, w=None, r=None):
        self.w = w
        self.r = dict(r) if r else {}


class _IMap:
    def __init__(self):
        self.b = [0, 1 << 60]
        self.rec = [_Rec()]

    def _split(self, x):
        i = bisect.bisect_right(self.b, x) - 1
        if self.b[i] != x:
            self.b.insert(i + 1, x)
            old = self.rec[i]
            self.rec.insert(i + 1, _Rec(old.w, old.r))

    def rng(self, lo, hi):
        self._split(lo)
        self._split(hi)
        i0 = bisect.bisect_left(self.b, lo)
        i1 = bisect.bisect_left(self.b, hi)
        return self.rec[i0:i1]


def _is_dram(ap):
    try:
        return 'DRam' in type(ap.tensor).__name__
    except Exception:
        return False


def ap_key(ap):
    pat = ap.ap
    pstride = pat[0][0]
    off = ap.offset
    col = off % pstride if pstride > 0 else off
    ext = 0
    for st, cnt in pat[1:]:
        ext += abs(st) * (cnt - 1)
    if ap.tensor.name.startswith('P'):
        return ap.tensor.name, 0, 512
    return ap.tensor.name, col, col + ext + 1


class Sched:
    def __init__(self, nc):
        self.nc = nc
        self.prog = {e: [] for e in ENG_ATTR}
        self.cnt = {e: 0 for e in ENG_ATTR}
        self.seen = {e: {} for e in ENG_ATTR}
        self.maps = {}
        self.semkeys = []
        self.semset = set()
        self.dma_n = {e: 0 for e in ENG_ATTR}
        self.slot_val = {}

    def _sem(self, key):
        if key not in self.semset:
            self.semset.add(key)
            self.semkeys.append(key)
        return key

    def _wait(self, eng, tok):
        key, val = tok
        if self.seen[eng].get(key, 0) >= val:
            return
        self.seen[eng][key] = val
        self.prog[eng].append(('wait', key, val))

    def _deps(self, eng, reads, writes, tok, pseudo):
        for ap in reads:
            name, lo, hi = ap_key(ap)
            m = self.maps.setdefault(name, _IMap())
            for rec in m.rng(lo, hi):
                if rec.w is not None and not (rec.w[2] == pseudo and pseudo == 'pe'):
                    self._wait(eng, rec.w[:2])
        for ap in writes:
            name, lo, hi = ap_key(ap)
            m = self.maps.setdefault(name, _IMap())
            for rec in m.rng(lo, hi):
                if rec.w is not None and rec.w[2] != pseudo:
                    self._wait(eng, rec.w[:2])
                for (k, e2), v in rec.r.items():
                    if e2 != pseudo:
                        self._wait(eng, (k, v))
        for ap in reads:
            name, lo, hi = ap_key(ap)
            for rec in self.maps[name].rng(lo, hi):
                rec.r[(tok[0], pseudo)] = tok[1]
        for ap in writes:
            name, lo, hi = ap_key(ap)
            for rec in self.maps[name].rng(lo, hi):
                rec.w = (tok[0], tok[1], pseudo)
                rec.r = {}

    def op(self, eng, fn, reads=(), writes=()):
        reads = [r for r in reads if r is not None and hasattr(r, 'tensor') and not _is_dram(r)]
        writes = [w for w in writes if not _is_dram(w)]
        n = self.cnt[eng] + 1
        self.cnt[eng] = n
        era, v = divmod(n - 1, ERA)
        key = self._sem((eng, era))
        tok = (key, v + 1)
        self._deps(eng, reads, writes, tok, eng)
        self.prog[eng].append(('op', fn, key))
        return tok

    def dma(self, out, in_, eng='sp'):
        j = self.dma_n[eng]
        self.dma_n[eng] = j + 1
        slot = j % NSLOT
        key = self._sem(('dma', eng, slot))
        prev = self.slot_val.get(key, 0)
        if prev > 0:
            self._wait(eng, (key, prev))
        tok = (key, prev + 16)
        self.slot_val[key] = prev + 16
        reads = [] if _is_dram(in_) else [in_]
        writes = [] if _is_dram(out) else [out]
        self._deps(eng, reads, writes, tok, 'dma_%s_%d_%d' % (eng, slot, j))
        self.prog[eng].append(('dma', out, in_, key))
        return tok

    def finish(self, eng='sp'):
        for key, val in self.slot_val.items():
            self._wait(eng, (key, val))

    def emit(self, es):
        nc = self.nc
        sems = {}
        for key in self.semkeys:
            sems[key] = es.enter_context(nc.semaphore("s_" + "_".join(str(x) for x in key)))
        block = es.enter_context(nc.Block())

        def run(eng_name):
            def body(e):
                for item in self.prog[eng_name]:
                    if item[0] == 'wait':
                        e.wait_ge(sems[item[1]], item[2])
                    elif item[0] == 'op':
                        item[1](e).then_inc(sems[item[2]], 1)
                    else:
                        e.dma_start(out=item[1], in_=item[2]).then_inc(sems[item[3]], 16)
            return body

        block.tensor(run('pe'))
        block.scalar(run('act'))
        block.vector(run('dve'))
        block.gpsimd(run('pool'))
        block.sync(run('sp'))


D = 1024
L = 2
NTT = 512
CH = 128
SB = 64
DFF = 2816
O_LH, O_Lh, O_SH, O_S5R, O_S5I, O_FH, NSTO = 0, 18, 24, 66, 90, 114, 378
V_NM, V_NF, V_BADA, V_LCW, V_LCB, V_LBA, V_LBX, V_LLAM = 0, 8, 16, 64, 72, 74, 76, 78
V_MU, V_W0, V_A0, V_KK, V_KA, V_RK, V_LNW, V_LNB = 80, 94, 98, 102, 106, 110, 114, 118
V_S5AR, V_S5AI, V_S5DT, V_S5D, V_GLUB, V_FCW, V_FCB, V_NFIN, NV = 122, 130, 138, 146, 148, 150, 282, 326, 334
W_WA, W_WX, W_GLU, W_BRE, W_BIM, W_CRE, W_CIM, NSW = 0, 256, 512, 1024, 1536, 2048, 2560, 3072
W_LORA, W_G2, NSWB = 0, 512, 1024
C_ID, C_BONES, C_ONES, C_MSU, C_MSL, C_MIU, C_RESET, NCONST = 0, 128, 256, 384, 512, 640, 768, 1280


def build(ntp, dbg=False, stage=99, sub=99, bits=255):
    TP = ntp * NTT
    nc = bass.Bass("TRN2", target_bir_lowering=False)

    def din(name, shape):
        return nc.dram_tensor(name, list(shape), F32, kind="ExternalInput").ap()

    def dout(name, shape):
        return nc.dram_tensor(name, list(shape), F32, kind="ExternalOutput").ap()

    xp = din("xp", [8, 128, TP])
    xs = din("xs", [8, 128, 32])
    cT = din("cT", [128, 32])
    vecs_d = din("vecs", [L, 128, NV])
    smallw_d = din("smallw", [L, 128, NSW])
    smallwB_d = din("smallwB", [L, 128, NSWB])
    consts_d = din("consts", [128, NCONST])
    win_d = din("win", [L, 10, 128, 2048])
    wout_d = din("wout", [L, 4, 128, 2048])
    wup_d = din("wup", [L, 22, 128, 2048])
    wdn_d = din("wdn", [L, 2, 8, 128, 1408])
    wada_d = din("wada", [L, 24, 128, 2048])
    st_in = din("st_in", [128, L * NSTO])
    sT_in = din("sT_in", [128, L * 3 * 256])
    y_d = dout("y", [8, 128, TP])
    ys_d = dout("ys", [8, 128, 32])
    st_out = dout("st_out", [128, L * NSTO])
    sT_out = dout("sT_out", [128, L * 3 * 256])
    dbg_d = dout("dbg", [128, 8192]) if dbg else None

    S = Sched(nc)
    es = contextlib.ExitStack()
    with es:
        NA = 37820
        A = es.enter_context(nc.sbuf_tensor("A", [128, NA], F32))
        WR = es.enter_context(nc.sbuf_tensor("WR", [128, 4096], F32))
        HYR = es.enter_context(nc.sbuf_tensor("HYR", [128, 8 * NTT], F32))
        ACTR = es.enter_context(nc.sbuf_tensor("ACTR", [128, 11 * NTT], F32))
        CHR = es.enter_context(nc.sbuf_tensor("CHR", [128, 1280], F32))
        PS = [es.enter_context(nc.psum_tensor("P%d" % i, [128, 512], F32)) for i in range(8)]
        cur = [0]

        def alloc(n):
            o = cur[0]
            cur[0] += n
            assert cur[0] <= NA, cur[0]
            return o

        def V(o, n):
            return A[:, o:o + n]

        psn = [0]

        def psum():
            b = PS[psn[0] % 6]
            psn[0] += 1
            return b

        def mm(out, lhsT, rhs, start=True, stop=True):
            S.op('pe', lambda e: e.matmul(out, lhsT=lhsT, rhs=rhs, start=start, stop=stop),
                 reads=[lhsT, rhs], writes=[out])

        def tr(out, in_, ident):
            S.op('pe', lambda e: e.transpose(out, in_, ident), reads=[in_, ident], writes=[out])

        def act(out, in_, func, bias=None, scale=1.0):
            kw = {}
            if bias is not None:
                kw['bias'] = bias
            S.op('act', lambda e: e.activation(out=out, in_=in_, func=func, scale=scale, **kw),
                 reads=[in_, bias, scale], writes=[out])

        def tt(eng, out, a, b, op):
            S.op(eng, lambda e: e.tensor_tensor(out=out, in0=a, in1=b, op=op), reads=[a, b], writes=[out])

        def ts(eng, out, a, s1, op0, s2=None, op1=None):
            if op1 is None:
                S.op(eng, lambda e: e.tensor_scalar(out=out, in0=a, scalar1=s1, scalar2=None, op0=op0),
                     reads=[a, s1], writes=[out])
            else:
                S.op(eng, lambda e: e.tensor_scalar(out=out, in0=a, scalar1=s1, scalar2=s2, op0=op0, op1=op1),
                     reads=[a, s1, s2], writes=[out])

        def stt(out, in0, scalar, in1, op0, op1):
            S.op('dve', lambda e: e.scalar_tensor_tensor(out=out, in0=in0, scalar=scalar, in1=in1, op0=op0, op1=op1),
                 reads=[in0, scalar, in1], writes=[out])

        def scan(out, d0, d1, init, op0=ALU.mult, op1=ALU.add):
            S.op('dve', lambda e: e.tensor_tensor_scan(out=out, data0=d0, data1=d1, initial=init, op0=op0, op1=op1),
                 reads=[d0, d1, init], writes=[out])

        def cp(eng, out, in_):
            if eng == 'act':
                act(out, in_, AF.Copy)
            else:
                S.op(eng, lambda e: e.tensor_copy(out=out, in_=in_), reads=[in_], writes=[out])

        def memset(eng, out, val):
            S.op(eng, lambda e: e.memset(out, val), writes=[out])

        def recip(out, in_):
            S.op('dve', lambda e: e.reciprocal(out=out, in_=in_), reads=[in_], writes=[out])

        dbgcol = [0]

        def dump(ap, n):
            if dbg_d is not None and dbgcol[0] + n <= 8192:
                S.dma(dbg_d[0:ap.shape[0], dbgcol[0]:dbgcol[0] + n], ap)
                dbgcol[0] += n

        o_const = alloc(NCONST)
        CON = V(o_const, NCONST)
        ident = A[:, o_const + C_ID:o_const + C_ID + 128]
        bones = A[:, o_const + C_BONES:o_const + C_BONES + 128]
        ones = A[:, o_const + C_ONES:o_const + C_ONES + 128]
        o_vec = alloc(L * NV)
        o_cT = alloc(32)
        o_mod = alloc(L * 48 * 4)
        o_sc = alloc(L * 2 * 8 * 4)
        o_st = alloc(L * NSTO)
        o_sT = alloc(L * 3 * 256)
        o_misc = alloc(64)
        o_lrusp = alloc(L * 2)
        o_lrusp2 = alloc(L * 2)
        o_omka = alloc(L * 4)
        o_s5rho = alloc(L * 8)
        o_s5t = alloc(L * 4 * 8 * SB)
        o_smw = alloc(NSW)
        o_stg = alloc(2048)
        o_X = alloc(8 * NTT)
        o_R1 = alloc(10400)
        o_R2 = alloc(9216)

        def vec(l, c, n=1):
            return A[:, o_vec + l * NV + c:o_vec + l * NV + c + n]

        def stc(l, c, n=1):
            return A[:, o_st + l * NSTO + c:o_st + l * NSTO + c + n]

        def s5tab(l, which, st_):
            o = o_s5t + ((l * 4 + which) * 8 + st_) * SB
            return A[:, o:o + SB]

        EPS = A[:, o_misc:o_misc + 1]
        GNEPS = A[:, o_misc + 1:o_misc + 2]
        ONEC = A[:, o_misc + 2:o_misc + 3]
        TINY = A[:, o_misc + 3:o_misc + 4]

        S.dma(CON, consts_d)
        S.dma(V(o_vec, L * NV).rearrange("p (l n) -> p l n", l=L), vecs_d.rearrange("l p n -> p l n"))
        S.dma(V(o_cT, 32), cT)
        S.dma(V(o_st, L * NSTO), st_in)
        S.dma(V(o_sT, L * 768), sT_in)
        memset('pool', EPS, 1e-6)
        memset('pool', GNEPS, 64e-5)
        memset('pool', ONEC, 1.0)
        memset('pool', TINY, 1e-24)

        slab_n = [0]

        def load_slab(src, n=2048, rnd=True):
            if not rnd:
                S.dma(V(o_stg, n), src)
                return None
            o = (slab_n[0] % 2) * 2048
            slab_n[0] += 1
            q = n // 4
            for i in range(4):
                S.dma(V(o_stg + i * q, q), src[:, i * q:(i + 1) * q])
                cp('pool', RR(WR[:, o + i * q:o + (i + 1) * q]), V(o_stg + i * q, q))
            return o

        sc_ = V(o_cT, 32)
        sig = V(o_R2, 32)
        act(sig, sc_, AF.Sigmoid)
        tt('pool', sc_, sc_, sig, ALU.mult)
        for l in range(L):
            for sl in range(24):
                load_slab(wada_d[l, sl], rnd=False)
                o = o_stg
                ps = psum()
                for m in range(2):
                    for k in range(8):
                        mm(ps[:, m * 4:m * 4 + 4], A[:, o + k * 256 + m * 128:o + k * 256 + m * 128 + 128],
                           A[:, o_cT + k * 4:o_cT + k * 4 + 4], start=(k == 0), stop=(k == 7))
                for m in range(2):
                    mt = sl * 2 + m
                    act(A[:, o_mod + (l * 48 + mt) * 4:o_mod + (l * 48 + mt) * 4 + 4], ps[:, m * 4:m * 4 + 4],
                        AF.Identity, bias=vec(l, V_BADA + mt))

        def modc(l, j, ct, seq):
            o = o_mod + (l * 48 + j * 8 + ct) * 4 + seq
            return A[:, o:o + 1]

        def nsc(l, which, ct, seq):
            o = o_sc + ((l * 2 + which) * 8 + ct) * 4 + seq
            return A[:, o:o + 1]

        for l in range(L):
            for which, (jj, vv) in enumerate(((1, V_NM), (4, V_NF))):
                for ct in range(8):
                    o = o_sc + ((l * 2 + which) * 8 + ct) * 4
                    om = o_mod + (l * 48 + jj * 8 + ct) * 4
                    ts('pool', A[:, o:o + 4], A[:, om:om + 4], ONEC, ALU.add, vec(l, vv + ct), ALU.mult)

        for l in range(L):
            t = V(o_R2, 2)
            act(t, vec(l, V_LLAM, 2), AF.Exp, scale=-1.0)
            act(t, t, AF.Ln, bias=ONEC)
            ts('pool', A[:, o_lrusp + l * 2:o_lrusp + l * 2 + 2], t, -8.0, ALU.mult)
            ts('pool', A[:, o_lrusp2 + l * 2:o_lrusp2 + l * 2 + 2], t, -16.0, ALU.mult)
            ts('pool', A[:, o_omka + l * 4:o_omka + l * 4 + 4], vec(l, V_KA, 4), -1.0, ALU.mult, 1.0, ALU.add)
            W8 = [V(o_R2 + 16 + 8 * i, 8) for i in range(24)]
            dt, mag, th, tq, ti, fr_, cs_, sn_ = W8[0:8]
            act(dt, vec(l, V_S5DT, 8), AF.Exp)
            tt('pool', mag, dt, vec(l, V_S5AR, 8), ALU.mult)
            act(mag, mag, AF.Exp)
            cp('pool', A[:, o_s5rho + l * 8:o_s5rho + l * 8 + 8], mag)
            tt('pool', th, dt, vec(l, V_S5AI, 8), ALU.mult)
            for (dst, offs) in ((sn_, 0.5), (cs_, 0.75)):
                ts('dve', tq, th, 1.0 / (2 * math.pi), ALU.mult, offs, ALU.add)
                tq_i = tq.bitcast(mybir.dt.int32)
                S.op('dve', lambda e, a=ti.bitcast(mybir.dt.int32), b=tq: e.tensor_copy(out=a, in_=b),
                     reads=[tq], writes=[ti])
                S.op('dve', lambda e, a=fr_, b=ti.bitcast(mybir.dt.int32): e.tensor_copy(out=a, in_=b),
                     reads=[ti], writes=[fr_])
                tt('dve', fr_, tq, fr_, ALU.subtract)
                ts('dve', ti, fr_, 0.0, ALU.is_lt)
                tt('dve', fr_, fr_, ti, ALU.add)
                ts('dve', fr_, fr_, 2 * math.pi, ALU.mult, -math.pi, ALU.add)
                ts('dve', fr_, fr_, math.pi, ALU.min, -math.pi, ALU.max)
                act(dst, fr_, AF.Sin)
            abr, abi, den, frr, fii, t1, t2, t3 = W8[8:16]
            tt('pool', abr, mag, cs_, ALU.mult)
            tt('pool', abi, mag, sn_, ALU.mult)
            are, aim = vec(l, V_S5AR, 8), vec(l, V_S5AI, 8)
            tt('pool', t1, are, are, ALU.mult)
            tt('pool', t2, aim, aim, ALU.mult)
            tt('pool', den, t1, t2, ALU.add)
            recip(den, den)
            ts('pool', t3, abr, -1.0, ALU.add)
            tt('pool', t1, t3, are, ALU.mult)
            tt('pool', t2, abi, aim, ALU.mult)
            tt('pool', t1, t1, t2, ALU.add)
            tt('pool', frr, t1, den, ALU.mult)
            tt('pool', t1, abi, are, ALU.mult)
            tt('pool', t2, t3, aim, ALU.mult)
            tt('pool', t1, t1, t2, ALU.subtract)
            tt('pool', fii, t1, den, ALU.mult)
            for st_ in range(8):
                Er, Ei = s5tab(l, 2, st_), s5tab(l, 3, st_)
                cp('pool', Er[:, 0:1], cs_[:, st_:st_ + 1])
                cp('pool', Ei[:, 0:1], sn_[:, st_:st_ + 1])
                n = 1
                tmpa, tmpb = V(o_R2 + 256, SB), V(o_R2 + 256 + SB, SB)
                while n < SB:
                    cr, ci = Er[:, n - 1:n], Ei[:, n - 1:n]
                    ts('dve', tmpa[:, 0:n], Er[:, 0:n], cr, ALU.mult)
                    ts('dve', tmpb[:, 0:n], Ei[:, 0:n], ci, ALU.mult)
                    tt('dve', Er[:, n:2 * n], tmpa[:, 0:n], tmpb[:, 0:n], ALU.subtract)
                    ts('dve', tmpa[:, 0:n], Er[:, 0:n], ci, ALU.mult)
                    ts('dve', tmpb[:, 0:n], Ei[:, 0:n], cr, ALU.mult)
                    tt('dve', Ei[:, n:2 * n], tmpa[:, 0:n], tmpb[:, 0:n], ALU.add)
                    n *= 2
                Epr, Epi = s5tab(l, 0, st_), s5tab(l, 1, st_)
                fr1, fi1 = frr[:, st_:st_ + 1], fii[:, st_:st_ + 1]
                ts('dve', tmpa, Er, fr1, ALU.mult)
                ts('dve', tmpb, Ei, fi1, ALU.mult)
                tt('dve', Epr, tmpa, tmpb, ALU.add)
                ts('dve', tmpa, Er, fi1, ALU.mult)
                ts('dve', tmpb, Ei, fr1, ALU.mult)
                tt('dve', Epi, tmpa, tmpb, ALU.subtract)

        Xb = [V(o_X + ct * NTT, NTT) for ct in range(8)]
        HYb = [HYR[:, ct * NTT:(ct + 1) * NTT] for ct in range(8)]
        FINb = [V(o_R1 + ct * NTT, NTT) for ct in range(8)]

        def rmsnorm_mod(l, which, NT, segs, final=False):
            ps = psum()
            for ct in range(8):
                sq = V(o_R2 + (ct % 2) * NTT, NTT)
                act(sq[:, 0:NT], Xb[ct][:, 0:NT], AF.Square)
                mm(ps[:, 0:NT], ones, sq[:, 0:NT], start=(ct == 0), stop=(ct == 7))
            rstd = V(o_R2 + 2 * NTT, NTT)
            act(rstd[:, 0:NT], ps[:, 0:NT], AF.Sqrt, bias=EPS, scale=1.0 / D)
            recip(rstd[:, 0:NT], rstd[:, 0:NT])
            for ct in range(8):
                ntmp = V(o_R2 + (3 + ct % 2) * NTT, NTT)
                tt('pool', ntmp[:, 0:NT], Xb[ct][:, 0:NT], rstd[:, 0:NT], ALU.mult)
                for (seq, c0, ln) in segs:
                    if final:
                        ts('dve', FINb[ct][:, c0:c0 + ln], ntmp[:, c0:c0 + ln], vec(0, V_NFIN + ct), ALU.mult)
                    else:
                        act(RR(HYb[ct][:, c0:c0 + ln]), ntmp[:, c0:c0 + ln], AF.Identity,
                            bias=modc(l, 0 if which == 0 else 3, ct, seq), scale=nsc(l, which, ct, seq))

        def process(NT, segs, C, src, dst):
            if stage < 1:
                return
            nseg = len(segs)
            SL = segs[0][2]
            S.dma(V(o_X, 8 * NTT).rearrange("p (c t) -> p c t", c=8)[:, :, 0:NT], src.rearrange("c p t -> p c t"))
            for l in range(L):
                S.dma(V(o_smw, NSW), smallw_d[l])
                ts('pool', V(o_smw + W_CIM, 512), V(o_smw + W_CIM, 512), -1.0, ALU.mult)
                smw = lambda c, n: A[:, o_smw + c:o_smw + c + n]
                rmsnorm_mod(l, 0, NT, segs)
                o_G = o_R1
                o_LX = o_G + 2 * NTT
                o_PR = o_LX + 2 * 520
                o_U5 = o_PR + 14 * 520
                assert o_U5 + 2 * NTT <= o_R1 + 10400
                Gb = [V(o_G + i * NTT, NTT) for i in range(2)]
                LXf = [V(o_LX + i * 520, 520) for i in range(2)]
                PRf = [V(o_PR + i * 520, 520) for i in range(14)]
                U5 = [V(o_U5 + i * NTT, NTT) for i in range(2)]

                def segview(buf, H):
                    return buf[:, 0:nseg * (H + SL)].rearrange("p (s t) -> p s t", s=nseg)

                for si, (seq, c0, ln) in enumerate(segs):
                    for ct in range(2):
                        cp('pool', segview(LXf[ct], 3)[:, si, 0:3], stc(l, O_LH + (ct * 3 + seq) * 3, 3))
                    for ct in range(14):
                        cp('pool', segview(PRf[ct], 2)[:, si, 1:2], stc(l, O_SH + ct * 3 + seq))
                for sl in range(10):
                    o = load_slab(win_d[l, sl])
                    pss = [psum() for _ in range(2)]
                    for k in range(8):
                        for m in range(2):
                            mm(pss[m][:, 0:NT], RR(WR[:, o + k * 256 + m * 128:o + k * 256 + m * 128 + 128]),
                               RR(HYb[k][:, 0:NT]), start=(k == 0), stop=(k == 7))
                    for m in range(2):
                        mt = sl * 2 + m
                        src_ps = pss[m][:, 0:NT]
                        if mt < 2:
                            cp('act', Gb[mt][:, 0:NT], src_ps)
                        elif mt < 4:
                            cp('act', segview(LXf[mt - 2], 3)[:, :, 3:3 + SL], src_ps.rearrange("p (s t) -> p s t", s=nseg))
                        elif mt < 18:
                            cp('act' if mt % 2 else 'dve', segview(PRf[mt - 4], 2)[:, :, 2:2 + SL],
                               src_ps.rearrange("p (s t) -> p s t", s=nseg))
                        else:
                            cp('dve', U5[mt - 18][:, 0:NT], src_ps)
                for si, (seq, c0, ln) in enumerate(segs):
                    for ct in range(2):
                        cp('pool', stc(l, O_LH + (ct * 3 + seq) * 3, 3), segview(LXf[ct], 3)[:, si, SL:SL + 3])
                    for ct in range(14):
                        cp('pool', stc(l, O_SH + ct * 3 + seq), segview(PRf[ct], 2)[:, si, SL + 1:SL + 2])

                if stage < 2:
                    continue
                for ct in range(2):
                    xc = V(o_R2, NTT)
                    xv = segview(LXf[ct], 3)
                    xc3 = xc[:, 0:NT].rearrange("p (s t) -> p s t", s=nseg)
                    cw = lambda k: vec(l, V_LCW + ct * 4 + k)
                    ts('dve', xc3, xv[:, :, 0:SL], cw(0), ALU.mult, vec(l, V_LCB + ct), ALU.add)
                    for k in range(1, 4):
                        stt(xc3, xv[:, :, k:k + SL], cw(k), xc3, ALU.mult, ALU.add)
                    ps1, ps2 = psum(), psum()
                    mm(ps1[:, 0:NT], smw(W_WA + ct * 128, 128), xc[:, 0:NT])
                    mm(ps2[:, 0:NT], smw(W_WX + ct * 128, 128), xc[:, 0:NT])
                    rg, ig = V(o_R2 + NTT, NTT), V(o_R2 + 2 * NTT, NTT)
                    act(rg[:, 0:NT], ps1[:, 0:NT], AF.Sigmoid, bias=vec(l, V_LBA + ct))
                    act(ig[:, 0:NT], ps2[:, 0:NT], AF.Sigmoid, bias=vec(l, V_LBX + ct))
                    aa, gn = V(o_R2 + 3 * NTT, NTT), V(o_R2 + 4 * NTT, NTT)
                    act(aa[:, 0:NT], rg[:, 0:NT], AF.Exp, scale=A[:, o_lrusp + l * 2 + ct:o_lrusp + l * 2 + ct + 1])
                    act(gn[:, 0:NT], rg[:, 0:NT], AF.Exp, scale=A[:, o_lrusp2 + l * 2 + ct:o_lrusp2 + l * 2 + ct + 1])
                    ts('pool', gn[:, 0:NT], gn[:, 0:NT], -1.0, ALU.mult, 1.0, ALU.add)
                    ts('pool', gn[:, 0:NT], gn[:, 0:NT], 0.0, ALU.max)
                    act(gn[:, 0:NT], gn[:, 0:NT], AF.Sqrt)
                    tt('pool', ig[:, 0:NT], ig[:, 0:NT], xc[:, 0:NT], ALU.mult)
                    tt('pool', ig[:, 0:NT], ig[:, 0:NT], gn[:, 0:NT], ALU.mult)
                    hh = V(o_R2 + 5 * NTT, NTT)
                    for (seq, c0, ln) in segs:
                        hst = stc(l, O_Lh + ct * 3 + seq)
                        scan(hh[:, c0:c0 + ln], aa[:, c0:c0 + ln], ig[:, c0:c0 + ln], hst)
                        cp('pool', hst, hh[:, c0 + ln - 1:c0 + ln])
                    act(rg[:, 0:NT], Gb[ct][:, 0:NT], AF.Gelu_apprx_tanh)
                    tt('pool', RR(HYb[ct][:, 0:NT]), hh[:, 0:NT], rg[:, 0:NT], ALU.mult)

                if stage < 3:
                    continue
                o_s = o_R2
                nsb_list = []
                for (seq, c0, ln) in segs:
                    for s0 in range(0, ln, SB):
                        nsb_list.append((seq, c0 + s0, min(SB, ln - s0), s0 + SB >= ln))
                SBL = nsb_list[0][2]
                nsb = len(nsb_list)
                ypss = [PS[6], PS[7]]
                for st_ in range(8):
                    kt, half, par = st_ // 4, (st_ % 4) // 2, st_ % 2
                    pr = slice(64 * half, 64 * half + 64)
                    pbr, pbi = psum(), psum()
                    mm(pbr[:, 0:NT], A[pr, o_smw + W_BRE + (kt * 2 + par) * 128:o_smw + W_BRE + (kt * 2 + par) * 128 + 128],
                       U5[kt][pr, 0:NT])
                    mm(pbi[:, 0:NT], A[pr, o_smw + W_BIM + (kt * 2 + par) * 128:o_smw + W_BIM + (kt * 2 + par) * 128 + 128],
                       U5[kt][pr, 0:NT])
                    base = o_s + (st_ % 2) * 4416
                    cre, cim, t1, t2, gre, gim, hre, him = [V(base + i * NTT, NTT) for i in range(8)]
                    def bt(tab):
                        return tab[:, 0:SBL].unsqueeze(1).to_broadcast([128, nsb, SBL])
                    v3 = lambda b: b[:, 0:NT].rearrange("p (s t) -> p s t", s=nsb)
                    Epr, Epi, Er, Ei = [s5tab(l, i, st_) for i in range(4)]
                    Rh = V(base + 4096 + 256, 64)
                    cp('pool', Rh, A[:, o_s5rho + l * 8 + st_:o_s5rho + l * 8 + st_ + 1].to_broadcast([128, 64]))
                    tt('dve', v3(t1), v3(pbr), bt(Epr), ALU.mult)
                    tt('dve', v3(t2), v3(pbi), bt(Epi), ALU.mult)
                    tt('pool', cre[:, 0:NT], t1[:, 0:NT], t2[:, 0:NT], ALU.subtract)
                    tt('dve', v3(t1), v3(pbi), bt(Epr), ALU.mult)
                    tt('dve', v3(t2), v3(pbr), bt(Epi), ALU.mult)
                    tt('pool', cim[:, 0:NT], t1[:, 0:NT], t2[:, 0:NT], ALU.add)
                    for (seq, c0, ln, last) in nsb_list:
                        hr0, hi0 = stc(l, O_S5R + st_ * 3 + seq), stc(l, O_S5I + st_ * 3 + seq)
                        cs = slice(c0, c0 + ln)
                        scan(gre[:, cs], Rh[:, 0:ln], cre[:, cs], hr0)
                        scan(gim[:, cs], Rh[:, 0:ln], cim[:, cs], hi0)
                        q1, q2, q3, q4 = [V(base + 4096 + i * 64, 64) for i in range(4)]
                        tt('pool', q1[:, 0:ln], gre[:, cs], Er[:, 0:ln], ALU.mult)
                        tt('pool', q2[:, 0:ln], gim[:, cs], Ei[:, 0:ln], ALU.mult)
                        tt('pool', hre[:, cs], q1[:, 0:ln], q2[:, 0:ln], ALU.subtract)
                        tt('dve', q3[:, 0:ln], gim[:, cs], Er[:, 0:ln], ALU.mult)
                        tt('dve', q4[:, 0:ln], gre[:, cs], Ei[:, 0:ln], ALU.mult)
                        tt('dve', him[:, cs], q3[:, 0:ln], q4[:, 0:ln], ALU.add)
                        cp('act', hr0, hre[:, c0 + ln - 1:c0 + ln])
                        cp('act', hi0, him[:, c0 + ln - 1:c0 + ln])
                    j, hf = st_ // 4, (st_ % 4) // 2
                    po = slice(64 * hf, 64 * hf + 64)
                    first = (st_ % 2 == 0)
                    mm(ypss[j][po, 0:NT], smw(W_CRE + st_ * 64, 64), hre[:, 0:NT], start=first, stop=False)
                    mm(ypss[j][po, 0:NT], smw(W_CIM + st_ * 64, 64), him[:, 0:NT], start=False, stop=(not first))
                zb = [V(o_s + i * NTT, NTT) for i in range(2)]
                for j in range(2):
                    stt(zb[j][:, 0:NT], U5[j][:, 0:NT], vec(l, V_S5D + j), ypss[j][:, 0:NT], ALU.mult, ALU.add)
                    act(zb[j][:, 0:NT], zb[j][:, 0:NT], AF.Gelu_apprx_tanh)
                for j in range(2):
                    ps = psum()
                    for k in range(2):
                        mm(ps[:, 0:NT], smw(W_GLU + k * 256 + j * 128, 128), zb[k][:, 0:NT], start=(k == 0), stop=(k == 1))
                    gt = V(o_s + 2 * NTT, NTT)
                    act(gt[:, 0:NT], ps[:, 0:NT], AF.Sigmoid, bias=vec(l, V_GLUB + j))
                    tt('pool', RR(HYb[6 + j][:, 0:NT]), zb[j][:, 0:NT], gt[:, 0:NT], ALU.mult)

                if stage < 4:
                    continue
                S.dma(V(o_smw, NSWB), smallwB_d[l])
                for ct in range(14):
                    pv = segview(PRf[ct], 2)
                    dtmp = V(o_R2 + (ct % 2) * NTT, NTT)
                    d3 = dtmp[:, 0:NT].rearrange("p (s t) -> p s t", s=nseg)
                    tt('pool', d3, pv[:, :, 1:1 + SL], pv[:, :, 2:2 + SL], ALU.subtract)
                    stt(pv[:, :, 2:2 + SL], d3, vec(l, V_MU + ct), pv[:, :, 2:2 + SL], ALU.mult, ALU.add)
                o_c = o_R2 + 2 * NTT

                def xm(ct):
                    if nseg == 1:
                        return PRf[ct][:, 2:2 + NT]
                    return None
                if nseg > 1:
                    for ct in range(14):
                        tmpc = V(o_c, NT)
                        cp('pool', tmpc.rearrange("p (s t) -> p s t", s=nseg), segview(PRf[ct], 2)[:, :, 2:2 + SL])
                        cp('pool', PRf[ct][:, 0:NT], tmpc)
                    xm = lambda ct: PRf[ct][:, 0:NT]
                lo_t = V(o_R2, NTT)
                act(lo_t[0:64, 0:NT], xm(12)[0:64, :], AF.Tanh)
                cp('pool', lo_t[64:128, 0:NT], xm(12)[64:128, :])
                gs_t = V(o_R2 + NTT, NTT)
                act(gs_t[:, 0:NT], xm(13), AF.Sigmoid)
                YR = [HYb[2 + hp] for hp in range(4)]
                for hp in range(4):
                    o_f = o_R2 + 2 * NTT
                    F = [V(o_f + i * NTT, NTT) for i in range(9)]
                    lw, aa, kkn, bb, csb, t1, t2, t3, t4 = F
                    r_, k_, v_ = xm(hp), xm(4 + hp), xm(8 + hp)
                    ps1, ps2, ps3 = psum(), psum(), psum()
                    mm(ps1[:, 0:NT], A[0:64, o_smw + W_LORA + hp * 128:o_smw + W_LORA + hp * 128 + 128], lo_t[0:64, 0:NT])
                    mm(ps2[:, 0:NT], A[64:128, o_smw + W_LORA + hp * 128:o_smw + W_LORA + hp * 128 + 128], lo_t[64:128, 0:NT])
                    mm(ps3[:, 0:NT], smw(W_G2 + hp * 128, 128), gs_t[:, 0:NT])
                    act(lw[:, 0:NT], ps1[:, 0:NT], AF.Sigmoid, bias=vec(l, V_W0 + hp))
                    ts('pool', lw[:, 0:NT], lw[:, 0:NT], -0.6065306597126334, ALU.mult)
                    act(aa[:, 0:NT], ps2[:, 0:NT], AF.Sigmoid, bias=vec(l, V_A0 + hp))
                    gg = t4
                    cp('act', gg[:, 0:NT], ps3[:, 0:NT])
                    ts('pool', kkn[:, 0:NT], k_, vec(l, V_KK + hp), ALU.mult)
                    tt('dve', t1[:, 0:NT], kkn[:, 0:NT], kkn[:, 0:NT], ALU.mult)
                    ps = psum()
                    mm(ps[:, 0:NT], bones, t1[:, 0:NT])
                    ts('dve', t1[:, 0:NT], ps[:, 0:NT], TINY, ALU.max)
                    act(t1[:, 0:NT], t1[:, 0:NT], AF.Sqrt)
                    recip(t1[:, 0:NT], t1[:, 0:NT])
                    tt('pool', kkn[:, 0:NT], kkn[:, 0:NT], t1[:, 0:NT], ALU.mult)
                    ts('pool', t1[:, 0:NT], aa[:, 0:NT], vec(l, V_KA + hp), ALU.mult,
                       A[:, o_omka + l * 4 + hp:o_omka + l * 4 + hp + 1], ALU.add)
                    tt('pool', k_, k_, t1[:, 0:NT], ALU.mult)
                    tt('dve', bb[:, 0:NT], kkn[:, 0:NT], aa[:, 0:NT], ALU.mult)
                    stt(t1[:, 0:NT], r_, vec(l, V_RK + hp), k_, ALU.mult, ALU.mult)
                    ps = psum()
                    mm(ps[:, 0:NT], bones, t1[:, 0:NT])
                    bonus = t3
                    tt('dve', bonus[:, 0:NT], ps[:, 0:NT], v_, ALU.mult)
                    rmask = A[:, o_const + C_RESET:o_const + C_RESET + NTT]
                    if C == CH:
                        scan(csb[:, 0:NT], rmask[:, 0:NT], lw[:, 0:NT], 0.0)
                    else:
                        for (seq, c0, ln) in segs:
                            scan(csb[:, c0:c0 + ln], rmask[:, 1:1 + ln], lw[:, c0:c0 + ln], 0.0)
                    nch = NT // C
                    c3 = lambda b: b[:, 0:NT].rearrange("p (c t) -> p c t", c=nch)
                    csC = c3(csb)[:, :, C - 1:C].to_broadcast([128, nch, C])
                    gex = t1
                    tt('pool', gex[:, 0:NT], csb[:, 0:NT], lw[:, 0:NT], ALU.subtract)
                    act(gex[:, 0:NT], gex[:, 0:NT], AF.Exp)
                    at_f = t1
                    tt('dve', at_f[:, 0:NT], kkn[:, 0:NT], gex[:, 0:NT], ALU.mult)
                    ig_ = kkn
                    act(ig_[:, 0:NT], csb[:, 0:NT], AF.Exp, scale=-1.0)
                    gC = lw
                    tt('pool', c3(gC), csC, c3(csb), ALU.subtract)
                    act(gC[:, 0:NT], gC[:, 0:NT], AF.Exp)
                    bh_f = aa
                    tt('dve', bh_f[:, 0:NT], bb[:, 0:NT], gC[:, 0:NT], ALU.mult)
                    tt('pool', bb[:, 0:NT], bb[:, 0:NT], ig_[:, 0:NT], ALU.mult)
                    kh_f = gC
                    tt('dve', kh_f[:, 0:NT], k_, gC[:, 0:NT], ALU.mult)
                    kt_f = ig_
                    tt('pool', kt_f[:, 0:NT], k_, ig_[:, 0:NT], ALU.mult)
                    act(csb[:, 0:NT], csb[:, 0:NT], AF.Exp)
                    gamC = csb
                    rt_f = t2
                    tt('dve', rt_f[:, 0:NT], r_, csb[:, 0:NT], ALU.mult)
                    bt_f = bb
                    o_m = o_R2 + 11 * NTT
                    chunks = []
                    for (seq, c0, ln) in segs:
                        for s0 in range(0, ln, C):
                            chunks.append((seq, c0 + s0))
                    for (seq, c0) in chunks:
                        if sub < 1:
                            break
                        cc = slice(c0, c0 + C)
                        pst = psum()
                        for i, srcf in enumerate((at_f, None, bh_f, kh_f)):
                            sap = v_[:, c0:c0 + C] if srcf is None else srcf[:, cc]
                            tr(pst[0:C, i * 128:i * 128 + 128], sap, ident)
                        TM = V(o_m, 512)
                        cp('act', TM[0:C, :], pst[0:C, :])
                        At_t, V_t, Bh_t, Kh_t = [TM[0:C, i * 128:i * 128 + 128] for i in range(4)]
                        U = []
                        for hl in range(2):
                            pr = slice(64 * hl, 64 * hl + 64)
                            ob = o_m + 512 + hl * 640
                            NTm, MTm, Nm, PBm, QKm = [A[0:C, ob + i * 128:ob + i * 128 + C] for i in range(5)]
                            Pa, PaT, Pb, PbT, Tm = [CHR[0:C, hl * 640 + i * 128:hl * 640 + i * 128 + C] for i in range(5)]
                            U.append(dict(pr=pr, S1=A[0:C, ob:ob + 128], NT=NTm, MT=MTm, N=Nm, PB=PBm, QK=QKm, Pa=Pa, PaT=PaT, Pb=Pb, PbT=PbT, T=Tm))
                        msu = A[0:C, o_const + C_MSU:o_const + C_MSU + C]
                        msl = A[0:C, o_const + C_MSL:o_const + C_MSL + C]
                        miu = A[0:C, o_const + C_MIU:o_const + C_MIU + C]
                        for u in U:
                            if not (bits & 2):
                                break
                            pr = u['pr']
                            pA, pB = psum(), psum()
                            mm(pA[0:C, 0:C], at_f[pr, cc], bt_f[pr, cc])
                            mm(pA[0:C, 128:128 + C], at_f[pr, cc], kt_f[pr, cc])
                            mm(pB[0:C, 0:C], bt_f[pr, cc], at_f[pr, cc])
                            mm(pB[0:C, 128:128 + C], bt_f[pr, cc], rt_f[pr, cc])
                            mm(pB[0:C, 256:256 + C], kt_f[pr, cc], rt_f[pr, cc])
                            if not (bits & 4):
                                continue
                            act(u['PaT'], pA[0:C, 0:C], AF.Copy, scale=-1.0); tt('pool', u['PaT'], u['PaT'], msl, ALU.mult)
                            cp('act', u['MT'], pA[0:C, 128:128 + C]); tt('pool', u['MT'], u['MT'], msl, ALU.mult)
                            if not (bits & 16):
                                continue
                            act(u['Pa'], pB[0:C, 0:C], AF.Copy, scale=-1.0); tt('pool', u['Pa'], u['Pa'], msu, ALU.mult)
                            cp('act', u['PB'], pB[0:C, 128:128 + C]); tt('pool', u['PB'], u['PB'], miu, ALU.mult)
                            cp('act', u['QK'], pB[0:C, 256:256 + C]); tt('pool', u['QK'], u['QK'], miu, ALU.mult)
                            if bits & 8:
                                tt('pool', u['T'], u['Pa'], ident[0:C, 0:C], ALU.add)
                        if sub < 2:
                            continue
                        nlev = int(round(math.log2(C))) - 1
                        for lev in range(1, nlev + 1):
                            for u in U:
                                Pp, PpT = (u['Pa'], u['PaT']) if lev % 2 == 1 else (u['Pb'], u['PbT'])
                                Pn, PnT = (u['Pb'], u['PbT']) if lev % 2 == 1 else (u['Pa'], u['PaT'])
                                pq = psum()
                                RC = (lambda a_: a_)
                                mm(pq[0:C, 0:C], RC(Pp), RC(PpT))
                                if lev < nlev:
                                    mm(pq[0:C, 128:128 + C], RC(PpT), RC(Pp))
                                cp('act', PnT, pq[0:C, 0:C])
                                if lev < nlev:
                                    cp('act', Pn, pq[0:C, 128:128 + C])
                                mm(pq[0:C, 256:256 + C], RC(PnT), RC(u['T']))
                                tt('dve', u['T'], pq[0:C, 256:256 + C], u['T'], ALU.add)
                        if sub < 3:
                            continue
                        for u in U:
                            pr = u['pr']
                            hc = slice(pr.start, pr.stop)
                            pq = psum()
                            mm(pq[0:C, 0:64], u['T'], At_t[:, hc])
                            mm(pq[0:C, 128:128 + C], u['MT'], u['T'])
                            Ab = u['S1'][:, 0:64]
                            MTT = u['N']
                            cp('act', Ab, pq[0:C, 0:64])
                            cp('act', MTT, pq[0:C, 128:128 + C])
                            mm(pq[0:C, 256:320], MTT, V_t[:, hc])
                            nUt = u['S1'][:, 64:128]
                            ts('dve', nUt, pq[0:C, 256:320], -1.0, ALU.mult)
                            pg = psum()
                            mm(pg[pr, 0:64], Ab, Bh_t[:, hc])
                            mm(pg[pr, 128:128 + C], Ab, u['PB'])
                            Gm = A[pr, o_m + 3072 + 0:o_m + 3072 + 64]
                            Rh = A[pr, o_m + 3072 + 64:o_m + 3072 + 64 + C]
                            stt(Gm, ident[pr, hc], gamC[pr, c0 + C - 1:c0 + C], pg[pr, 0:64], ALU.mult, ALU.subtract)
                            tt('dve', Rh, rt_f[pr, cc], pg[pr, 128:128 + C], ALU.subtract)
                            STm = A[pr, o_sT + (l * 3 + seq) * 256 + hp * 64:o_sT + (l * 3 + seq) * 256 + hp * 64 + 64]
                            ph = psum()
                            mm(ph[pr, 128:128 + C], nUt, u['PB'], start=True, stop=False)
                            mm(ph[pr, 128:128 + C], V_t[:, hc], u['QK'], start=False, stop=False)
                            mm(ph[pr, 128:128 + C], STm, Rh, start=False, stop=True)
                            cp('act', RR(YR[hp][pr, cc]), ph[pr, 128:128 + C])
                            ph2 = psum()
                            mm(ph2[pr, 0:64], Bh_t[:, hc], nUt, start=True, stop=False)
                            mm(ph2[pr, 0:64], Kh_t[:, hc], V_t[:, hc], start=False, stop=False)
                            mm(ph2[pr, 0:64], Gm, STm, start=False, stop=True)
                            cp('dve', STm, ph2[pr, 0:64])
                    y = YR[hp]
                    ps = psum()
                    mm(ps[:, 0:NT], bones, y[:, 0:NT])
                    yc = t1
                    stt(yc[:, 0:NT], ps[:, 0:NT], -1.0 / 64, y[:, 0:NT], ALU.mult, ALU.add)
                    tt('dve', t2[:, 0:NT], yc[:, 0:NT], yc[:, 0:NT], ALU.mult)
                    ps = psum()
                    mm(ps[:, 0:NT], bones, t2[:, 0:NT])
                    act(t2[:, 0:NT], ps[:, 0:NT], AF.Sqrt, bias=GNEPS, scale=1.0 / 64)
                    recip(t2[:, 0:NT], t2[:, 0:NT])
                    tt('pool', yc[:, 0:NT], yc[:, 0:NT], t2[:, 0:NT], ALU.mult)
                    ts('pool', yc[:, 0:NT], yc[:, 0:NT], vec(l, V_LNW + hp), ALU.mult, vec(l, V_LNB + hp), ALU.add)
                    tt('dve', yc[:, 0:NT], yc[:, 0:NT], bonus[:, 0:NT], ALU.add)
                    tt('pool', RR(y[:, 0:NT]), yc[:, 0:NT], gg[:, 0:NT], ALU.mult)

                if stage < 5:
                    continue
                for sl in range(4):
                    o = load_slab(wout_d[l, sl])
                    pss = [psum() for _ in range(2)]
                    for k in range(8):
                        for m in range(2):
                            mm(pss[m][:, 0:NT], RR(WR[:, o + k * 256 + m * 128:o + k * 256 + m * 128 + 128]),
                               RR(HYb[k][:, 0:NT]), start=(k == 0), stop=(k == 7))
                    for m in range(2):
                        ct = sl * 2 + m
                        for (seq, c0, ln) in segs:
                            stt(Xb[ct][:, c0:c0 + ln], pss[m][:, c0:c0 + ln], modc(l, 2, ct, seq), Xb[ct][:, c0:c0 + ln],
                                ALU.mult, ALU.add)
                if stage < 6:
                    continue
                rmsnorm_mod(l, 1, NT, segs)
                ACTB = [ACTR[:, i * NTT:(i + 1) * NTT] for i in range(11)]
                for half in range(2):
                    for j in range(11):
                        mg = half * 11 + j
                        o = load_slab(wup_d[l, mg])
                        pss = [psum() for _ in range(2)]
                        for k in range(8):
                            for m in range(2):
                                mm(pss[m][:, 0:NT], RR(WR[:, o + k * 256 + m * 128:o + k * 256 + m * 128 + 128]),
                                   RR(HYb[k][:, 0:NT]), start=(k == 0), stop=(k == 7))
                        cv = []
                        for m in range(2):
                            ctp = mg * 2 + m
                            bi = (j % 2) * 2 + m
                            ub = V(o_R2 + bi * 520, 520)
                            uv = segview(ub, 2)
                            for si, (seq, c0, ln) in enumerate(segs):
                                cp('pool', uv[:, si, 0:2], stc(l, O_FH + (ctp * 3 + seq) * 2, 2))
                            cp('act', uv[:, :, 2:2 + SL], pss[m][:, 0:NT].rearrange("p (s t) -> p s t", s=nseg))
                            for si, (seq, c0, ln) in enumerate(segs):
                                cp('pool', stc(l, O_FH + (ctp * 3 + seq) * 2, 2), uv[:, si, SL:SL + 2])
                            co = V(o_R2 + 4 * 520 + bi * NTT, NTT)
                            co3 = co[:, 0:NT].rearrange("p (s t) -> p s t", s=nseg)
                            cv.append((co, co3, uv, ctp))
                        for (co, co3, uv, ctp) in cv:
                            ts('dve', co3, uv[:, :, 0:SL], vec(l, V_FCW + ctp * 3), ALU.mult, vec(l, V_FCB + ctp), ALU.add)
                        for kk_ in (1, 2):
                            for (co, co3, uv, ctp) in cv:
                                stt(co3, uv[:, :, kk_:kk_ + SL], vec(l, V_FCW + ctp * 3 + kk_), co3, ALU.mult, ALU.add)
                        val, gate = cv[0][0], cv[1][0]
                        act(gate[:, 0:NT], gate[:, 0:NT], AF.Silu)
                        tt('pool', RR(ACTB[j][:, 0:NT]), val[:, 0:NT], gate[:, 0:NT], ALU.mult)
                    for mt in range(8):
                        o = load_slab(wdn_d[l, half, mt], 1408)
                        ps = psum()
                        for k in range(11):
                            mm(ps[:, 0:NT], RR(WR[:, o + k * 128:o + k * 128 + 128]), RR(ACTB[k][:, 0:NT]),
                               start=(k == 0), stop=(k == 10))
                        for (seq, c0, ln) in segs:
                            stt(Xb[mt][:, c0:c0 + ln], ps[:, c0:c0 + ln], modc(l, 5, mt, seq), Xb[mt][:, c0:c0 + ln],
                                ALU.mult, ALU.add)
            rmsnorm_mod(0, 0, NT, segs, final=True)
            S.dma(dst.rearrange("c p t -> p c t"), V(o_R1, 8 * NTT).rearrange("p (c t) -> p c t", c=8)[:, :, 0:NT])

        import os as _os
        _kt = _os.environ.get('KTILES', 'ps')
        for ti in range(ntp if 'p' in _kt else 0):
            process(NTT, [(0, 0, NTT)], CH, xp[:, :, ti * NTT:(ti + 1) * NTT], y_d[:, :, ti * NTT:(ti + 1) * NTT])
        if 's' in _kt:
            process(32, [(1, 0, 16), (2, 16, 16)], 16, xs, ys_d)
        S.dma(st_out, V(o_st, L * NSTO))
        S.dma(sT_out, V(o_sT, L * 768))
        S.finish()
        S.emit(es)
    return nc


def _col(v):
    v = np.asarray(v, np.float32).reshape(-1, 128)
    return np.ascontiguousarray(v.T)


def _ffn_base(ctp):
    m, i = divmod(ctp, 2)
    return 128 * m if i == 0 else DFF + 128 * m


def _shared(inp):
    f = lambda k: np.asarray(inp[k], np.float32)
    vecs = np.zeros((L, 128, NV), np.float32)
    smallw = np.zeros((L, 128, NSW), np.float32)
    smallwB = np.zeros((L, 128, NSWB), np.float32)
    for l in range(L):
        v = vecs[l]
        v[:, V_NM:V_NM + 8] = _col(f('norm_mix')[l])
        v[:, V_NF:V_NF + 8] = _col(f('norm_ffn')[l])
        v[:, V_BADA:V_BADA + 48] = _col(f('b_ada')[l])
        for ct in range(2):
            for k in range(4):
                v[:, V_LCW + ct * 4 + k] = f('lru_conv_w')[l, k, ct * 128:(ct + 1) * 128]
        v[:, V_LCB:V_LCB + 2] = _col(f('lru_conv_b')[l])
        v[:, V_LBA:V_LBA + 2] = _col(f('lru_ba')[l])
        v[:, V_LBX:V_LBX + 2] = _col(f('lru_bx')[l])
        v[:, V_LLAM:V_LLAM + 2] = _col(f('lru_lambda')[l])
        v[:, V_MU:V_MU + 14] = _col(f('rwkv_mu')[l])
        for nm, o in (('rwkv_w0', V_W0), ('rwkv_a0', V_A0), ('rwkv_k_k', V_KK), ('rwkv_k_a', V_KA),
                      ('rwkv_r_k', V_RK), ('rwkv_ln_w', V_LNW), ('rwkv_ln_b', V_LNB)):
            v[:, o:o + 4] = _col(f(nm)[l].reshape(-1))
        v[:, V_S5AR:V_S5AR + 8] = _col(f('s5_a_re')[l].reshape(-1))
        v[:, V_S5AI:V_S5AI + 8] = _col(f('s5_a_im')[l].reshape(-1))
        v[:, V_S5DT:V_S5DT + 8] = _col(np.repeat(f('s5_log_dt')[l], 64))
        v[:, V_S5D:V_S5D + 2] = _col(f('s5_d')[l])
        v[:, V_GLUB:V_GLUB + 2] = _col(f('s5_glu_b')[l])
        for ctp in range(44):
            b = _ffn_base(ctp)
            for k in range(3):
                v[:, V_FCW + ctp * 3 + k] = f('ffn_conv_w')[l, k, b:b + 128]
            v[:, V_FCB + ctp] = f('ffn_conv_b')[l, b:b + 128]
        v[:, V_NFIN:V_NFIN + 8] = _col(f('norm_final'))
        w = smallw[l]
        for nm, o in (('lru_wa', W_WA), ('lru_wx', W_WX)):
            for ct in range(2):
                for hh in range(2):
                    w[hh * 64:(hh + 1) * 64, o + ct * 128 + hh * 64:o + ct * 128 + hh * 64 + 64] = f(nm)[l, 2 * ct + hh]
        smallwB[l][0:64, W_LORA:W_LORA + 512] = f('rwkv_w2')[l]
        smallwB[l][64:128, W_LORA:W_LORA + 512] = f('rwkv_a2')[l]
        smallwB[l][:, W_G2:W_G2 + 512] = f('rwkv_g2')[l]
        for nm, o in (('s5_b_re', W_BRE), ('s5_b_im', W_BIM)):
            bsrc = f(nm)[l]
            for st_ in range(8):
                kt, half, par = st_ // 4, (st_ % 4) // 2, st_ % 2
                for gg in (2 * st_, 2 * st_ + 1):
                    p0 = gg * 16 - kt * 128
                    s0 = (gg - 2 * st_) * 64
                    c0 = o + (kt * 2 + par) * 128 + s0
                    w[p0:p0 + 16, c0:c0 + 64] = bsrc[gg].T
        for nm, o in (('s5_c_re', W_CRE), ('s5_c_im', W_CIM)):
            csrc = f(nm)[l]
            for st_ in range(8):
                for gg in (2 * st_, 2 * st_ + 1):
                    p0 = (gg - 2 * st_) * 64
                    c0 = o + st_ * 64 + (gg - 4 * (st_ // 2)) * 16
                    w[p0:p0 + 64, c0:c0 + 16] = csrc[gg].T
        w[:, W_GLU:W_GLU + 512] = f('s5_glu_w')[l].reshape(2, 128, 256).transpose(1, 0, 2).reshape(128, 512)
    consts = np.zeros((128, NCONST), np.float32)
    ii, jj = np.meshgrid(np.arange(128), np.arange(128), indexing='ij')
    consts[:, C_ID:C_ID + 128] = np.eye(128)
    consts[:, C_BONES:C_BONES + 128] = (ii // 64 == jj // 64)
    consts[:, C_ONES:C_ONES + 128] = 1.0
    consts[:, C_MSU:C_MSU + 128] = (ii < jj)
    consts[:, C_MSL:C_MSL + 128] = (jj < ii)
    consts[:, C_MIU:C_MIU + 128] = (ii <= jj)
    consts[:, C_RESET:C_RESET + 512] = (np.arange(512) % 128 != 0)[None, :]

    def slab(wm, ncol):
        n = ncol // 256
        return np.ascontiguousarray(wm.reshape(8, 128, n, 256).transpose(2, 1, 0, 3).reshape(n, 128, 2048))
    colidx = np.concatenate([np.arange(_ffn_base(c), _ffn_base(c) + 128) for c in range(44)])
    sh = dict(vecs=vecs, smallw=smallw, smallwB=smallwB, consts=consts)
    sh['win'] = np.stack([slab(f('w_in')[l], 2560) for l in range(L)])
    sh['wout'] = np.stack([slab(f('w_out')[l], 1024) for l in range(L)])
    sh['wup'] = np.stack([slab(f('ffn_up')[l][:, colidx], 5632) for l in range(L)])
    sh['wdn'] = np.stack([np.ascontiguousarray(f('ffn_down')[l].reshape(2, 11, 128, 8, 128).transpose(0, 3, 2, 1, 4).reshape(2, 8, 128, 1408))
                          for l in range(L)])
    sh['wada'] = np.stack([slab(f('w_ada')[l], 6144) for l in range(L)])
    return sh


def _core_inputs(inp, c, TP):
    f = lambda k: np.asarray(inp[k], np.float32)
    b, s0, s1 = c // 2, 2 * c, 2 * c + 1
    m = {}
    m['xp'] = np.ascontiguousarray(f('x_prompt')[b].T).reshape(8, 128, TP)
    m['xs'] = np.ascontiguousarray(f('x_sample')[[s0, s1]].reshape(32, D).T).reshape(8, 128, 32)
    cl = np.stack([f('c_prompt')[b], f('c_sample')[s0], f('c_sample')[s1], np.zeros(D, np.float32)])
    m['cT'] = np.ascontiguousarray(cl.reshape(4, 8, 128).transpose(2, 1, 0).reshape(128, 32))
    st = np.zeros((128, L * NSTO), np.float32)
    sT = np.zeros((128, L * 768), np.float32)
    for l in range(L):
        o = l * NSTO
        for seq, s in ((1, s0), (2, s1)):
            for ct in range(2):
                for j in range(3):
                    st[:, o + O_LH + (ct * 3 + seq) * 3 + j] = f('state_lru_conv')[l, s, j, ct * 128:(ct + 1) * 128]
                st[:, o + O_Lh + ct * 3 + seq] = f('state_lru_h')[l, s, ct * 128:(ct + 1) * 128]
            for ct in range(14):
                st[:, o + O_SH + ct * 3 + seq] = f('state_rwkv_shift')[l, s, ct * 128:(ct + 1) * 128]
            for st_ in range(8):
                st[:, o + O_S5R + st_ * 3 + seq] = f('state_s5_re')[l, s].reshape(-1)[st_ * 128:(st_ + 1) * 128]
                st[:, o + O_S5I + st_ * 3 + seq] = f('state_s5_im')[l, s].reshape(-1)[st_ * 128:(st_ + 1) * 128]
            for ctp in range(44):
                bb = _ffn_base(ctp)
                for j in range(2):
                    st[:, o + O_FH + (ctp * 3 + seq) * 2 + j] = f('state_ffn_conv')[l, s, j, bb:bb + 128]
            Sm = f('state_rwkv_S')[l, s].reshape(4, 2, 64, 64).transpose(1, 3, 0, 2).reshape(128, 256)
            sT[:, (l * 3 + seq) * 256:(l * 3 + seq + 1) * 256] = Sm
    m['st_in'] = st
    m['sT_in'] = sT
    return m


_NC_CACHE = {}


def kernel(**inp):
    TP = int(np.asarray(inp['x_prompt']).shape[1])
    ntp = TP // NTT
    import os
    if ntp not in _NC_CACHE:
        _NC_CACHE[ntp] = build(ntp, stage=int(os.environ.get('KSTAGE', '99')), sub=int(os.environ.get('KSUB', '99')), bits=int(os.environ.get('KBITS', '255')))
    nc = _NC_CACHE[ntp]
    sh = _shared(inp)
    in_maps = []
    for c in range(8):
        m = dict(sh)
        m.update(_core_inputs(inp, c, TP))
        in_maps.append(m)
    res = run_bass_kernel_spmd(nc, in_maps, core_ids=list(range(8)))
    R = res.results
    B, SB_ = 4, 16
    y_prompt = np.zeros((B, TP, D), np.float32)
    y_sample = np.zeros((SB_, 16, D), np.float32)

    def mk(nb):
        return [np.zeros((L, nb, 3, 256), np.float32), np.zeros((L, nb, 256), np.float32),
                np.zeros((L, nb, 1792), np.float32), np.zeros((L, nb, 8, 64, 64), np.float32),
                np.zeros((L, nb, 16, 64), np.float32), np.zeros((L, nb, 16, 64), np.float32),
                np.zeros((L, nb, 2, 2 * DFF), np.float32)]
    P, Sg = mk(B), mk(SB_)

    def unpack(dst, bi, st, sT, seq):
        for l in range(L):
            o = l * NSTO
            for ct in range(2):
                for j in range(3):
                    dst[0][l, bi, j, ct * 128:(ct + 1) * 128] = st[:, o + O_LH + (ct * 3 + seq) * 3 + j]
                dst[1][l, bi, ct * 128:(ct + 1) * 128] = st[:, o + O_Lh + ct * 3 + seq]
            for ct in range(14):
                dst[2][l, bi, ct * 128:(ct + 1) * 128] = st[:, o + O_SH + ct * 3 + seq]
            re = np.zeros(1024, np.float32)
            im = np.zeros(1024, np.float32)
            for st_ in range(8):
                re[st_ * 128:(st_ + 1) * 128] = st[:, o + O_S5R + st_ * 3 + seq]
                im[st_ * 128:(st_ + 1) * 128] = st[:, o + O_S5I + st_ * 3 + seq]
            dst[4][l, bi] = re.reshape(16, 64)
            dst[5][l, bi] = im.reshape(16, 64)
            for ctp in range(44):
                bb = _ffn_base(ctp)
                for j in range(2):
                    dst[6][l, bi, j, bb:bb + 128] = st[:, o + O_FH + (ctp * 3 + seq) * 2 + j]
            Sm = sT[:, (l * 3 + seq) * 256:(l * 3 + seq + 1) * 256].reshape(2, 64, 4, 64)
            dst[3][l, bi] = Sm.transpose(2, 0, 3, 1).reshape(8, 64, 64)

    for c in range(8):
        r = R[c]
        b, s0, s1 = c // 2, 2 * c, 2 * c + 1
        ys = np.asarray(r['ys']).reshape(D, 32).T
        y_sample[s0] = ys[0:16]
        y_sample[s1] = ys[16:32]
        st, sT = np.asarray(r['st_out']), np.asarray(r['sT_out'])
        unpack(Sg, s0, st, sT, 1)
        unpack(Sg, s1, st, sT, 2)
        if c % 2 == 0:
            y_prompt[b] = np.asarray(r['y']).reshape(D, TP).T
            unpack(P, b, st, sT, 0)
    return (y_prompt, y_sample, *P, *Sg)
```

```python
import bisect
import contextlib
import math
import numpy as np
import concourse.bass as bass
import concourse.mybir as mybir
from concourse.bass_utils import run_bass_kernel_spmd

F32 = mybir.dt.float32
F32R = mybir.dt.float32r
USE_R = True


def RR(ap):
    return ap.bitcast(F32R) if USE_R else ap
ALU = mybir.AluOpType
AF = mybir.ActivationFunctionType

ENG_ATTR = {'pe': 'tensor', 'act': 'scalar', 'dve': 'vector', 'pool': 'gpsimd', 'sp': 'sync'}
ERA = 30000
NSLOT = 8


class _Rec:
    __slots__ = ('w', 'r')

    def __init__(self, w=None, r=None):
        self.w = w
        self.r = dict(r) if r else {}


class _IMap:
    def __init__(self):
        self.b = [0, 1 << 60]
        self.rec = [_Rec()]

    def _split(self, x):
        i = bisect.bisect_right(self.b, x) - 1
        if self.b[i] != x:
            self.b.insert(i + 1, x)
            old = self.rec[i]
            self.rec.insert(i + 1, _Rec(old.w, old.r))

    def rng(self, lo, hi):
        self._split(lo)
        self._split(hi)
        i0 = bisect.bisect_left(self.b, lo)
        i1 = bisect.bisect_left(self.b, hi)
        return self.rec[i0:i1]


def _is_dram(ap):
    try:
        return 'DRam' in type(ap.tensor).__name__
    except Exception:
        return False


def ap_key(ap):
    pat = ap.ap
    pstride = pat[0][0]
    off = ap.offset
    col = off % pstride if pstride > 0 else off
    ext = 0
    for st, cnt in pat[1:]:
        ext += abs(st) * (cnt - 1)
    if ap.tensor.name.startswith('P'):
        return ap.tensor.name, 0, 512
    return ap.tensor.name, col, col + ext + 1


class Sched:
    def __init__(self, nc):
        self.nc = nc
        self.prog = {e: [] for e in ENG_ATTR}
        self.cnt = {e: 0 for e in ENG_ATTR}
        self.seen = {e: {} for e in ENG_ATTR}
        self.maps = {}
        self.semkeys = []
        self.semset = set()
        self.dma_n = {e: 0 for e in ENG_ATTR}
        self.slot_val = {}

    def _sem(self, key):
        if key not in self.semset:
            self.semset.add(key)
            self.semkeys.append(key)
        return key

    def _wait(self, eng, tok):
        key, val = tok
        if self.seen[eng].get(key, 0) >= val:
            return
        self.seen[eng][key] = val
        self.prog[eng].append(('wait', key, val))

    def _deps(self, eng, reads, writes, tok, pseudo):
        for ap in reads:
            name, lo, hi = ap_key(ap)
            m = self.maps.setdefault(name, _IMap())
            for rec in m.rng(lo, hi):
                if rec.w is not None and not (rec.w[2] == pseudo and pseudo == 'pe'):
                    self._wait(eng, rec.w[:2])
        for ap in writes:
            name, lo, hi = ap_key(ap)
            m = self.maps.setdefault(name, _IMap())
            for rec in m.rng(lo, hi):
                if rec.w is not None and rec.w[2] != pseudo:
                    self._wait(eng, rec.w[:2])
                for (k, e2), v in rec.r.items():
                    if e2 != pseudo:
                        self._wait(eng, (k, v))
        for ap in reads:
            name, lo, hi = ap_key(ap)
            for rec in self.maps[name].rng(lo, hi):
                rec.r[(tok[0], pseudo)] = tok[1]
        for ap in writes:
            name, lo, hi = ap_key(ap)
            for rec in self.maps[name].rng(lo, hi):
                rec.w = (tok[0], tok[1], pseudo)
                rec.r = {}

    def op(self, eng, fn, reads=(), writes=()):
        reads = [r for r in reads if r is not None and hasattr(r, 'tensor') and not _is_dram(r)]
        writes = [w for w in writes if not _is_dram(w)]
        n = self.cnt[eng] + 1
        self.cnt[eng] = n
        era, v = divmod(n - 1, ERA)
        key = self._sem((eng, era))
        tok = (key, v + 1)
        self._deps(eng, reads, writes, tok, eng)
        self.prog[eng].append(('op', fn, key))
        return tok

    def dma(self, out, in_, eng='sp'):
        j = self.dma_n[eng]
        self.dma_n[eng] = j + 1
        slot = j % NSLOT
        key = self._sem(('dma', eng, slot))
        prev = self.slot_val.get(key, 0)
        if prev > 0:
            self._wait(eng, (key, prev))
        tok = (key, prev + 16)
        self.slot_val[key] = prev + 16
        reads = [] if _is_dram(in_) else [in_]
        writes = [] if _is_dram(out) else [out]
        self._deps(eng, reads, writes, tok, 'dma_%s_%d_%d' % (eng, slot, j))
        self.prog[eng].append(('dma', out, in_, key))
        return tok

    def finish(self, eng='sp'):
        for key, val in self.slot_val.items():
            self._wait(eng, (key, val))

    def emit(self, es):
        nc = self.nc
        sems = {}
        for key in self.semkeys:
            sems[key] = es.enter_context(nc.semaphore("s_" + "_".join(str(x) for x in key)))
        block = es.enter_context(nc.Block())

        def run(eng_name):
            def body(e):
                for item in self.prog[eng_name]:
                    if item[0] == 'wait':
                        e.wait_ge(sems[item[1]], item[2])
                    elif item[0] == 'op':
                        item[1](e).then_inc(sems[item[2]], 1)
                    else:
                        e.dma_start(out=item[1], in_=item[2]).then_inc(sems[item[3]], 16)
            return body

        block.tensor(run('pe'))
        block.scalar(run('act'))
        block.vector(run('dve'))
        block.gpsimd(run('pool'))
        block.sync(run('sp'))


D = 1024
L = 2
NTT = 512
CH = 128
SB = 64
DFF = 2816
O_LH, O_Lh, O_SH, O_S5R, O_S5I, O_FH, NSTO = 0, 18, 24, 66, 90, 114, 378
V_NM, V_NF, V_BADA, V_LCW, V_LCB, V_LBA, V_LBX, V_LLAM = 0, 8, 16, 64, 72, 74, 76, 78
V_MU, V_W0, V_A0, V_KK, V_KA, V_RK, V_LNW, V_LNB = 80, 94, 98, 102, 106, 110, 114, 118
V_S5AR, V_S5AI, V_S5DT, V_S5D, V_GLUB, V_FCW, V_FCB, V_NFIN, NV = 122, 130, 138, 146, 148, 150, 282, 326, 334
W_WA, W_WX, W_GLU, W_BRE, W_BIM, W_CRE, W_CIM, NSW = 0, 256, 512, 1024, 1536, 2048, 2560, 3072
W_LORA, W_G2, NSWB = 0, 512, 1024
C_ID, C_BONES, C_ONES, C_MSU, C_MSL, C_MIU, C_RESET, C_NMSU, C_NMSL, NCONST = 0, 128, 256, 384, 512, 640, 768, 1280, 1408, 1536


def build(ntp, dbg=False, stage=99, sub=99, bits=255):
    TP = ntp * NTT
    nc = bass.Bass("TRN2", target_bir_lowering=False)

    def din(name, shape):
        return nc.dram_tensor(name, list(shape), F32, kind="ExternalInput").ap()

    def dout(name, shape):
        return nc.dram_tensor(name, list(shape), F32, kind="ExternalOutput").ap()

    xp = din("xp", [8, 128, TP])
    xs = din("xs", [8, 128, 32])
    cT = din("cT", [128, 32])
    vecs_d = din("vecs", [L, 128, NV])
    smallw_d = din("smallw", [L, 128, NSW])
    smallwB_d = din("smallwB", [L, 128, NSWB])
    consts_d = din("consts", [128, NCONST])
    win_d = din("win", [L, 10, 128, 2048])
    wout_d = din("wout", [L, 4, 128, 2048])
    wup_d = din("wup", [L, 22, 128, 2048])
    wdn_d = din("wdn", [L, 2, 8, 128, 1408])
    wada_d = din("wada", [L, 24, 128, 2048])
    st_in = din("st_in", [128, L * NSTO])
    sT_in = din("sT_in", [128, L * 3 * 256])
    y_d = dout("y", [8, 128, TP])
    ys_d = dout("ys", [8, 128, 32])
    st_out = dout("st_out", [128, L * NSTO])
    sT_out = dout("sT_out", [128, L * 3 * 256])
    dbg_d = dout("dbg", [128, 8192]) if dbg else None

    S = Sched(nc)
    es = contextlib.ExitStack()
    with es:
        NA = 39100
        A = es.enter_context(nc.sbuf_tensor("A", [128, NA], F32))
        WR = es.enter_context(nc.sbuf_tensor("WR", [128, 4096], F32))
        HYR = es.enter_context(nc.sbuf_tensor("HYR", [128, 8 * NTT], F32))
        ACTR = es.enter_context(nc.sbuf_tensor("ACTR", [128, 11 * NTT], F32))
        PS = [es.enter_context(nc.psum_tensor("P%d" % i, [128, 512], F32)) for i in range(8)]
        cur = [0]

        def alloc(n):
            o = cur[0]
            cur[0] += n
            assert cur[0] <= NA, cur[0]
            return o

        def V(o, n):
            return A[:, o:o + n]

        psn = [0]

        def psum():
            b = PS[psn[0] % 6]
            psn[0] += 1
            return b

        def mm(out, lhsT, rhs, start=True, stop=True):
            S.op('pe', lambda e: e.matmul(out, lhsT=lhsT, rhs=rhs, start=start, stop=stop),
                 reads=[lhsT, rhs], writes=[out])

        def tr(out, in_, ident):
            S.op('pe', lambda e: e.transpose(out, in_, ident), reads=[in_, ident], writes=[out])

        def act(out, in_, func, bias=None, scale=1.0):
            kw = {}
            if bias is not None:
                kw['bias'] = bias
            S.op('act', lambda e: e.activation(out=out, in_=in_, func=func, scale=scale, **kw),
                 reads=[in_, bias, scale], writes=[out])

        def tt(eng, out, a, b, op):
            S.op(eng, lambda e: e.tensor_tensor(out=out, in0=a, in1=b, op=op), reads=[a, b], writes=[out])

        def ts(eng, out, a, s1, op0, s2=None, op1=None):
            if op1 is None:
                S.op(eng, lambda e: e.tensor_scalar(out=out, in0=a, scalar1=s1, scalar2=None, op0=op0),
                     reads=[a, s1], writes=[out])
            else:
                S.op(eng, lambda e: e.tensor_scalar(out=out, in0=a, scalar1=s1, scalar2=s2, op0=op0, op1=op1),
                     reads=[a, s1, s2], writes=[out])

        def stt(out, in0, scalar, in1, op0, op1):
            S.op('dve', lambda e: e.scalar_tensor_tensor(out=out, in0=in0, scalar=scalar, in1=in1, op0=op0, op1=op1),
                 reads=[in0, scalar, in1], writes=[out])

        def scan(out, d0, d1, init, op0=ALU.mult, op1=ALU.add):
            S.op('dve', lambda e: e.tensor_tensor_scan(out=out, data0=d0, data1=d1, initial=init, op0=op0, op1=op1),
                 reads=[d0, d1, init], writes=[out])

        def cp(eng, out, in_):
            if eng == 'act':
                act(out, in_, AF.Copy)
            else:
                S.op(eng, lambda e: e.tensor_copy(out=out, in_=in_), reads=[in_], writes=[out])

        def memset(eng, out, val):
            S.op(eng, lambda e: e.memset(out, val), writes=[out])

        def recip(out, in_):
            S.op('dve', lambda e: e.reciprocal(out=out, in_=in_), reads=[in_], writes=[out])

        dbgcol = [0]

        def dump(ap, n):
            if dbg_d is not None and dbgcol[0] + n <= 8192:
                S.dma(dbg_d[0:ap.shape[0], dbgcol[0]:dbgcol[0] + n], ap)
                dbgcol[0] += n

        o_const = alloc(NCONST)
        CON = V(o_const, NCONST)
        ident = A[:, o_const + C_ID:o_const + C_ID + 128]
        bones = A[:, o_const + C_BONES:o_const + C_BONES + 128]
        ones = A[:, o_const + C_ONES:o_const + C_ONES + 128]
        o_vec = alloc(L * NV)
        o_cT = alloc(32)
        o_mod = alloc(L * 48 * 4)
        o_sc = alloc(L * 2 * 8 * 4)
        o_st = alloc(L * NSTO)
        o_sT = alloc(L * 3 * 256)
        o_misc = alloc(64)
        o_lrusp = alloc(L * 2)
        o_lrusp2 = alloc(L * 2)
        o_omka = alloc(L * 4)
        o_s5rho = alloc(L * 8)
        o_s5t = alloc(L * 5 * 8 * SB)
        o_smw = alloc(NSW)
        o_stg = alloc(2048)
        o_X = alloc(8 * NTT)
        o_R1 = alloc(10400)
        o_R2 = alloc(9216)

        def vec(l, c, n=1):
            return A[:, o_vec + l * NV + c:o_vec + l * NV + c + n]

        def stc(l, c, n=1):
            return A[:, o_st + l * NSTO + c:o_st + l * NSTO + c + n]

        def s5tab(l, which, st_):
            o = o_s5t + ((l * 5 + which) * 8 + st_) * SB
            return A[:, o:o + SB]

        EPS = A[:, o_misc:o_misc + 1]
        GNEPS = A[:, o_misc + 1:o_misc + 2]
        ONEC = A[:, o_misc + 2:o_misc + 3]
        TINY = A[:, o_misc + 3:o_misc + 4]

        S.dma(CON, consts_d)
        S.dma(V(o_vec, L * NV).rearrange("p (l n) -> p l n", l=L), vecs_d.rearrange("l p n -> p l n"))
        S.dma(V(o_cT, 32), cT)
        S.dma(V(o_st, L * NSTO), st_in)
        S.dma(V(o_sT, L * 768), sT_in)
        memset('pool', EPS, 1e-6)
        memset('pool', GNEPS, 64e-5)
        memset('pool', ONEC, 1.0)
        memset('pool', TINY, 1e-24)

        slab_n = [0]

        def load_slab(src, n=2048, rnd=True):
            if not rnd:
                S.dma(V(o_stg, n), src)
                return None
            o = (slab_n[0] % 2) * 2048
            slab_n[0] += 1
            q = n // 4
            for i in range(4):
                S.dma(V(o_stg + i * q, q), src[:, i * q:(i + 1) * q])
                cp('act' if i % 2 == 0 else 'dve', RR(WR[:, o + i * q:o + (i + 1) * q]), V(o_stg + i * q, q))
            return o

        sc_ = V(o_cT, 32)
        sig = V(o_R2, 32)
        act(sig, sc_, AF.Sigmoid)
        tt('pool', sc_, sc_, sig, ALU.mult)
        for l in range(L):
            for sl in range(24):
                load_slab(wada_d[l, sl], rnd=False)
                o = o_stg
                ps = psum()
                for m in range(2):
                    for k in range(8):
                        mm(ps[:, m * 4:m * 4 + 4], A[:, o + k * 256 + m * 128:o + k * 256 + m * 128 + 128],
                           A[:, o_cT + k * 4:o_cT + k * 4 + 4], start=(k == 0), stop=(k == 7))
                for m in range(2):
                    mt = sl * 2 + m
                    act(A[:, o_mod + (l * 48 + mt) * 4:o_mod + (l * 48 + mt) * 4 + 4], ps[:, m * 4:m * 4 + 4],
                        AF.Identity, bias=vec(l, V_BADA + mt))

        def modc(l, j, ct, seq):
            o = o_mod + (l * 48 + j * 8 + ct) * 4 + seq
            return A[:, o:o + 1]

        def nsc(l, which, ct, seq):
            o = o_sc + ((l * 2 + which) * 8 + ct) * 4 + seq
            return A[:, o:o + 1]

        for l in range(L):
            for which, (jj, vv) in enumerate(((1, V_NM), (4, V_NF))):
                for ct in range(8):
                    o = o_sc + ((l * 2 + which) * 8 + ct) * 4
                    om = o_mod + (l * 48 + jj * 8 + ct) * 4
                    ts('pool', A[:, o:o + 4], A[:, om:om + 4], ONEC, ALU.add, vec(l, vv + ct), ALU.mult)

        for l in range(L):
            t = V(o_R2, 2)
            act(t, vec(l, V_LLAM, 2), AF.Exp, scale=-1.0)
            act(t, t, AF.Ln, bias=ONEC)
            ts('pool', A[:, o_lrusp + l * 2:o_lrusp + l * 2 + 2], t, -8.0, ALU.mult)
            ts('pool', A[:, o_lrusp2 + l * 2:o_lrusp2 + l * 2 + 2], t, -16.0, ALU.mult)
            ts('pool', A[:, o_omka + l * 4:o_omka + l * 4 + 4], vec(l, V_KA, 4), -1.0, ALU.mult, 1.0, ALU.add)
            W8 = [V(o_R2 + 16 + 8 * i, 8) for i in range(24)]
            dt, mag, th, tq, ti, fr_, cs_, sn_ = W8[0:8]
            act(dt, vec(l, V_S5DT, 8), AF.Exp)
            tt('pool', mag, dt, vec(l, V_S5AR, 8), ALU.mult)
            act(mag, mag, AF.Exp)
            cp('pool', A[:, o_s5rho + l * 8:o_s5rho + l * 8 + 8], mag)
            tt('pool', th, dt, vec(l, V_S5AI, 8), ALU.mult)
            for (dst, offs) in ((sn_, 0.5), (cs_, 0.75)):
                ts('dve', tq, th, 1.0 / (2 * math.pi), ALU.mult, offs, ALU.add)
                tq_i = tq.bitcast(mybir.dt.int32)
                S.op('dve', lambda e, a=ti.bitcast(mybir.dt.int32), b=tq: e.tensor_copy(out=a, in_=b),
                     reads=[tq], writes=[ti])
                S.op('dve', lambda e, a=fr_, b=ti.bitcast(mybir.dt.int32): e.tensor_copy(out=a, in_=b),
                     reads=[ti], writes=[fr_])
                tt('dve', fr_, tq, fr_, ALU.subtract)
                ts('dve', ti, fr_, 0.0, ALU.is_lt)
                tt('dve', fr_, fr_, ti, ALU.add)
                ts('dve', fr_, fr_, 2 * math.pi, ALU.mult, -math.pi, ALU.add)
                ts('dve', fr_, fr_, math.pi, ALU.min, -math.pi, ALU.max)
                act(dst, fr_, AF.Sin)
            abr, abi, den, frr, fii, t1, t2, t3 = W8[8:16]
            tt('pool', abr, mag, cs_, ALU.mult)
            tt('pool', abi, mag, sn_, ALU.mult)
            are, aim = vec(l, V_S5AR, 8), vec(l, V_S5AI, 8)
            tt('pool', t1, are, are, ALU.mult)
            tt('pool', t2, aim, aim, ALU.mult)
            tt('pool', den, t1, t2, ALU.add)
            recip(den, den)
            ts('pool', t3, abr, -1.0, ALU.add)
            tt('pool', t1, t3, are, ALU.mult)
            tt('pool', t2, abi, aim, ALU.mult)
            tt('pool', t1, t1, t2, ALU.add)
            tt('pool', frr, t1, den, ALU.mult)
            tt('pool', t1, abi, are, ALU.mult)
            tt('pool', t2, t3, aim, ALU.mult)
            tt('pool', t1, t1, t2, ALU.subtract)
            tt('pool', fii, t1, den, ALU.mult)
            for st_ in range(8):
                Er, Ei = s5tab(l, 2, st_), s5tab(l, 3, st_)
                cp('pool', Er[:, 0:1], cs_[:, st_:st_ + 1])
                cp('pool', Ei[:, 0:1], sn_[:, st_:st_ + 1])
                n = 1
                tmpa, tmpb = V(o_R2 + 256, SB), V(o_R2 + 256 + SB, SB)
                while n < SB:
                    cr, ci = Er[:, n - 1:n], Ei[:, n - 1:n]
                    ts('dve', tmpa[:, 0:n], Er[:, 0:n], cr, ALU.mult)
                    ts('dve', tmpb[:, 0:n], Ei[:, 0:n], ci, ALU.mult)
                    tt('dve', Er[:, n:2 * n], tmpa[:, 0:n], tmpb[:, 0:n], ALU.subtract)
                    ts('dve', tmpa[:, 0:n], Er[:, 0:n], ci, ALU.mult)
                    ts('dve', tmpb[:, 0:n], Ei[:, 0:n], cr, ALU.mult)
                    tt('dve', Ei[:, n:2 * n], tmpa[:, 0:n], tmpb[:, 0:n], ALU.add)
                    n *= 2
                Epr, Epi = s5tab(l, 0, st_), s5tab(l, 1, st_)
                fr1, fi1 = frr[:, st_:st_ + 1], fii[:, st_:st_ + 1]
                ts('dve', tmpa, Er, fr1, ALU.mult)
                ts('dve', tmpb, Ei, fi1, ALU.mult)
                tt('dve', Epr, tmpa, tmpb, ALU.add)
                ts('dve', tmpa, Er, fi1, ALU.mult)
                ts('dve', tmpb, Ei, fr1, ALU.mult)
                tt('dve', Epi, tmpa, tmpb, ALU.subtract)
                cp('pool', s5tab(l, 4, st_), mag[:, st_:st_ + 1].to_broadcast([128, SB]))

        Xb = [V(o_X + ct * NTT, NTT) for ct in range(8)]
        HYb = [HYR[:, ct * NTT:(ct + 1) * NTT] for ct in range(8)]
        FINb = [V(o_R1 + ct * NTT, NTT) for ct in range(8)]

        def rmsnorm_mod(l, which, NT, segs, final=False):
            ps = psum()
            for ct in range(8):
                sq = V(o_R2 + (ct % 2) * NTT, NTT)
                act(sq[:, 0:NT], Xb[ct][:, 0:NT], AF.Square)
                mm(ps[:, 0:NT], ones, sq[:, 0:NT], start=(ct == 0), stop=(ct == 7))
            rstd = V(o_R2 + 2 * NTT, NTT)
            act(rstd[:, 0:NT], ps[:, 0:NT], AF.Sqrt, bias=EPS, scale=1.0 / D)
            recip(rstd[:, 0:NT], rstd[:, 0:NT])
            for ct in range(8):
                ntmp = V(o_R2 + (3 + ct % 2) * NTT, NTT)
                tt('pool', ntmp[:, 0:NT], Xb[ct][:, 0:NT], rstd[:, 0:NT], ALU.mult)
                for (seq, c0, ln) in segs:
                    if final:
                        ts('dve', FINb[ct][:, c0:c0 + ln], ntmp[:, c0:c0 + ln], vec(0, V_NFIN + ct), ALU.mult)
                    else:
                        act(RR(HYb[ct][:, c0:c0 + ln]), ntmp[:, c0:c0 + ln], AF.Identity,
                            bias=modc(l, 0 if which == 0 else 3, ct, seq), scale=nsc(l, which, ct, seq))

        def process(NT, segs, C, src, dst):
            if stage < 1:
                return
            nseg = len(segs)
            SL = segs[0][2]
            S.dma(V(o_X, 8 * NTT).rearrange("p (c t) -> p c t", c=8)[:, :, 0:NT], src.rearrange("c p t -> p c t"))
            for l in range(L):
                S.dma(V(o_smw, NSW), smallw_d[l])
                ts('pool', V(o_smw + W_CIM, 512), V(o_smw + W_CIM, 512), -1.0, ALU.mult)
                smw = lambda c, n: A[:, o_smw + c:o_smw + c + n]
                rmsnorm_mod(l, 0, NT, segs)
                o_G = o_R1
                o_LX = o_G + 2 * NTT
                o_PR = o_LX + 2 * 520
                o_U5 = o_PR + 14 * 520
                assert o_U5 + 2 * NTT <= o_R1 + 10400
                Gb = [V(o_G + i * NTT, NTT) for i in range(2)]
                LXf = [V(o_LX + i * 520, 520) for i in range(2)]
                PRf = [V(o_PR + i * 520, 520) for i in range(14)]
                U5 = [V(o_U5 + i * NTT, NTT) for i in range(2)]

                def segview(buf, H):
                    return buf[:, 0:nseg * (H + SL)].rearrange("p (s t) -> p s t", s=nseg)

                for si, (seq, c0, ln) in enumerate(segs):
                    for ct in range(2):
                        cp('pool', segview(LXf[ct], 3)[:, si, 0:3], stc(l, O_LH + (ct * 3 + seq) * 3, 3))
                    for ct in range(14):
                        cp('pool', segview(PRf[ct], 2)[:, si, 1:2], stc(l, O_SH + ct * 3 + seq))
                for sl in range(10):
                    o = load_slab(win_d[l, sl])
                    pss = [psum() for _ in range(2)]
                    for k in range(8):
                        for m in range(2):
                            mm(pss[m][:, 0:NT], RR(WR[:, o + k * 256 + m * 128:o + k * 256 + m * 128 + 128]),
                               RR(HYb[k][:, 0:NT]), start=(k == 0), stop=(k == 7))
                    for m in range(2):
                        mt = sl * 2 + m
                        src_ps = pss[m][:, 0:NT]
                        if mt < 2:
                            cp('act', Gb[mt][:, 0:NT], src_ps)
                        elif mt < 4:
                            cp('act', segview(LXf[mt - 2], 3)[:, :, 3:3 + SL], src_ps.rearrange("p (s t) -> p s t", s=nseg))
                        elif mt < 18:
                            cp('act' if mt % 2 else 'dve', segview(PRf[mt - 4], 2)[:, :, 2:2 + SL],
                               src_ps.rearrange("p (s t) -> p s t", s=nseg))
                        else:
                            cp('dve', U5[mt - 18][:, 0:NT], src_ps)
                for si, (seq, c0, ln) in enumerate(segs):
                    for ct in range(2):
                        cp('pool', stc(l, O_LH + (ct * 3 + seq) * 3, 3), segview(LXf[ct], 3)[:, si, SL:SL + 3])
                    for ct in range(14):
                        cp('pool', stc(l, O_SH + ct * 3 + seq), segview(PRf[ct], 2)[:, si, SL + 1:SL + 2])

                if stage < 2:
                    continue
                for ct in range(2):
                    xc = V(o_R2, NTT)
                    xv = segview(LXf[ct], 3)
                    xc3 = xc[:, 0:NT].rearrange("p (s t) -> p s t", s=nseg)
                    cw = lambda k: vec(l, V_LCW + ct * 4 + k)
                    ts('dve', xc3, xv[:, :, 0:SL], cw(0), ALU.mult, vec(l, V_LCB + ct), ALU.add)
                    for k in range(1, 4):
                        stt(xc3, xv[:, :, k:k + SL], cw(k), xc3, ALU.mult, ALU.add)
                    ps1, ps2 = psum(), psum()
                    mm(ps1[:, 0:NT], smw(W_WA + ct * 128, 128), xc[:, 0:NT])
                    mm(ps2[:, 0:NT], smw(W_WX + ct * 128, 128), xc[:, 0:NT])
                    rg, ig = V(o_R2 + NTT, NTT), V(o_R2 + 2 * NTT, NTT)
                    act(rg[:, 0:NT], ps1[:, 0:NT], AF.Sigmoid, bias=vec(l, V_LBA + ct))
                    act(ig[:, 0:NT], ps2[:, 0:NT], AF.Sigmoid, bias=vec(l, V_LBX + ct))
                    aa, gn = V(o_R2 + 3 * NTT, NTT), V(o_R2 + 4 * NTT, NTT)
                    act(aa[:, 0:NT], rg[:, 0:NT], AF.Exp, scale=A[:, o_lrusp + l * 2 + ct:o_lrusp + l * 2 + ct + 1])
                    act(gn[:, 0:NT], rg[:, 0:NT], AF.Exp, scale=A[:, o_lrusp2 + l * 2 + ct:o_lrusp2 + l * 2 + ct + 1])
                    ts('pool', gn[:, 0:NT], gn[:, 0:NT], -1.0, ALU.mult, 1.0, ALU.add)
                    ts('pool', gn[:, 0:NT], gn[:, 0:NT], 0.0, ALU.max)
                    act(gn[:, 0:NT], gn[:, 0:NT], AF.Sqrt)
                    tt('pool', ig[:, 0:NT], ig[:, 0:NT], xc[:, 0:NT], ALU.mult)
                    tt('pool', ig[:, 0:NT], ig[:, 0:NT], gn[:, 0:NT], ALU.mult)
                    hh = V(o_R2 + 5 * NTT, NTT)
                    for (seq, c0, ln) in segs:
                        hst = stc(l, O_Lh + ct * 3 + seq)
                        scan(hh[:, c0:c0 + ln], aa[:, c0:c0 + ln], ig[:, c0:c0 + ln], hst)
                        cp('pool', hst, hh[:, c0 + ln - 1:c0 + ln])
                    act(rg[:, 0:NT], Gb[ct][:, 0:NT], AF.Gelu_apprx_tanh)
                    tt('pool', RR(HYb[ct][:, 0:NT]), hh[:, 0:NT], rg[:, 0:NT], ALU.mult)

                if stage < 3:
                    continue
                o_s = o_R2
                nsb_list = []
                for (seq, c0, ln) in segs:
                    for s0 in range(0, ln, SB):
                        nsb_list.append((seq, c0 + s0, min(SB, ln - s0), s0 + SB >= ln, s0 == 0))
                SBL = nsb_list[0][2]
                nsb = len(nsb_list)
                ypss = [PS[6], PS[7]]
                for st_ in range(8):
                    kt, half, par = st_ // 4, (st_ % 4) // 2, st_ % 2
                    pr = slice(64 * half, 64 * half + 64)
                    pbr, pbi = psum(), psum()
                    mm(pbr[:, 0:NT], A[pr, o_smw + W_BRE + (kt * 2 + par) * 128:o_smw + W_BRE + (kt * 2 + par) * 128 + 128],
                       U5[kt][pr, 0:NT])
                    mm(pbi[:, 0:NT], A[pr, o_smw + W_BIM + (kt * 2 + par) * 128:o_smw + W_BIM + (kt * 2 + par) * 128 + 128],
                       U5[kt][pr, 0:NT])
                    base = o_s + (st_ % 2) * 4352
                    cre, cim, t1, t2, gre, gim, hre, him = [V(base + i * NTT, NTT) for i in range(8)]
                    def bt(tab):
                        return tab[:, 0:SBL].unsqueeze(1).to_broadcast([128, nsb, SBL])
                    v3 = lambda b: b[:, 0:NT].rearrange("p (s t) -> p s t", s=nsb)
                    Epr, Epi, Er, Ei, Rh = [s5tab(l, i, st_) for i in range(5)]
                    tt('dve', v3(t1), v3(pbr), bt(Epr), ALU.mult)
                    tt('dve', v3(t2), v3(pbi), bt(Epi), ALU.mult)
                    tt('pool', cre[:, 0:NT], t1[:, 0:NT], t2[:, 0:NT], ALU.subtract)
                    tt('dve', v3(t1), v3(pbi), bt(Epr), ALU.mult)
                    tt('dve', v3(t2), v3(pbr), bt(Epi), ALU.mult)
                    tt('pool', cim[:, 0:NT], t1[:, 0:NT], t2[:, 0:NT], ALU.add)
                    for (seq, c0, ln, last, first_sb) in nsb_list:
                        hr0, hi0 = stc(l, O_S5R + st_ * 3 + seq), stc(l, O_S5I + st_ * 3 + seq)
                        ini_r = hr0 if first_sb else hre[:, c0 - 1:c0]
                        ini_i = hi0 if first_sb else him[:, c0 - 1:c0]
                        cs = slice(c0, c0 + ln)
                        scan(gre[:, cs], Rh[:, 0:ln], cre[:, cs], ini_r)
                        scan(gim[:, cs], Rh[:, 0:ln], cim[:, cs], ini_i)
                        q1, q2, q3, q4 = [V(base + 4096 + i * 64, 64) for i in range(4)]
                        tt('pool', q1[:, 0:ln], gre[:, cs], Er[:, 0:ln], ALU.mult)
                        tt('pool', q2[:, 0:ln], gim[:, cs], Ei[:, 0:ln], ALU.mult)
                        tt('pool', hre[:, cs], q1[:, 0:ln], q2[:, 0:ln], ALU.subtract)
                        tt('dve', q3[:, 0:ln], gim[:, cs], Er[:, 0:ln], ALU.mult)
                        tt('dve', q4[:, 0:ln], gre[:, cs], Ei[:, 0:ln], ALU.mult)
                        tt('dve', him[:, cs], q3[:, 0:ln], q4[:, 0:ln], ALU.add)
                        if last:
                            cp('act', hr0, hre[:, c0 + ln - 1:c0 + ln])
                            cp('act', hi0, him[:, c0 + ln - 1:c0 + ln])
                    j, hf = st_ // 4, (st_ % 4) // 2
                    po = slice(64 * hf, 64 * hf + 64)
                    first = (st_ % 2 == 0)
                    mm(ypss[j][po, 0:NT], smw(W_CRE + st_ * 64, 64), hre[:, 0:NT], start=first, stop=False)
                    mm(ypss[j][po, 0:NT], smw(W_CIM + st_ * 64, 64), him[:, 0:NT], start=False, stop=(not first))
                zb = [V(o_s + i * NTT, NTT) for i in range(2)]
                for j in range(2):
                    stt(zb[j][:, 0:NT], U5[j][:, 0:NT], vec(l, V_S5D + j), ypss[j][:, 0:NT], ALU.mult, ALU.add)
                    act(zb[j][:, 0:NT], zb[j][:, 0:NT], AF.Gelu_apprx_tanh)
                for j in range(2):
                    ps = psum()
                    for k in range(2):
                        mm(ps[:, 0:NT], smw(W_GLU + k * 256 + j * 128, 128), zb[k][:, 0:NT], start=(k == 0), stop=(k == 1))
                    gt = V(o_s + 2 * NTT, NTT)
                    act(gt[:, 0:NT], ps[:, 0:NT], AF.Sigmoid, bias=vec(l, V_GLUB + j))
                    tt('pool', RR(HYb[6 + j][:, 0:NT]), zb[j][:, 0:NT], gt[:, 0:NT], ALU.mult)

                if stage < 4:
                    continue
                S.dma(V(o_smw, NSWB), smallwB_d[l])
                for ct in range(14):
                    pv = segview(PRf[ct], 2)
                    dtmp = V(o_R2 + (ct % 2) * NTT, NTT)
                    d3 = dtmp[:, 0:NT].rearrange("p (s t) -> p s t", s=nseg)
                    tt('pool', d3, pv[:, :, 1:1 + SL], pv[:, :, 2:2 + SL], ALU.subtract)
                    stt(pv[:, :, 2:2 + SL], d3, vec(l, V_MU + ct), pv[:, :, 2:2 + SL], ALU.mult, ALU.add)
                o_c = o_R2 + 2 * NTT

                def xm(ct):
                    if nseg == 1:
                        return PRf[ct][:, 2:2 + NT]
                    return None
                if nseg > 1:
                    for ct in range(14):
                        tmpc = V(o_c, NT)
                        cp('pool', tmpc.rearrange("p (s t) -> p s t", s=nseg), segview(PRf[ct], 2)[:, :, 2:2 + SL])
                        cp('pool', PRf[ct][:, 0:NT], tmpc)
                    xm = lambda ct: PRf[ct][:, 0:NT]
                lo_t = V(o_R2, NTT)
                act(lo_t[0:64, 0:NT], xm(12)[0:64, :], AF.Tanh)
                cp('pool', lo_t[64:128, 0:NT], xm(12)[64:128, :])
                gs_t = V(o_R2 + NTT, NTT)
                act(gs_t[:, 0:NT], xm(13), AF.Sigmoid)
                YR = [HYb[2 + hp] for hp in range(4)]
                for hp in range(4):
                    o_f = o_R2 + 2 * NTT
                    F = [V(o_f + i * NTT, NTT) for i in range(9)]
                    lw, aa, kkn, bb, csb, t1, t2, t3, t4 = F
                    r_, k_, v_ = xm(hp), xm(4 + hp), xm(8 + hp)
                    ps1, ps2, ps3 = psum(), psum(), psum()
                    mm(ps1[:, 0:NT], A[0:64, o_smw + W_LORA + hp * 128:o_smw + W_LORA + hp * 128 + 128], lo_t[0:64, 0:NT])
                    mm(ps2[:, 0:NT], A[64:128, o_smw + W_LORA + hp * 128:o_smw + W_LORA + hp * 128 + 128], lo_t[64:128, 0:NT])
                    mm(ps3[:, 0:NT], smw(W_G2 + hp * 128, 128), gs_t[:, 0:NT])
                    act(lw[:, 0:NT], ps1[:, 0:NT], AF.Sigmoid, bias=vec(l, V_W0 + hp))
                    ts('pool', lw[:, 0:NT], lw[:, 0:NT], -0.6065306597126334, ALU.mult)
                    act(aa[:, 0:NT], ps2[:, 0:NT], AF.Sigmoid, bias=vec(l, V_A0 + hp))
                    gg = t4
                    cp('act', gg[:, 0:NT], ps3[:, 0:NT])
                    ts('pool', kkn[:, 0:NT], k_, vec(l, V_KK + hp), ALU.mult)
                    tt('dve', t1[:, 0:NT], kkn[:, 0:NT], kkn[:, 0:NT], ALU.mult)
                    ps = psum()
                    mm(ps[:, 0:NT], bones, t1[:, 0:NT])
                    ts('dve', t1[:, 0:NT], ps[:, 0:NT], TINY, ALU.max)
                    act(t1[:, 0:NT], t1[:, 0:NT], AF.Sqrt)
                    recip(t1[:, 0:NT], t1[:, 0:NT])
                    tt('pool', kkn[:, 0:NT], kkn[:, 0:NT], t1[:, 0:NT], ALU.mult)
                    ts('pool', t1[:, 0:NT], aa[:, 0:NT], vec(l, V_KA + hp), ALU.mult,
                       A[:, o_omka + l * 4 + hp:o_omka + l * 4 + hp + 1], ALU.add)
                    tt('pool', k_, k_, t1[:, 0:NT], ALU.mult)
                    tt('dve', bb[:, 0:NT], kkn[:, 0:NT], aa[:, 0:NT], ALU.mult)
                    stt(t1[:, 0:NT], r_, vec(l, V_RK + hp), k_, ALU.mult, ALU.mult)
                    ps = psum()
                    mm(ps[:, 0:NT], bones, t1[:, 0:NT])
                    bonus = t3
                    tt('dve', bonus[:, 0:NT], ps[:, 0:NT], v_, ALU.mult)
                    rmask = A[:, o_const + C_RESET:o_const + C_RESET + NTT]
                    if C == CH:
                        scan(csb[:, 0:NT], rmask[:, 0:NT], lw[:, 0:NT], 0.0)
                    else:
                        for (seq, c0, ln) in segs:
                            scan(csb[:, c0:c0 + ln], rmask[:, 1:1 + ln], lw[:, c0:c0 + ln], 0.0)
                    nch = NT // C
                    c3 = lambda b: b[:, 0:NT].rearrange("p (c t) -> p c t", c=nch)
                    csC = c3(csb)[:, :, C - 1:C].to_broadcast([128, nch, C])
                    gex = t1
                    tt('pool', gex[:, 0:NT], csb[:, 0:NT], lw[:, 0:NT], ALU.subtract)
                    act(gex[:, 0:NT], gex[:, 0:NT], AF.Exp)
                    at_f = t1
                    tt('dve', at_f[:, 0:NT], kkn[:, 0:NT], gex[:, 0:NT], ALU.mult)
                    ig_ = kkn
                    act(ig_[:, 0:NT], csb[:, 0:NT], AF.Exp, scale=-1.0)
                    gC = lw
                    tt('pool', c3(gC), csC, c3(csb), ALU.subtract)
                    act(gC[:, 0:NT], gC[:, 0:NT], AF.Exp)
                    bh_f = aa
                    tt('dve', bh_f[:, 0:NT], bb[:, 0:NT], gC[:, 0:NT], ALU.mult)
                    tt('pool', bb[:, 0:NT], bb[:, 0:NT], ig_[:, 0:NT], ALU.mult)
                    kh_f = gC
                    tt('dve', kh_f[:, 0:NT], k_, gC[:, 0:NT], ALU.mult)
                    kt_f = ig_
                    tt('pool', kt_f[:, 0:NT], k_, ig_[:, 0:NT], ALU.mult)
                    act(csb[:, 0:NT], csb[:, 0:NT], AF.Exp)
                    gamC = csb
                    rt_f = t2
                    tt('dve', rt_f[:, 0:NT], r_, csb[:, 0:NT], ALU.mult)
                    bt_f = bb
                    o_m = o_R2 + 11 * NTT
                    chunks = []
                    for (seq, c0, ln) in segs:
                        for s0 in range(0, ln, C):
                            chunks.append((seq, c0 + s0))
                    for (seq, c0) in chunks:
                        if sub < 1:
                            break
                        cc = slice(c0, c0 + C)
                        pst = psum()
                        for i, srcf in enumerate((at_f, None, bh_f, kh_f)):
                            sap = v_[:, c0:c0 + C] if srcf is None else srcf[:, cc]
                            tr(pst[0:C, i * 128:i * 128 + 128], sap, ident)
                        TM = V(o_m, 512)
                        cp('act', TM[0:C, :], pst[0:C, :])
                        At_t, V_t, Bh_t, Kh_t = [TM[0:C, i * 128:i * 128 + 128] for i in range(4)]
                        U = []
                        for hl in range(2):
                            pr = slice(64 * hl, 64 * hl + 64)
                            ob = o_m + 512 + hl * 1280
                            NTm, MTm, Nm, PBm, QKm, Pa, PaT, Pb, PbT, Tm = [A[0:C, ob + i * 128:ob + i * 128 + C] for i in range(10)]
                            U.append(dict(pr=pr, S1=A[0:C, ob:ob + 128], NT=NTm, MT=MTm, N=Nm, PB=PBm, QK=QKm, Pa=Pa, PaT=PaT, Pb=Pb, PbT=PbT, T=Tm))
                        msu = A[0:C, o_const + C_MSU:o_const + C_MSU + C]
                        msl = A[0:C, o_const + C_MSL:o_const + C_MSL + C]
                        miu = A[0:C, o_const + C_MIU:o_const + C_MIU + C]
                        nmsu = A[0:C, o_const + C_NMSU:o_const + C_NMSU + C]
                        nmsl = A[0:C, o_const + C_NMSL:o_const + C_NMSL + C]
                        for u in U:
                            if not (bits & 2):
                                break
                            pr = u['pr']
                            pA, pB = psum(), psum()
                            mm(pA[0:C, 0:C], at_f[pr, cc], bt_f[pr, cc])
                            mm(pA[0:C, 128:128 + C], at_f[pr, cc], kt_f[pr, cc])
                            mm(pB[0:C, 0:C], bt_f[pr, cc], at_f[pr, cc])
                            mm(pB[0:C, 128:128 + C], bt_f[pr, cc], rt_f[pr, cc])
                            mm(pB[0:C, 256:256 + C], kt_f[pr, cc], rt_f[pr, cc])
                            if not (bits & 4):
                                continue
                            cp('act', u['PaT'], pA[0:C, 0:C]); tt('pool', u['PaT'], u['PaT'], nmsl, ALU.mult)
                            cp('act', u['MT'], pA[0:C, 128:128 + C]); tt('pool', u['MT'], u['MT'], msl, ALU.mult)
                            if not (bits & 16):
                                continue
                            cp('act', u['Pa'], pB[0:C, 0:C]); tt('pool', u['Pa'], u['Pa'], nmsu, ALU.mult)
                            cp('act', u['PB'], pB[0:C, 128:128 + C]); tt('pool', u['PB'], u['PB'], miu, ALU.mult)
                            cp('act', u['QK'], pB[0:C, 256:256 + C]); tt('pool', u['QK'], u['QK'], miu, ALU.mult)
                            if bits & 8:
                                tt('pool', u['T'], u['Pa'], ident[0:C, 0:C], ALU.add)
                        if sub < 2:
                            continue
                        nlev = int(round(math.log2(C))) - 1
                        for lev in range(1, nlev + 1):
                            for u in U:
                                Pp, PpT = (u['Pa'], u['PaT']) if lev % 2 == 1 else (u['Pb'], u['PbT'])
                                Pn, PnT = (u['Pb'], u['PbT']) if lev % 2 == 1 else (u['Pa'], u['PaT'])
                                pq = psum()
                                mm(pq[0:C, 0:C], Pp, PpT)
                                if lev < nlev:
                                    mm(pq[0:C, 128:128 + C], PpT, Pp)
                                cp('act', PnT, pq[0:C, 0:C])
                                if lev < nlev:
                                    cp('act', Pn, pq[0:C, 128:128 + C])
                                mm(pq[0:C, 256:256 + C], PnT, u['T'])
                                tt('dve', u['T'], pq[0:C, 256:256 + C], u['T'], ALU.add)
                        if sub < 3:
                            continue
                        for u in U:
                            pr = u['pr']
                            hc = slice(pr.start, pr.stop)
                            pq = psum()
                            mm(pq[0:C, 0:64], u['T'], At_t[:, hc])
                            mm(pq[0:C, 128:128 + C], u['MT'], u['T'])
                            Ab = u['S1'][:, 0:64]
                            MTT = u['N']
                            cp('act', Ab, pq[0:C, 0:64])
                            cp('act', MTT, pq[0:C, 128:128 + C])
                            mm(pq[0:C, 256:320], MTT, V_t[:, hc])
                            nUt = u['S1'][:, 64:128]
                            ts('dve', nUt, pq[0:C, 256:320], -1.0, ALU.mult)
                            pg = psum()
                            mm(pg[pr, 0:64], Ab, Bh_t[:, hc])
                            mm(pg[pr, 128:128 + C], Ab, u['PB'])
                            Gm = A[pr, o_m + 3072 + 0:o_m + 3072 + 64]
                            Rh = A[pr, o_m + 3072 + 64:o_m + 3072 + 64 + C]
                            stt(Gm, ident[pr, hc], gamC[pr, c0 + C - 1:c0 + C], pg[pr, 0:64], ALU.mult, ALU.subtract)
                            tt('dve', Rh, rt_f[pr, cc], pg[pr, 128:128 + C], ALU.subtract)
                            STm = A[pr, o_sT + (l * 3 + seq) * 256 + hp * 64:o_sT + (l * 3 + seq) * 256 + hp * 64 + 64]
                            ph = psum()
                            mm(ph[pr, 128:128 + C], nUt, u['PB'], start=True, stop=False)
                            mm(ph[pr, 128:128 + C], V_t[:, hc], u['QK'], start=False, stop=False)
                            mm(ph[pr, 128:128 + C], STm, Rh, start=False, stop=True)
                            cp('act', RR(YR[hp][pr, cc]), ph[pr, 128:128 + C])
                            ph2 = psum()
                            mm(ph2[pr, 0:64], Bh_t[:, hc], nUt, start=True, stop=False)
                            mm(ph2[pr, 0:64], Kh_t[:, hc], V_t[:, hc], start=False, stop=False)
                            mm(ph2[pr, 0:64], Gm, STm, start=False, stop=True)
                            cp('dve', STm, ph2[pr, 0:64])
                    y = YR[hp]
                    ps = psum()
                    mm(ps[:, 0:NT], bones, y[:, 0:NT])
                    yc = t1
                    stt(yc[:, 0:NT], ps[:, 0:NT], -1.0 / 64, y[:, 0:NT], ALU.mult, ALU.add)
                    tt('dve', t2[:, 0:NT], yc[:, 0:NT], yc[:, 0:NT], ALU.mult)
                    ps = psum()
                    mm(ps[:, 0:NT], bones, t2[:, 0:NT])
                    act(t2[:, 0:NT], ps[:, 0:NT], AF.Sqrt, bias=GNEPS, scale=1.0 / 64)
                    recip(t2[:, 0:NT], t2[:, 0:NT])
                    tt('pool', yc[:, 0:NT], yc[:, 0:NT], t2[:, 0:NT], ALU.mult)
                    ts('pool', yc[:, 0:NT], yc[:, 0:NT], vec(l, V_LNW + hp), ALU.mult, vec(l, V_LNB + hp), ALU.add)
                    tt('dve', yc[:, 0:NT], yc[:, 0:NT], bonus[:, 0:NT], ALU.add)
                    tt('pool', RR(y[:, 0:NT]), yc[:, 0:NT], gg[:, 0:NT], ALU.mult)

                if stage < 5:
                    continue
                for sl in range(4):
                    o = load_slab(wout_d[l, sl])
                    pss = [psum() for _ in range(2)]
                    for k in range(8):
                        for m in range(2):
                            mm(pss[m][:, 0:NT], RR(WR[:, o + k * 256 + m * 128:o + k * 256 + m * 128 + 128]),
                               RR(HYb[k][:, 0:NT]), start=(k == 0), stop=(k == 7))
                    for m in range(2):
                        ct = sl * 2 + m
                        for (seq, c0, ln) in segs:
                            stt(Xb[ct][:, c0:c0 + ln], pss[m][:, c0:c0 + ln], modc(l, 2, ct, seq), Xb[ct][:, c0:c0 + ln],
                                ALU.mult, ALU.add)
                if stage < 6:
                    continue
                rmsnorm_mod(l, 1, NT, segs)
                ACTB = [ACTR[:, i * NTT:(i + 1) * NTT] for i in range(11)]
                for half in range(2):
                    for j in range(11):
                        mg = half * 11 + j
                        o = load_slab(wup_d[l, mg])
                        pss = [psum() for _ in range(2)]
                        for k in range(8):
                            for m in range(2):
                                mm(pss[m][:, 0:NT], RR(WR[:, o + k * 256 + m * 128:o + k * 256 + m * 128 + 128]),
                                   RR(HYb[k][:, 0:NT]), start=(k == 0), stop=(k == 7))
                        cv = []
                        for m in range(2):
                            ctp = mg * 2 + m
                            bi = (j % 2) * 2 + m
                            ub = V(o_R2 + bi * 520, 520)
                            uv = segview(ub, 2)
                            for si, (seq, c0, ln) in enumerate(segs):
                                cp('pool', uv[:, si, 0:2], stc(l, O_FH + (ctp * 3 + seq) * 2, 2))
                            cp('act', uv[:, :, 2:2 + SL], pss[m][:, 0:NT].rearrange("p (s t) -> p s t", s=nseg))
                            for si, (seq, c0, ln) in enumerate(segs):
                                cp('pool', stc(l, O_FH + (ctp * 3 + seq) * 2, 2), uv[:, si, SL:SL + 2])
                            co = V(o_R2 + 4 * 520 + bi * NTT, NTT)
                            co3 = co[:, 0:NT].rearrange("p (s t) -> p s t", s=nseg)
                            cv.append((co, co3, uv, ctp))
                        for (co, co3, uv, ctp) in cv:
                            ts('dve', co3, uv[:, :, 0:SL], vec(l, V_FCW + ctp * 3), ALU.mult, vec(l, V_FCB + ctp), ALU.add)
                        for kk_ in (1, 2):
                            for (co, co3, uv, ctp) in cv:
                                stt(co3, uv[:, :, kk_:kk_ + SL], vec(l, V_FCW + ctp * 3 + kk_), co3, ALU.mult, ALU.add)
                        val, gate = cv[0][0], cv[1][0]
                        act(gate[:, 0:NT], gate[:, 0:NT], AF.Silu)
                        tt('pool', RR(ACTB[j][:, 0:NT]), val[:, 0:NT], gate[:, 0:NT], ALU.mult)
                    for mt in range(8):
                        o = load_slab(wdn_d[l, half, mt], 1408)
                        ps = psum()
                        for k in range(11):
                            mm(ps[:, 0:NT], RR(WR[:, o + k * 128:o + k * 128 + 128]), RR(ACTB[k][:, 0:NT]),
                               start=(k == 0), stop=(k == 10))
                        for (seq, c0, ln) in segs:
                            stt(Xb[mt][:, c0:c0 + ln], ps[:, c0:c0 + ln], modc(l, 5, mt, seq), Xb[mt][:, c0:c0 + ln],
                                ALU.mult, ALU.add)
            rmsnorm_mod(0, 0, NT, segs, final=True)
            S.dma(dst.rearrange("c p t -> p c t"), V(o_R1, 8 * NTT).rearrange("p (c t) -> p c t", c=8)[:, :, 0:NT])

        import os as _os
        _kt = _os.environ.get('KTILES', 'ps')
        for ti in range(ntp if 'p' in _kt else 0):
            process(NTT, [(0, 0, NTT)], CH, xp[:, :, ti * NTT:(ti + 1) * NTT], y_d[:, :, ti * NTT:(ti + 1) * NTT])
        if 's' in _kt:
            process(32, [(1, 0, 16), (2, 16, 16)], 16, xs, ys_d)
        S.dma(st_out, V(o_st, L * NSTO))
        S.dma(sT_out, V(o_sT, L * 768))
        S.finish()
        S.emit(es)
    return nc


def _col(v):
    v = np.asarray(v, np.float32).reshape(-1, 128)
    return np.ascontiguousarray(v.T)


def _ffn_base(ctp):
    m, i = divmod(ctp, 2)
    return 128 * m if i == 0 else DFF + 128 * m


def _shared(inp):
    f = lambda k: np.asarray(inp[k], np.float32)
    vecs = np.zeros((L, 128, NV), np.float32)
    smallw = np.zeros((L, 128, NSW), np.float32)
    smallwB = np.zeros((L, 128, NSWB), np.float32)
    for l in range(L):
        v = vecs[l]
        v[:, V_NM:V_NM + 8] = _col(f('norm_mix')[l])
        v[:, V_NF:V_NF + 8] = _col(f('norm_ffn')[l])
        v[:, V_BADA:V_BADA + 48] = _col(f('b_ada')[l])
        for ct in range(2):
            for k in range(4):
                v[:, V_LCW + ct * 4 + k] = f('lru_conv_w')[l, k, ct * 128:(ct + 1) * 128]
        v[:, V_LCB:V_LCB + 2] = _col(f('lru_conv_b')[l])
        v[:, V_LBA:V_LBA + 2] = _col(f('lru_ba')[l])
        v[:, V_LBX:V_LBX + 2] = _col(f('lru_bx')[l])
        v[:, V_LLAM:V_LLAM + 2] = _col(f('lru_lambda')[l])
        v[:, V_MU:V_MU + 14] = _col(f('rwkv_mu')[l])
        for nm, o in (('rwkv_w0', V_W0), ('rwkv_a0', V_A0), ('rwkv_k_k', V_KK), ('rwkv_k_a', V_KA),
                      ('rwkv_r_k', V_RK), ('rwkv_ln_w', V_LNW), ('rwkv_ln_b', V_LNB)):
            v[:, o:o + 4] = _col(f(nm)[l].reshape(-1))
        v[:, V_S5AR:V_S5AR + 8] = _col(f('s5_a_re')[l].reshape(-1))
        v[:, V_S5AI:V_S5AI + 8] = _col(f('s5_a_im')[l].reshape(-1))
        v[:, V_S5DT:V_S5DT + 8] = _col(np.repeat(f('s5_log_dt')[l], 64))
        v[:, V_S5D:V_S5D + 2] = _col(f('s5_d')[l])
        v[:, V_GLUB:V_GLUB + 2] = _col(f('s5_glu_b')[l])
        for ctp in range(44):
            b = _ffn_base(ctp)
            for k in range(3):
                v[:, V_FCW + ctp * 3 + k] = f('ffn_conv_w')[l, k, b:b + 128]
            v[:, V_FCB + ctp] = f('ffn_conv_b')[l, b:b + 128]
        v[:, V_NFIN:V_NFIN + 8] = _col(f('norm_final'))
        w = smallw[l]
        for nm, o in (('lru_wa', W_WA), ('lru_wx', W_WX)):
            for ct in range(2):
                for hh in range(2):
                    w[hh * 64:(hh + 1) * 64, o + ct * 128 + hh * 64:o + ct * 128 + hh * 64 + 64] = f(nm)[l, 2 * ct + hh]
        smallwB[l][0:64, W_LORA:W_LORA + 512] = f('rwkv_w2')[l]
        smallwB[l][64:128, W_LORA:W_LORA + 512] = f('rwkv_a2')[l]
        smallwB[l][:, W_G2:W_G2 + 512] = f('rwkv_g2')[l]
        for nm, o in (('s5_b_re', W_BRE), ('s5_b_im', W_BIM)):
            bsrc = f(nm)[l]
            for st_ in range(8):
                kt, half, par = st_ // 4, (st_ % 4) // 2, st_ % 2
                for gg in (2 * st_, 2 * st_ + 1):
                    p0 = gg * 16 - kt * 128
                    s0 = (gg - 2 * st_) * 64
                    c0 = o + (kt * 2 + par) * 128 + s0
                    w[p0:p0 + 16, c0:c0 + 64] = bsrc[gg].T
        for nm, o in (('s5_c_re', W_CRE), ('s5_c_im', W_CIM)):
            csrc = f(nm)[l]
            for st_ in range(8):
                for gg in (2 * st_, 2 * st_ + 1):
                    p0 = (gg - 2 * st_) * 64
                    c0 = o + st_ * 64 + (gg - 4 * (st_ // 2)) * 16
                    w[p0:p0 + 64, c0:c0 + 16] = csrc[gg].T
        w[:, W_GLU:W_GLU + 512] = f('s5_glu_w')[l].reshape(2, 128, 256).transpose(1, 0, 2).reshape(128, 512)
    consts = np.zeros((128, NCONST), np.float32)
    ii, jj = np.meshgrid(np.arange(128), np.arange(128), indexing='ij')
    consts[:, C_ID:C_ID + 128] = np.eye(128)
    consts[:, C_BONES:C_BONES + 128] = (ii // 64 == jj // 64)
    consts[:, C_ONES:C_ONES + 128] = 1.0
    consts[:, C_MSU:C_MSU + 128] = (ii < jj)
    consts[:, C_MSL:C_MSL + 128] = (jj < ii)
    consts[:, C_MIU:C_MIU + 128] = (ii <= jj)
    consts[:, C_RESET:C_RESET + 512] = (np.arange(512) % 128 != 0)[None, :]
    consts[:, C_NMSU:C_NMSU + 128] = -1.0 * (ii < jj)
    consts[:, C_NMSL:C_NMSL + 128] = -1.0 * (jj < ii)

    def slab(wm, ncol):
        n = ncol // 256
        return np.ascontiguousarray(wm.reshape(8, 128, n, 256).transpose(2, 1, 0, 3).reshape(n, 128, 2048))
    colidx = np.concatenate([np.arange(_ffn_base(c), _ffn_base(c) + 128) for c in range(44)])
    sh = dict(vecs=vecs, smallw=smallw, smallwB=smallwB, consts=consts)
    sh['win'] = np.stack([slab(f('w_in')[l], 2560) for l in range(L)])
    sh['wout'] = np.stack([slab(f('w_out')[l], 1024) for l in range(L)])
    sh['wup'] = np.stack([slab(f('ffn_up')[l][:, colidx], 5632) for l in range(L)])
    sh['wdn'] = np.stack([np.ascontiguousarray(f('ffn_down')[l].reshape(2, 11, 128, 8, 128).transpose(0, 3, 2, 1, 4).reshape(2, 8, 128, 1408))
                          for l in range(L)])
    sh['wada'] = np.stack([slab(f('w_ada')[l], 6144) for l in range(L)])
    return sh


def _core_inputs(inp, c, TP):
    f = lambda k: np.asarray(inp[k], np.float32)
    b, s0, s1 = c // 2, 2 * c, 2 * c + 1
    m = {}
    m['xp'] = np.ascontiguousarray(f('x_prompt')[b].T).reshape(8, 128, TP)
    m['xs'] = np.ascontiguousarray(f('x_sample')[[s0, s1]].reshape(32, D).T).reshape(8, 128, 32)
    cl = np.stack([f('c_prompt')[b], f('c_sample')[s0], f('c_sample')[s1], np.zeros(D, np.float32)])
    m['cT'] = np.ascontiguousarray(cl.reshape(4, 8, 128).transpose(2, 1, 0).reshape(128, 32))
    st = np.zeros((128, L * NSTO), np.float32)
    sT = np.zeros((128, L * 768), np.float32)
    for l in range(L):
        o = l * NSTO
        for seq, s in ((1, s0), (2, s1)):
            for ct in range(2):
                for j in range(3):
                    st[:, o + O_LH + (ct * 3 + seq) * 3 + j] = f('state_lru_conv')[l, s, j, ct * 128:(ct + 1) * 128]
                st[:, o + O_Lh + ct * 3 + seq] = f('state_lru_h')[l, s, ct * 128:(ct + 1) * 128]
            for ct in range(14):
                st[:, o + O_SH + ct * 3 + seq] = f('state_rwkv_shift')[l, s, ct * 128:(ct + 1) * 128]
            for st_ in range(8):
                st[:, o + O_S5R + st_ * 3 + seq] = f('state_s5_re')[l, s].reshape(-1)[st_ * 128:(st_ + 1) * 128]
                st[:, o + O_S5I + st_ * 3 + seq] = f('state_s5_im')[l, s].reshape(-1)[st_ * 128:(st_ + 1) * 128]
            for ctp in range(44):
                bb = _ffn_base(ctp)
                for j in range(2):
                    st[:, o + O_FH + (ctp * 3 + seq) * 2 + j] = f('state_ffn_conv')[l, s, j, bb:bb + 128]
            Sm = f('state_rwkv_S')[l, s].reshape(4, 2, 64, 64).transpose(1, 3, 0, 2).reshape(128, 256)
            sT[:, (l * 3 + seq) * 256:(l * 3 + seq + 1) * 256] = Sm
    m['st_in'] = st
    m['sT_in'] = sT
    return m


_NC_CACHE = {}


def kernel(**inp):
    TP = int(np.asarray(inp['x_prompt']).shape[1])
    ntp = TP // NTT
    import os
    if ntp not in _NC_CACHE:
        _NC_CACHE[ntp] = build(ntp, stage=int(os.environ.get('KSTAGE', '99')), sub=int(os.environ.get('KSUB', '99')), bits=int(os.environ.get('KBITS', '255')))
    nc = _NC_CACHE[ntp]
    sh = _shared(inp)
    in_maps = []
    for c in range(8):
        m = dict(sh)
        m.update(_core_inputs(inp, c, TP))
        in_maps.append(m)
    res = run_bass_kernel_spmd(nc, in_maps, core_ids=list(range(8)))
    R = res.results
    B, SB_ = 4, 16
    y_prompt = np.zeros((B, TP, D), np.float32)
    y_sample = np.zeros((SB_, 16, D), np.float32)

    def mk(nb):
        return [np.zeros((L, nb, 3, 256), np.float32), np.zeros((L, nb, 256), np.float32),
                np.zeros((L, nb, 1792), np.float32), np.zeros((L, nb, 8, 64, 64), np.float32),
                np.zeros((L, nb, 16, 64), np.float32), np.zeros((L, nb, 16, 64), np.float32),
                np.zeros((L, nb, 2, 2 * DFF), np.float32)]
    P, Sg = mk(B), mk(SB_)

    def unpack(dst, bi, st, sT, seq):
        for l in range(L):
            o = l * NSTO
            for ct in range(2):
                for j in range(3):
                    dst[0][l, bi, j, ct * 128:(ct + 1) * 128] = st[:, o + O_LH + (ct * 3 + seq) * 3 + j]
                dst[1][l, bi, ct * 128:(ct + 1) * 128] = st[:, o + O_Lh + ct * 3 + seq]
            for ct in range(14):
                dst[2][l, bi, ct * 128:(ct + 1) * 128] = st[:, o + O_SH + ct * 3 + seq]
            re = np.zeros(1024, np.float32)
            im = np.zeros(1024, np.float32)
            for st_ in range(8):
                re[st_ * 128:(st_ + 1) * 128] = st[:, o + O_S5R + st_ * 3 + seq]
                im[st_ * 128:(st_ + 1) * 128] = st[:, o + O_S5I + st_ * 3 + seq]
            dst[4][l, bi] = re.reshape(16, 64)
            dst[5][l, bi] = im.reshape(16, 64)
            for ctp in range(44):
                bb = _ffn_base(ctp)
                for j in range(2):
                    dst[6][l, bi, j, bb:bb + 128] = st[:, o + O_FH + (ctp * 3 + seq) * 2 + j]
            Sm = sT[:, (l * 3 + seq) * 256:(l * 3 + seq + 1) * 256].reshape(2, 64, 4, 64)
            dst[3][l, bi] = Sm.transpose(2, 0, 3, 1).reshape(8, 64, 64)

    for c in range(8):
        r = R[c]
        b, s0, s1 = c // 2, 2 * c, 2 * c + 1
        ys = np.asarray(r['ys']).reshape(D, 32).T
        y_sample[s0] = ys[0:16]
        y_sample[s1] = ys[16:32]
        st, sT = np.asarray(r['st_out']), np.asarray(r['sT_out'])
        unpack(Sg, s0, st, sT, 1)
        unpack(Sg, s1, st, sT, 2)
        if c % 2 == 0:
            y_prompt[b] = np.asarray(r['y']).reshape(D, TP).T
            unpack(P, b, st, sT, 0)
    return (y_prompt, y_sample, *P, *Sg)
```

```python
import bisect
import contextlib
import math
import numpy as np
import concourse.bass as bass
import concourse.mybir as mybir
from concourse.bass_utils import run_bass_kernel_spmd

F32 = mybir.dt.float32
F32R = mybir.dt.float32r
USE_R = True


def RR(ap):
    return ap.bitcast(F32R) if USE_R else ap
ALU = mybir.AluOpType
AF = mybir.ActivationFunctionType

ENG_ATTR = {'pe': 'tensor', 'act': 'scalar', 'dve': 'vector', 'pool': 'gpsimd', 'sp': 'sync'}
ERA = 30000
NSLOT = 8


class _Rec:
    __slots__ = ('w', 'r')

    def __init__(self, w=None, r=None):
        self.w = w
        self.r = dict(r) if r else {}


class _IMap:
    def __init__(self):
        self.b = [0, 1 << 60]
        self.rec = [_Rec()]

    def _split(self, x):
        i = bisect.bisect_right(self.b, x) - 1
        if self.b[i] != x:
            self.b.insert(i + 1, x)
            old = self.rec[i]
            self.rec.insert(i + 1, _Rec(old.w, old.r))

    def rng(self, lo, hi):
        self._split(lo)
        self._split(hi)
        i0 = bisect.bisect_left(self.b, lo)
        i1 = bisect.bisect_left(self.b, hi)
        return self.rec[i0:i1]


def _is_dram(ap):
    try:
        return 'DRam' in type(ap.tensor).__name__
    except Exception:
        return False


def ap_key(ap):
    pat = ap.ap
    pstride = pat[0][0]
    off = ap.offset
    col = off % pstride if pstride > 0 else off
    ext = 0
    for st, cnt in pat[1:]:
        ext += abs(st) * (cnt - 1)
    if ap.tensor.name.startswith('P'):
        return ap.tensor.name, 0, 512
    return ap.tensor.name, col, col + ext + 1


class Sched:
    def __init__(self, nc):
        self.nc = nc
        self.prog = {e: [] for e in ENG_ATTR}
        self.cnt = {e: 0 for e in ENG_ATTR}
        self.seen = {e: {} for e in ENG_ATTR}
        self.maps = {}
        self.semkeys = []
        self.semset = set()
        self.dma_n = {e: 0 for e in ENG_ATTR}
        self.slot_val = {}

    def _sem(self, key):
        if key not in self.semset:
            self.semset.add(key)
            self.semkeys.append(key)
        return key

    def _wait(self, eng, tok):
        key, val = tok
        if self.seen[eng].get(key, 0) >= val:
            return
        self.seen[eng][key] = val
        self.prog[eng].append(('wait', key, val))

    def _deps(self, eng, reads, writes, tok, pseudo):
        for ap in reads:
            name, lo, hi = ap_key(ap)
            m = self.maps.setdefault(name, _IMap())
            for rec in m.rng(lo, hi):
                if rec.w is not None and not (rec.w[2] == pseudo and pseudo == 'pe'):
                    self._wait(eng, rec.w[:2])
        for ap in writes:
            name, lo, hi = ap_key(ap)
            m = self.maps.setdefault(name, _IMap())
            for rec in m.rng(lo, hi):
                if rec.w is not None and rec.w[2] != pseudo:
                    self._wait(eng, rec.w[:2])
                for (k, e2), v in rec.r.items():
                    if e2 != pseudo:
                        self._wait(eng, (k, v))
        for ap in reads:
            name, lo, hi = ap_key(ap)
            for rec in self.maps[name].rng(lo, hi):
                rec.r[(tok[0], pseudo)] = tok[1]
        for ap in writes:
            name, lo, hi = ap_key(ap)
            for rec in self.maps[name].rng(lo, hi):
                rec.w = (tok[0], tok[1], pseudo)
                rec.r = {}

    def op(self, eng, fn, reads=(), writes=()):
        reads = [r for r in reads if r is not None and hasattr(r, 'tensor') and not _is_dram(r)]
        writes = [w for w in writes if not _is_dram(w)]
        n = self.cnt[eng] + 1
        self.cnt[eng] = n
        era, v = divmod(n - 1, ERA)
        key = self._sem((eng, era))
        tok = (key, v + 1)
        self._deps(eng, reads, writes, tok, eng)
        self.prog[eng].append(('op', fn, key))
        return tok

    def dma(self, out, in_, eng='sp'):
        j = self.dma_n[eng]
        self.dma_n[eng] = j + 1
        slot = j % NSLOT
        key = self._sem(('dma', eng, slot))
        prev = self.slot_val.get(key, 0)
        if prev > 0:
            self._wait(eng, (key, prev))
        tok = (key, prev + 16)
        self.slot_val[key] = prev + 16
        reads = [] if _is_dram(in_) else [in_]
        writes = [] if _is_dram(out) else [out]
        self._deps(eng, reads, writes, tok, 'dma_%s_%d_%d' % (eng, slot, j))
        self.prog[eng].append(('dma', out, in_, key))
        return tok

    def finish(self, eng='sp'):
        for key, val in self.slot_val.items():
            self._wait(eng, (key, val))

    def emit(self, es):
        nc = self.nc
        sems = {}
        for key in self.semkeys:
            sems[key] = es.enter_context(nc.semaphore("s_" + "_".join(str(x) for x in key)))
        block = es.enter_context(nc.Block())

        def run(eng_name):
            def body(e):
                for item in self.prog[eng_name]:
                    if item[0] == 'wait':
                        e.wait_ge(sems[item[1]], item[2])
                    elif item[0] == 'op':
                        item[1](e).then_inc(sems[item[2]], 1)
                    else:
                        e.dma_start(out=item[1], in_=item[2]).then_inc(sems[item[3]], 16)
            return body

        block.tensor(run('pe'))
        block.scalar(run('act'))
        block.vector(run('dve'))
        block.gpsimd(run('pool'))
        block.sync(run('sp'))


D = 1024
L = 2
NTT = 512
CH = 128
SB = 64
DFF = 2816
O_LH, O_Lh, O_SH, O_S5R, O_S5I, O_FH, NSTO = 0, 18, 24, 66, 90, 114, 378
V_NM, V_NF, V_BADA, V_LCW, V_LCB, V_LBA, V_LBX, V_LLAM = 0, 8, 16, 64, 72, 74, 76, 78
V_MU, V_W0, V_A0, V_KK, V_KA, V_RK, V_LNW, V_LNB = 80, 94, 98, 102, 106, 110, 114, 118
V_S5AR, V_S5AI, V_S5DT, V_S5D, V_GLUB, V_FCW, V_FCB, V_NFIN, NV = 122, 130, 138, 146, 148, 150, 282, 326, 334
W_WA, W_WX, W_GLU, W_BRE, W_BIM, W_CRE, W_CIM, NSW = 0, 256, 512, 1024, 1536, 2048, 2560, 3072
W_LORA, W_G2, NSWB = 0, 512, 1024
C_ID, C_BONES, C_ONES, C_MSU, C_MSL, C_MIU, C_RESET, C_NMSU, C_NMSL, NCONST = 0, 128, 256, 384, 512, 640, 768, 1280, 1408, 1536


def build(ntp, dbg=False, stage=99, sub=99, bits=255):
    TP = ntp * NTT
    nc = bass.Bass("TRN2", target_bir_lowering=False)

    def din(name, shape):
        return nc.dram_tensor(name, list(shape), F32, kind="ExternalInput").ap()

    def dout(name, shape):
        return nc.dram_tensor(name, list(shape), F32, kind="ExternalOutput").ap()

    xp = din("xp", [8, 128, TP])
    xs = din("xs", [8, 128, 32])
    cT = din("cT", [128, 32])
    vecs_d = din("vecs", [L, 128, NV])
    smallw_d = din("smallw", [L, 128, NSW])
    smallwB_d = din("smallwB", [L, 128, NSWB])
    consts_d = din("consts", [128, NCONST])
    win_d = din("win", [L, 10, 128, 2048])
    wout_d = din("wout", [L, 4, 128, 2048])
    wup_d = din("wup", [L, 22, 128, 2048])
    wdn_d = din("wdn", [L, 2, 8, 128, 1408])
    wada_d = din("wada", [L, 24, 128, 2048])
    st_in = din("st_in", [128, L * NSTO])
    sT_in = din("sT_in", [128, L * 3 * 256])
    y_d = dout("y", [8, 128, TP])
    ys_d = dout("ys", [8, 128, 32])
    st_out = dout("st_out", [128, L * NSTO])
    sT_out = dout("sT_out", [128, L * 3 * 256])
    dbg_d = dout("dbg", [128, 8192]) if dbg else None

    S = Sched(nc)
    es = contextlib.ExitStack()
    with es:
        NA = 39100
        A = es.enter_context(nc.sbuf_tensor("A", [128, NA], F32))
        WR = es.enter_context(nc.sbuf_tensor("WR", [128, 4096], F32))
        HYR = es.enter_context(nc.sbuf_tensor("HYR", [128, 8 * NTT], F32))
        ACTR = es.enter_context(nc.sbuf_tensor("ACTR", [128, 11 * NTT], F32))
        PS = [es.enter_context(nc.psum_tensor("P%d" % i, [128, 512], F32)) for i in range(8)]
        cur = [0]

        def alloc(n):
            o = cur[0]
            cur[0] += n
            assert cur[0] <= NA, cur[0]
            return o

        def V(o, n):
            return A[:, o:o + n]

        psn = [0]

        def psum():
            b = PS[psn[0] % 6]
            psn[0] += 1
            return b

        def mm(out, lhsT, rhs, start=True, stop=True):
            S.op('pe', lambda e: e.matmul(out, lhsT=lhsT, rhs=rhs, start=start, stop=stop),
                 reads=[lhsT, rhs], writes=[out])

        def tr(out, in_, ident):
            S.op('pe', lambda e: e.transpose(out, in_, ident), reads=[in_, ident], writes=[out])

        def act(out, in_, func, bias=None, scale=1.0):
            kw = {}
            if bias is not None:
                kw['bias'] = bias
            S.op('act', lambda e: e.activation(out=out, in_=in_, func=func, scale=scale, **kw),
                 reads=[in_, bias, scale], writes=[out])

        def tt(eng, out, a, b, op):
            S.op(eng, lambda e: e.tensor_tensor(out=out, in0=a, in1=b, op=op), reads=[a, b], writes=[out])

        def ts(eng, out, a, s1, op0, s2=None, op1=None):
            if op1 is None:
                S.op(eng, lambda e: e.tensor_scalar(out=out, in0=a, scalar1=s1, scalar2=None, op0=op0),
                     reads=[a, s1], writes=[out])
            else:
                S.op(eng, lambda e: e.tensor_scalar(out=out, in0=a, scalar1=s1, scalar2=s2, op0=op0, op1=op1),
                     reads=[a, s1, s2], writes=[out])

        def stt(out, in0, scalar, in1, op0, op1):
            S.op('dve', lambda e: e.scalar_tensor_tensor(out=out, in0=in0, scalar=scalar, in1=in1, op0=op0, op1=op1),
                 reads=[in0, scalar, in1], writes=[out])

        def scan(out, d0, d1, init, op0=ALU.mult, op1=ALU.add):
            S.op('dve', lambda e: e.tensor_tensor_scan(out=out, data0=d0, data1=d1, initial=init, op0=op0, op1=op1),
                 reads=[d0, d1, init], writes=[out])

        def cp(eng, out, in_):
            if eng == 'act':
                act(out, in_, AF.Copy)
            else:
                S.op(eng, lambda e: e.tensor_copy(out=out, in_=in_), reads=[in_], writes=[out])

        def memset(eng, out, val):
            S.op(eng, lambda e: e.memset(out, val), writes=[out])

        def recip(out, in_):
            S.op('dve', lambda e: e.reciprocal(out=out, in_=in_), reads=[in_], writes=[out])

        dbgcol = [0]

        def dump(ap, n):
            if dbg_d is not None and dbgcol[0] + n <= 8192:
                S.dma(dbg_d[0:ap.shape[0], dbgcol[0]:dbgcol[0] + n], ap)
                dbgcol[0] += n

        o_const = alloc(NCONST)
        CON = V(o_const, NCONST)
        ident = A[:, o_const + C_ID:o_const + C_ID + 128]
        bones = A[:, o_const + C_BONES:o_const + C_BONES + 128]
        ones = A[:, o_const + C_ONES:o_const + C_ONES + 128]
        o_vec = alloc(L * NV)
        o_cT = alloc(32)
        o_mod = alloc(L * 48 * 4)
        o_sc = alloc(L * 2 * 8 * 4)
        o_st = alloc(L * NSTO)
        o_sT = alloc(L * 3 * 256)
        o_misc = alloc(64)
        o_lrusp = alloc(L * 2)
        o_lrusp2 = alloc(L * 2)
        o_omka = alloc(L * 4)
        o_s5rho = alloc(L * 8)
        o_s5t = alloc(L * 5 * 8 * SB)
        o_smw = alloc(NSW)
        o_stg = alloc(2048)
        o_X = alloc(8 * NTT)
        o_R1 = alloc(10400)
        o_R2 = alloc(9216)

        def vec(l, c, n=1):
            return A[:, o_vec + l * NV + c:o_vec + l * NV + c + n]

        def stc(l, c, n=1):
            return A[:, o_st + l * NSTO + c:o_st + l * NSTO + c + n]

        def s5tab(l, which, st_):
            o = o_s5t + ((l * 5 + which) * 8 + st_) * SB
            return A[:, o:o + SB]

        EPS = A[:, o_misc:o_misc + 1]
        GNEPS = A[:, o_misc + 1:o_misc + 2]
        ONEC = A[:, o_misc + 2:o_misc + 3]
        TINY = A[:, o_misc + 3:o_misc + 4]

        S.dma(CON, consts_d)
        S.dma(V(o_vec, L * NV).rearrange("p (l n) -> p l n", l=L), vecs_d.rearrange("l p n -> p l n"))
        S.dma(V(o_cT, 32), cT)
        S.dma(V(o_st, L * NSTO), st_in)
        S.dma(V(o_sT, L * 768), sT_in)
        memset('pool', EPS, 1e-6)
        memset('pool', GNEPS, 64e-5)
        memset('pool', ONEC, 1.0)
        memset('pool', TINY, 1e-24)

        slab_n = [0]

        def load_slab(src, n=2048, rnd=True):
            if not rnd:
                S.dma(V(o_stg, n), src)
                return None
            o = (slab_n[0] % 2) * 2048
            slab_n[0] += 1
            q = n // 4
            for i in range(4):
                S.dma(V(o_stg + i * q, q), src[:, i * q:(i + 1) * q])
                cp('act' if i % 2 == 0 else 'dve', RR(WR[:, o + i * q:o + (i + 1) * q]), V(o_stg + i * q, q))
            return o

        def slab_iter(srcs):
            nxt = load_slab(*srcs[0])
            for i in range(len(srcs)):
                cur = nxt
                if i + 1 < len(srcs):
                    nxt = load_slab(*srcs[i + 1])
                yield cur

        sc_ = V(o_cT, 32)
        sig = V(o_R2, 32)
        act(sig, sc_, AF.Sigmoid)
        tt('pool', sc_, sc_, sig, ALU.mult)
        for l in range(L):
            for sl in range(24):
                load_slab(wada_d[l, sl], rnd=False)
                o = o_stg
                ps = psum()
                for m in range(2):
                    for k in range(8):
                        mm(ps[:, m * 4:m * 4 + 4], A[:, o + k * 256 + m * 128:o + k * 256 + m * 128 + 128],
                           A[:, o_cT + k * 4:o_cT + k * 4 + 4], start=(k == 0), stop=(k == 7))
                for m in range(2):
                    mt = sl * 2 + m
                    act(A[:, o_mod + (l * 48 + mt) * 4:o_mod + (l * 48 + mt) * 4 + 4], ps[:, m * 4:m * 4 + 4],
                        AF.Identity, bias=vec(l, V_BADA + mt))

        def modc(l, j, ct, seq):
            o = o_mod + (l * 48 + j * 8 + ct) * 4 + seq
            return A[:, o:o + 1]

        def nsc(l, which, ct, seq):
            o = o_sc + ((l * 2 + which) * 8 + ct) * 4 + seq
            return A[:, o:o + 1]

        for l in range(L):
            for which, (jj, vv) in enumerate(((1, V_NM), (4, V_NF))):
                for ct in range(8):
                    o = o_sc + ((l * 2 + which) * 8 + ct) * 4
                    om = o_mod + (l * 48 + jj * 8 + ct) * 4
                    ts('pool', A[:, o:o + 4], A[:, om:om + 4], ONEC, ALU.add, vec(l, vv + ct), ALU.mult)

        for l in range(L):
            t = V(o_R2, 2)
            act(t, vec(l, V_LLAM, 2), AF.Exp, scale=-1.0)
            act(t, t, AF.Ln, bias=ONEC)
            ts('pool', A[:, o_lrusp + l * 2:o_lrusp + l * 2 + 2], t, -8.0, ALU.mult)
            ts('pool', A[:, o_lrusp2 + l * 2:o_lrusp2 + l * 2 + 2], t, -16.0, ALU.mult)
            ts('pool', A[:, o_omka + l * 4:o_omka + l * 4 + 4], vec(l, V_KA, 4), -1.0, ALU.mult, 1.0, ALU.add)
            W8 = [V(o_R2 + 16 + 8 * i, 8) for i in range(24)]
            dt, mag, th, tq, ti, fr_, cs_, sn_ = W8[0:8]
            act(dt, vec(l, V_S5DT, 8), AF.Exp)
            tt('pool', mag, dt, vec(l, V_S5AR, 8), ALU.mult)
            act(mag, mag, AF.Exp)
            cp('pool', A[:, o_s5rho + l * 8:o_s5rho + l * 8 + 8], mag)
            tt('pool', th, dt, vec(l, V_S5AI, 8), ALU.mult)
            for (dst, offs) in ((sn_, 0.5), (cs_, 0.75)):
                ts('dve', tq, th, 1.0 / (2 * math.pi), ALU.mult, offs, ALU.add)
                tq_i = tq.bitcast(mybir.dt.int32)
                S.op('dve', lambda e, a=ti.bitcast(mybir.dt.int32), b=tq: e.tensor_copy(out=a, in_=b),
                     reads=[tq], writes=[ti])
                S.op('dve', lambda e, a=fr_, b=ti.bitcast(mybir.dt.int32): e.tensor_copy(out=a, in_=b),
                     reads=[ti], writes=[fr_])
                tt('dve', fr_, tq, fr_, ALU.subtract)
                ts('dve', ti, fr_, 0.0, ALU.is_lt)
                tt('dve', fr_, fr_, ti, ALU.add)
                ts('dve', fr_, fr_, 2 * math.pi, ALU.mult, -math.pi, ALU.add)
                ts('dve', fr_, fr_, math.pi, ALU.min, -math.pi, ALU.max)
                act(dst, fr_, AF.Sin)
            abr, abi, den, frr, fii, t1, t2, t3 = W8[8:16]
            tt('pool', abr, mag, cs_, ALU.mult)
            tt('pool', abi, mag, sn_, ALU.mult)
            are, aim = vec(l, V_S5AR, 8), vec(l, V_S5AI, 8)
            tt('pool', t1, are, are, ALU.mult)
            tt('pool', t2, aim, aim, ALU.mult)
            tt('pool', den, t1, t2, ALU.add)
            recip(den, den)
            ts('pool', t3, abr, -1.0, ALU.add)
            tt('pool', t1, t3, are, ALU.mult)
            tt('pool', t2, abi, aim, ALU.mult)
            tt('pool', t1, t1, t2, ALU.add)
            tt('pool', frr, t1, den, ALU.mult)
            tt('pool', t1, abi, are, ALU.mult)
            tt('pool', t2, t3, aim, ALU.mult)
            tt('pool', t1, t1, t2, ALU.subtract)
            tt('pool', fii, t1, den, ALU.mult)
            for st_ in range(8):
                Er, Ei = s5tab(l, 2, st_), s5tab(l, 3, st_)
                cp('pool', Er[:, 0:1], cs_[:, st_:st_ + 1])
                cp('pool', Ei[:, 0:1], sn_[:, st_:st_ + 1])
                n = 1
                tmpa, tmpb = V(o_R2 + 256, SB), V(o_R2 + 256 + SB, SB)
                while n < SB:
                    cr, ci = Er[:, n - 1:n], Ei[:, n - 1:n]
                    ts('dve', tmpa[:, 0:n], Er[:, 0:n], cr, ALU.mult)
                    ts('dve', tmpb[:, 0:n], Ei[:, 0:n], ci, ALU.mult)
                    tt('dve', Er[:, n:2 * n], tmpa[:, 0:n], tmpb[:, 0:n], ALU.subtract)
                    ts('dve', tmpa[:, 0:n], Er[:, 0:n], ci, ALU.mult)
                    ts('dve', tmpb[:, 0:n], Ei[:, 0:n], cr, ALU.mult)
                    tt('dve', Ei[:, n:2 * n], tmpa[:, 0:n], tmpb[:, 0:n], ALU.add)
                    n *= 2
                Epr, Epi = s5tab(l, 0, st_), s5tab(l, 1, st_)
                fr1, fi1 = frr[:, st_:st_ + 1], fii[:, st_:st_ + 1]
                ts('dve', tmpa, Er, fr1, ALU.mult)
                ts('dve', tmpb, Ei, fi1, ALU.mult)
                tt('dve', Epr, tmpa, tmpb, ALU.add)
                ts('dve', tmpa, Er, fi1, ALU.mult)
                ts('dve', tmpb, Ei, fr1, ALU.mult)
                tt('dve', Epi, tmpa, tmpb, ALU.subtract)
                cp('pool', s5tab(l, 4, st_), mag[:, st_:st_ + 1].to_broadcast([128, SB]))

        Xb = [V(o_X + ct * NTT, NTT) for ct in range(8)]
        HYb = [HYR[:, ct * NTT:(ct + 1) * NTT] for ct in range(8)]
        FINb = [V(o_R1 + ct * NTT, NTT) for ct in range(8)]

        def rmsnorm_mod(l, which, NT, segs, final=False):
            ps = psum()
            for ct in range(8):
                sq = V(o_R2 + (ct % 2) * NTT, NTT)
                act(sq[:, 0:NT], Xb[ct][:, 0:NT], AF.Square)
                mm(ps[:, 0:NT], ones, sq[:, 0:NT], start=(ct == 0), stop=(ct == 7))
            rstd = V(o_R2 + 2 * NTT, NTT)
            act(rstd[:, 0:NT], ps[:, 0:NT], AF.Sqrt, bias=EPS, scale=1.0 / D)
            recip(rstd[:, 0:NT], rstd[:, 0:NT])
            for ct in range(8):
                ntmp = V(o_R2 + (3 + ct % 2) * NTT, NTT)
                tt('pool', ntmp[:, 0:NT], Xb[ct][:, 0:NT], rstd[:, 0:NT], ALU.mult)
                for (seq, c0, ln) in segs:
                    if final:
                        ts('dve', FINb[ct][:, c0:c0 + ln], ntmp[:, c0:c0 + ln], vec(0, V_NFIN + ct), ALU.mult)
                    else:
                        act(RR(HYb[ct][:, c0:c0 + ln]), ntmp[:, c0:c0 + ln], AF.Identity,
                            bias=modc(l, 0 if which == 0 else 3, ct, seq), scale=nsc(l, which, ct, seq))

        def process(NT, segs, C, src, dst):
            if stage < 1:
                return
            nseg = len(segs)
            SL = segs[0][2]
            S.dma(V(o_X, 8 * NTT).rearrange("p (c t) -> p c t", c=8)[:, :, 0:NT], src.rearrange("c p t -> p c t"))
            for l in range(L):
                S.dma(V(o_smw, NSW), smallw_d[l])
                ts('pool', V(o_smw + W_CIM, 512), V(o_smw + W_CIM, 512), -1.0, ALU.mult)
                smw = lambda c, n: A[:, o_smw + c:o_smw + c + n]
                rmsnorm_mod(l, 0, NT, segs)
                o_G = o_R1
                o_LX = o_G + 2 * NTT
                o_PR = o_LX + 2 * 520
                o_U5 = o_PR + 14 * 520
                assert o_U5 + 2 * NTT <= o_R1 + 10400
                Gb = [V(o_G + i * NTT, NTT) for i in range(2)]
                LXf = [V(o_LX + i * 520, 520) for i in range(2)]
                PRf = [V(o_PR + i * 520, 520) for i in range(14)]
                U5 = [V(o_U5 + i * NTT, NTT) for i in range(2)]

                def segview(buf, H):
                    return buf[:, 0:nseg * (H + SL)].rearrange("p (s t) -> p s t", s=nseg)

                for si, (seq, c0, ln) in enumerate(segs):
                    for ct in range(2):
                        cp('pool', segview(LXf[ct], 3)[:, si, 0:3], stc(l, O_LH + (ct * 3 + seq) * 3, 3))
                    for ct in range(14):
                        cp('pool', segview(PRf[ct], 2)[:, si, 1:2], stc(l, O_SH + ct * 3 + seq))
                sl_it = slab_iter([(win_d[l, sl], 2048) for sl in range(10)])
                for sl in range(10):
                    o = next(sl_it)
                    pss = [psum() for _ in range(2)]
                    for k in range(8):
                        for m in range(2):
                            mm(pss[m][:, 0:NT], RR(WR[:, o + k * 256 + m * 128:o + k * 256 + m * 128 + 128]),
                               RR(HYb[k][:, 0:NT]), start=(k == 0), stop=(k == 7))
                    for m in range(2):
                        mt = sl * 2 + m
                        src_ps = pss[m][:, 0:NT]
                        if mt < 2:
                            cp('act', Gb[mt][:, 0:NT], src_ps)
                        elif mt < 4:
                            cp('act', segview(LXf[mt - 2], 3)[:, :, 3:3 + SL], src_ps.rearrange("p (s t) -> p s t", s=nseg))
                        elif mt < 18:
                            cp('act' if mt % 2 else 'dve', segview(PRf[mt - 4], 2)[:, :, 2:2 + SL],
                               src_ps.rearrange("p (s t) -> p s t", s=nseg))
                        else:
                            cp('dve', U5[mt - 18][:, 0:NT], src_ps)
                for si, (seq, c0, ln) in enumerate(segs):
                    for ct in range(2):
                        cp('pool', stc(l, O_LH + (ct * 3 + seq) * 3, 3), segview(LXf[ct], 3)[:, si, SL:SL + 3])
                    for ct in range(14):
                        cp('pool', stc(l, O_SH + ct * 3 + seq), segview(PRf[ct], 2)[:, si, SL + 1:SL + 2])

                if stage < 2:
                    continue
                for ct in range(2):
                    xc = V(o_R2, NTT)
                    xv = segview(LXf[ct], 3)
                    xc3 = xc[:, 0:NT].rearrange("p (s t) -> p s t", s=nseg)
                    cw = lambda k: vec(l, V_LCW + ct * 4 + k)
                    ts('dve', xc3, xv[:, :, 0:SL], cw(0), ALU.mult, vec(l, V_LCB + ct), ALU.add)
                    for k in range(1, 4):
                        stt(xc3, xv[:, :, k:k + SL], cw(k), xc3, ALU.mult, ALU.add)
                    ps1, ps2 = psum(), psum()
                    mm(ps1[:, 0:NT], smw(W_WA + ct * 128, 128), xc[:, 0:NT])
                    mm(ps2[:, 0:NT], smw(W_WX + ct * 128, 128), xc[:, 0:NT])
                    rg, ig = V(o_R2 + NTT, NTT), V(o_R2 + 2 * NTT, NTT)
                    act(rg[:, 0:NT], ps1[:, 0:NT], AF.Sigmoid, bias=vec(l, V_LBA + ct))
                    act(ig[:, 0:NT], ps2[:, 0:NT], AF.Sigmoid, bias=vec(l, V_LBX + ct))
                    aa, gn = V(o_R2 + 3 * NTT, NTT), V(o_R2 + 4 * NTT, NTT)
                    act(aa[:, 0:NT], rg[:, 0:NT], AF.Exp, scale=A[:, o_lrusp + l * 2 + ct:o_lrusp + l * 2 + ct + 1])
                    act(gn[:, 0:NT], rg[:, 0:NT], AF.Exp, scale=A[:, o_lrusp2 + l * 2 + ct:o_lrusp2 + l * 2 + ct + 1])
                    ts('pool', gn[:, 0:NT], gn[:, 0:NT], -1.0, ALU.mult, 1.0, ALU.add)
                    ts('pool', gn[:, 0:NT], gn[:, 0:NT], 0.0, ALU.max)
                    act(gn[:, 0:NT], gn[:, 0:NT], AF.Sqrt)
                    tt('pool', ig[:, 0:NT], ig[:, 0:NT], xc[:, 0:NT], ALU.mult)
                    tt('pool', ig[:, 0:NT], ig[:, 0:NT], gn[:, 0:NT], ALU.mult)
                    hh = V(o_R2 + 5 * NTT, NTT)
                    for (seq, c0, ln) in segs:
                        hst = stc(l, O_Lh + ct * 3 + seq)
                        scan(hh[:, c0:c0 + ln], aa[:, c0:c0 + ln], ig[:, c0:c0 + ln], hst)
                        cp('pool', hst, hh[:, c0 + ln - 1:c0 + ln])
                    act(rg[:, 0:NT], Gb[ct][:, 0:NT], AF.Gelu_apprx_tanh)
                    tt('pool', RR(HYb[ct][:, 0:NT]), hh[:, 0:NT], rg[:, 0:NT], ALU.mult)

                if stage < 3:
                    continue
                o_s = o_R2
                nsb_list = []
                for (seq, c0, ln) in segs:
                    for s0 in range(0, ln, SB):
                        nsb_list.append((seq, c0 + s0, min(SB, ln - s0), s0 + SB >= ln, s0 == 0))
                SBL = nsb_list[0][2]
                nsb = len(nsb_list)
                ypss = [PS[6], PS[7]]
                for st_ in range(8):
                    kt, half, par = st_ // 4, (st_ % 4) // 2, st_ % 2
                    pr = slice(64 * half, 64 * half + 64)
                    pbr, pbi = psum(), psum()
                    mm(pbr[:, 0:NT], A[pr, o_smw + W_BRE + (kt * 2 + par) * 128:o_smw + W_BRE + (kt * 2 + par) * 128 + 128],
                       U5[kt][pr, 0:NT])
                    mm(pbi[:, 0:NT], A[pr, o_smw + W_BIM + (kt * 2 + par) * 128:o_smw + W_BIM + (kt * 2 + par) * 128 + 128],
                       U5[kt][pr, 0:NT])
                    base = o_s + (st_ % 2) * 4352
                    cre, cim, t1, t2, gre, gim, hre, him = [V(base + i * NTT, NTT) for i in range(8)]
                    def bt(tab):
                        return tab[:, 0:SBL].unsqueeze(1).to_broadcast([128, nsb, SBL])
                    v3 = lambda b: b[:, 0:NT].rearrange("p (s t) -> p s t", s=nsb)
                    Epr, Epi, Er, Ei, Rh = [s5tab(l, i, st_) for i in range(5)]
                    tt('dve', v3(t1), v3(pbr), bt(Epr), ALU.mult)
                    tt('dve', v3(t2), v3(pbi), bt(Epi), ALU.mult)
                    tt('pool', cre[:, 0:NT], t1[:, 0:NT], t2[:, 0:NT], ALU.subtract)
                    tt('dve', v3(t1), v3(pbi), bt(Epr), ALU.mult)
                    tt('dve', v3(t2), v3(pbr), bt(Epi), ALU.mult)
                    tt('pool', cim[:, 0:NT], t1[:, 0:NT], t2[:, 0:NT], ALU.add)
                    for (seq, c0, ln, last, first_sb) in nsb_list:
                        hr0, hi0 = stc(l, O_S5R + st_ * 3 + seq), stc(l, O_S5I + st_ * 3 + seq)
                        ini_r = hr0 if first_sb else hre[:, c0 - 1:c0]
                        ini_i = hi0 if first_sb else him[:, c0 - 1:c0]
                        cs = slice(c0, c0 + ln)
                        scan(gre[:, cs], Rh[:, 0:ln], cre[:, cs], ini_r)
                        scan(gim[:, cs], Rh[:, 0:ln], cim[:, cs], ini_i)
                        q1, q2, q3, q4 = [V(base + 4096 + i * 64, 64) for i in range(4)]
                        tt('pool', q1[:, 0:ln], gre[:, cs], Er[:, 0:ln], ALU.mult)
                        tt('pool', q2[:, 0:ln], gim[:, cs], Ei[:, 0:ln], ALU.mult)
                        tt('pool', hre[:, cs], q1[:, 0:ln], q2[:, 0:ln], ALU.subtract)
                        tt('dve', q3[:, 0:ln], gim[:, cs], Er[:, 0:ln], ALU.mult)
                        tt('dve', q4[:, 0:ln], gre[:, cs], Ei[:, 0:ln], ALU.mult)
                        tt('dve', him[:, cs], q3[:, 0:ln], q4[:, 0:ln], ALU.add)
                        if last:
                            cp('act', hr0, hre[:, c0 + ln - 1:c0 + ln])
                            cp('act', hi0, him[:, c0 + ln - 1:c0 + ln])
                    j, hf = st_ // 4, (st_ % 4) // 2
                    po = slice(64 * hf, 64 * hf + 64)
                    first = (st_ % 2 == 0)
                    mm(ypss[j][po, 0:NT], smw(W_CRE + st_ * 64, 64), hre[:, 0:NT], start=first, stop=False)
                    mm(ypss[j][po, 0:NT], smw(W_CIM + st_ * 64, 64), him[:, 0:NT], start=False, stop=(not first))
                zb = [V(o_s + i * NTT, NTT) for i in range(2)]
                for j in range(2):
                    stt(zb[j][:, 0:NT], U5[j][:, 0:NT], vec(l, V_S5D + j), ypss[j][:, 0:NT], ALU.mult, ALU.add)
                    act(zb[j][:, 0:NT], zb[j][:, 0:NT], AF.Gelu_apprx_tanh)
                for j in range(2):
                    ps = psum()
                    for k in range(2):
                        mm(ps[:, 0:NT], smw(W_GLU + k * 256 + j * 128, 128), zb[k][:, 0:NT], start=(k == 0), stop=(k == 1))
                    gt = V(o_s + 2 * NTT, NTT)
                    act(gt[:, 0:NT], ps[:, 0:NT], AF.Sigmoid, bias=vec(l, V_GLUB + j))
                    tt('pool', RR(HYb[6 + j][:, 0:NT]), zb[j][:, 0:NT], gt[:, 0:NT], ALU.mult)

                if stage < 4:
                    continue
                S.dma(V(o_smw, NSWB), smallwB_d[l])
                for ct in range(14):
                    pv = segview(PRf[ct], 2)
                    dtmp = V(o_R2 + (ct % 2) * NTT, NTT)
                    d3 = dtmp[:, 0:NT].rearrange("p (s t) -> p s t", s=nseg)
                    tt('pool', d3, pv[:, :, 1:1 + SL], pv[:, :, 2:2 + SL], ALU.subtract)
                    stt(pv[:, :, 2:2 + SL], d3, vec(l, V_MU + ct), pv[:, :, 2:2 + SL], ALU.mult, ALU.add)
                o_c = o_R2 + 2 * NTT

                def xm(ct):
                    if nseg == 1:
                        return PRf[ct][:, 2:2 + NT]
                    return None
                if nseg > 1:
                    for ct in range(14):
                        tmpc = V(o_c, NT)
                        cp('pool', tmpc.rearrange("p (s t) -> p s t", s=nseg), segview(PRf[ct], 2)[:, :, 2:2 + SL])
                        cp('pool', PRf[ct][:, 0:NT], tmpc)
                    xm = lambda ct: PRf[ct][:, 0:NT]
                lo_t = V(o_R2, NTT)
                act(lo_t[0:64, 0:NT], xm(12)[0:64, :], AF.Tanh)
                cp('pool', lo_t[64:128, 0:NT], xm(12)[64:128, :])
                gs_t = V(o_R2 + NTT, NTT)
                act(gs_t[:, 0:NT], xm(13), AF.Sigmoid)
                YR = [HYb[2 + hp] for hp in range(4)]
                for hp in range(4):
                    o_f = o_R2 + 2 * NTT
                    F = [V(o_f + i * NTT, NTT) for i in range(9)]
                    lw, aa, kkn, bb, csb, t1, t2, t3, t4 = F
                    r_, k_, v_ = xm(hp), xm(4 + hp), xm(8 + hp)
                    ps1, ps2, ps3 = psum(), psum(), psum()
                    mm(ps1[:, 0:NT], A[0:64, o_smw + W_LORA + hp * 128:o_smw + W_LORA + hp * 128 + 128], lo_t[0:64, 0:NT])
                    mm(ps2[:, 0:NT], A[64:128, o_smw + W_LORA + hp * 128:o_smw + W_LORA + hp * 128 + 128], lo_t[64:128, 0:NT])
                    mm(ps3[:, 0:NT], smw(W_G2 + hp * 128, 128), gs_t[:, 0:NT])
                    act(lw[:, 0:NT], ps1[:, 0:NT], AF.Sigmoid, bias=vec(l, V_W0 + hp))
                    ts('pool', lw[:, 0:NT], lw[:, 0:NT], -0.6065306597126334, ALU.mult)
                    act(aa[:, 0:NT], ps2[:, 0:NT], AF.Sigmoid, bias=vec(l, V_A0 + hp))
                    gg = t4
                    cp('act', gg[:, 0:NT], ps3[:, 0:NT])
                    ts('pool', kkn[:, 0:NT], k_, vec(l, V_KK + hp), ALU.mult)
                    tt('dve', t1[:, 0:NT], kkn[:, 0:NT], kkn[:, 0:NT], ALU.mult)
                    ps = psum()
                    mm(ps[:, 0:NT], bones, t1[:, 0:NT])
                    ts('dve', t1[:, 0:NT], ps[:, 0:NT], TINY, ALU.max)
                    act(t1[:, 0:NT], t1[:, 0:NT], AF.Sqrt)
                    recip(t1[:, 0:NT], t1[:, 0:NT])
                    tt('pool', kkn[:, 0:NT], kkn[:, 0:NT], t1[:, 0:NT], ALU.mult)
                    ts('pool', t1[:, 0:NT], aa[:, 0:NT], vec(l, V_KA + hp), ALU.mult,
                       A[:, o_omka + l * 4 + hp:o_omka + l * 4 + hp + 1], ALU.add)
                    tt('pool', k_, k_, t1[:, 0:NT], ALU.mult)
                    tt('dve', bb[:, 0:NT], kkn[:, 0:NT], aa[:, 0:NT], ALU.mult)
                    stt(t1[:, 0:NT], r_, vec(l, V_RK + hp), k_, ALU.mult, ALU.mult)
                    ps = psum()
                    mm(ps[:, 0:NT], bones, t1[:, 0:NT])
                    bonus = t3
                    tt('dve', bonus[:, 0:NT], ps[:, 0:NT], v_, ALU.mult)
                    rmask = A[:, o_const + C_RESET:o_const + C_RESET + NTT]
                    if C == CH:
                        scan(csb[:, 0:NT], rmask[:, 0:NT], lw[:, 0:NT], 0.0)
                    else:
                        for (seq, c0, ln) in segs:
                            scan(csb[:, c0:c0 + ln], rmask[:, 1:1 + ln], lw[:, c0:c0 + ln], 0.0)
                    nch = NT // C
                    c3 = lambda b: b[:, 0:NT].rearrange("p (c t) -> p c t", c=nch)
                    csC = c3(csb)[:, :, C - 1:C].to_broadcast([128, nch, C])
                    gex = t1
                    tt('pool', gex[:, 0:NT], csb[:, 0:NT], lw[:, 0:NT], ALU.subtract)
                    act(gex[:, 0:NT], gex[:, 0:NT], AF.Exp)
                    at_f = t1
                    tt('dve', at_f[:, 0:NT], kkn[:, 0:NT], gex[:, 0:NT], ALU.mult)
                    ig_ = kkn
                    act(ig_[:, 0:NT], csb[:, 0:NT], AF.Exp, scale=-1.0)
                    gC = lw
                    tt('pool', c3(gC), csC, c3(csb), ALU.subtract)
                    act(gC[:, 0:NT], gC[:, 0:NT], AF.Exp)
                    bh_f = aa
                    tt('dve', bh_f[:, 0:NT], bb[:, 0:NT], gC[:, 0:NT], ALU.mult)
                    tt('pool', bb[:, 0:NT], bb[:, 0:NT], ig_[:, 0:NT], ALU.mult)
                    kh_f = gC
                    tt('dve', kh_f[:, 0:NT], k_, gC[:, 0:NT], ALU.mult)
                    kt_f = ig_
                    tt('pool', kt_f[:, 0:NT], k_, ig_[:, 0:NT], ALU.mult)
                    act(csb[:, 0:NT], csb[:, 0:NT], AF.Exp)
                    gamC = csb
                    rt_f = t2
                    tt('dve', rt_f[:, 0:NT], r_, csb[:, 0:NT], ALU.mult)
                    bt_f = bb
                    o_m = o_R2 + 11 * NTT
                    chunks = []
                    for (seq, c0, ln) in segs:
                        for s0 in range(0, ln, C):
                            chunks.append((seq, c0 + s0))
                    for (seq, c0) in chunks:
                        if sub < 1:
                            break
                        cc = slice(c0, c0 + C)
                        pst = psum()
                        for i, srcf in enumerate((at_f, None, bh_f, kh_f)):
                            sap = v_[:, c0:c0 + C] if srcf is None else srcf[:, cc]
                            tr(pst[0:C, i * 128:i * 128 + 128], sap, ident)
                        TM = V(o_m, 512)
                        cp('act', TM[0:C, :], pst[0:C, :])
                        At_t, V_t, Bh_t, Kh_t = [TM[0:C, i * 128:i * 128 + 128] for i in range(4)]
                        U = []
                        for hl in range(2):
                            pr = slice(64 * hl, 64 * hl + 64)
                            ob = o_m + 512 + hl * 1280
                            NTm, MTm, Nm, PBm, QKm, Pa, PaT, Pb, PbT, Tm = [A[0:C, ob + i * 128:ob + i * 128 + C] for i in range(10)]
                            U.append(dict(pr=pr, S1=A[0:C, ob:ob + 128], NT=NTm, MT=MTm, N=Nm, PB=PBm, QK=QKm, Pa=Pa, PaT=PaT, Pb=Pb, PbT=PbT, T=Tm))
                        msu = A[0:C, o_const + C_MSU:o_const + C_MSU + C]
                        msl = A[0:C, o_const + C_MSL:o_const + C_MSL + C]
                        miu = A[0:C, o_const + C_MIU:o_const + C_MIU + C]
                        nmsu = A[0:C, o_const + C_NMSU:o_const + C_NMSU + C]
                        nmsl = A[0:C, o_const + C_NMSL:o_const + C_NMSL + C]
                        for u in U:
                            if not (bits & 2):
                                break
                            pr = u['pr']
                            pA, pB = psum(), psum()
                            mm(pA[0:C, 0:C], at_f[pr, cc], bt_f[pr, cc])
                            mm(pA[0:C, 128:128 + C], at_f[pr, cc], kt_f[pr, cc])
                            mm(pB[0:C, 0:C], bt_f[pr, cc], at_f[pr, cc])
                            mm(pB[0:C, 128:128 + C], bt_f[pr, cc], rt_f[pr, cc])
                            mm(pB[0:C, 256:256 + C], kt_f[pr, cc], rt_f[pr, cc])
                            if not (bits & 4):
                                continue
                            cp('act', u['PaT'], pA[0:C, 0:C]); tt('pool', u['PaT'], u['PaT'], nmsl, ALU.mult)
                            cp('act', u['MT'], pA[0:C, 128:128 + C]); tt('pool', u['MT'], u['MT'], msl, ALU.mult)
                            if not (bits & 16):
                                continue
                            cp('act', u['Pa'], pB[0:C, 0:C]); tt('pool', u['Pa'], u['Pa'], nmsu, ALU.mult)
                            cp('act', u['PB'], pB[0:C, 128:128 + C]); tt('pool', u['PB'], u['PB'], miu, ALU.mult)
                            cp('act', u['QK'], pB[0:C, 256:256 + C]); tt('pool', u['QK'], u['QK'], miu, ALU.mult)
                            if bits & 8:
                                tt('pool', u['T'], u['Pa'], ident[0:C, 0:C], ALU.add)
                        if sub < 2:
                            continue
                        nlev = int(round(math.log2(C))) - 1
                        for lev in range(1, nlev + 1):
                            for u in U:
                                Pp, PpT = (u['Pa'], u['PaT']) if lev % 2 == 1 else (u['Pb'], u['PbT'])
                                Pn, PnT = (u['Pb'], u['PbT']) if lev % 2 == 1 else (u['Pa'], u['PaT'])
                                pq = psum()
                                mm(pq[0:C, 0:C], Pp, PpT)
                                if lev < nlev:
                                    mm(pq[0:C, 128:128 + C], PpT, Pp)
                                cp('act', PnT, pq[0:C, 0:C])
                                if lev < nlev:
                                    cp('act', Pn, pq[0:C, 128:128 + C])
                                mm(pq[0:C, 256:256 + C], PnT, u['T'])
                                tt('dve', u['T'], pq[0:C, 256:256 + C], u['T'], ALU.add)
                        if sub < 3:
                            continue
                        for u in U:
                            pr = u['pr']
                            hc = slice(pr.start, pr.stop)
                            pq = psum()
                            mm(pq[0:C, 0:64], u['T'], At_t[:, hc])
                            mm(pq[0:C, 128:128 + C], u['MT'], u['T'])
                            Ab = u['S1'][:, 0:64]
                            MTT = u['N']
                            cp('act', Ab, pq[0:C, 0:64])
                            cp('act', MTT, pq[0:C, 128:128 + C])
                            mm(pq[0:C, 256:320], MTT, V_t[:, hc])
                            nUt = u['S1'][:, 64:128]
                            ts('dve', nUt, pq[0:C, 256:320], -1.0, ALU.mult)
                            pg = psum()
                            mm(pg[pr, 0:64], Ab, Bh_t[:, hc])
                            mm(pg[pr, 128:128 + C], Ab, u['PB'])
                            Gm = A[pr, o_m + 3072 + 0:o_m + 3072 + 64]
                            Rh = A[pr, o_m + 3072 + 64:o_m + 3072 + 64 + C]
                            stt(Gm, ident[pr, hc], gamC[pr, c0 + C - 1:c0 + C], pg[pr, 0:64], ALU.mult, ALU.subtract)
                            tt('dve', Rh, rt_f[pr, cc], pg[pr, 128:128 + C], ALU.subtract)
                            STm = A[pr, o_sT + (l * 3 + seq) * 256 + hp * 64:o_sT + (l * 3 + seq) * 256 + hp * 64 + 64]
                            ph = psum()
                            mm(ph[pr, 128:128 + C], nUt, u['PB'], start=True, stop=False)
                            mm(ph[pr, 128:128 + C], V_t[:, hc], u['QK'], start=False, stop=False)
                            mm(ph[pr, 128:128 + C], STm, Rh, start=False, stop=True)
                            cp('act', RR(YR[hp][pr, cc]), ph[pr, 128:128 + C])
                            ph2 = psum()
                            mm(ph2[pr, 0:64], Bh_t[:, hc], nUt, start=True, stop=False)
                            mm(ph2[pr, 0:64], Kh_t[:, hc], V_t[:, hc], start=False, stop=False)
                            mm(ph2[pr, 0:64], Gm, STm, start=False, stop=True)
                            cp('dve', STm, ph2[pr, 0:64])
                    y = YR[hp]
                    ps = psum()
                    mm(ps[:, 0:NT], bones, y[:, 0:NT])
                    yc = t1
                    stt(yc[:, 0:NT], ps[:, 0:NT], -1.0 / 64, y[:, 0:NT], ALU.mult, ALU.add)
                    tt('dve', t2[:, 0:NT], yc[:, 0:NT], yc[:, 0:NT], ALU.mult)
                    ps = psum()
                    mm(ps[:, 0:NT], bones, t2[:, 0:NT])
                    act(t2[:, 0:NT], ps[:, 0:NT], AF.Sqrt, bias=GNEPS, scale=1.0 / 64)
                    recip(t2[:, 0:NT], t2[:, 0:NT])
                    tt('pool', yc[:, 0:NT], yc[:, 0:NT], t2[:, 0:NT], ALU.mult)
                    ts('pool', yc[:, 0:NT], yc[:, 0:NT], vec(l, V_LNW + hp), ALU.mult, vec(l, V_LNB + hp), ALU.add)
                    tt('dve', yc[:, 0:NT], yc[:, 0:NT], bonus[:, 0:NT], ALU.add)
                    tt('pool', RR(y[:, 0:NT]), yc[:, 0:NT], gg[:, 0:NT], ALU.mult)

                if stage < 5:
                    continue
                sl_it = slab_iter([(wout_d[l, sl], 2048) for sl in range(4)])
                for sl in range(4):
                    o = next(sl_it)
                    pss = [psum() for _ in range(2)]
                    for k in range(8):
                        for m in range(2):
                            mm(pss[m][:, 0:NT], RR(WR[:, o + k * 256 + m * 128:o + k * 256 + m * 128 + 128]),
                               RR(HYb[k][:, 0:NT]), start=(k == 0), stop=(k == 7))
                    for m in range(2):
                        ct = sl * 2 + m
                        for (seq, c0, ln) in segs:
                            stt(Xb[ct][:, c0:c0 + ln], pss[m][:, c0:c0 + ln], modc(l, 2, ct, seq), Xb[ct][:, c0:c0 + ln],
                                ALU.mult, ALU.add)
                if stage < 6:
                    continue
                rmsnorm_mod(l, 1, NT, segs)
                ACTB = [ACTR[:, i * NTT:(i + 1) * NTT] for i in range(11)]
                ffn_srcs = []
                for half in range(2):
                    ffn_srcs += [(wup_d[l, half * 11 + j], 2048) for j in range(11)]
                    ffn_srcs += [(wdn_d[l, half, mt], 1408) for mt in range(8)]
                ffn_it = slab_iter(ffn_srcs)
                for half in range(2):
                    for j in range(11):
                        mg = half * 11 + j
                        o = next(ffn_it)
                        pss = [psum() for _ in range(2)]
                        for k in range(8):
                            for m in range(2):
                                mm(pss[m][:, 0:NT], RR(WR[:, o + k * 256 + m * 128:o + k * 256 + m * 128 + 128]),
                                   RR(HYb[k][:, 0:NT]), start=(k == 0), stop=(k == 7))
                        cv = []
                        for m in range(2):
                            ctp = mg * 2 + m
                            bi = (j % 2) * 2 + m
                            ub = V(o_R2 + bi * 520, 520)
                            uv = segview(ub, 2)
                            for si, (seq, c0, ln) in enumerate(segs):
                                cp('pool', uv[:, si, 0:2], stc(l, O_FH + (ctp * 3 + seq) * 2, 2))
                            cp('act', uv[:, :, 2:2 + SL], pss[m][:, 0:NT].rearrange("p (s t) -> p s t", s=nseg))
                            for si, (seq, c0, ln) in enumerate(segs):
                                cp('pool', stc(l, O_FH + (ctp * 3 + seq) * 2, 2), uv[:, si, SL:SL + 2])
                            co = V(o_R2 + 4 * 520 + bi * NTT, NTT)
                            co3 = co[:, 0:NT].rearrange("p (s t) -> p s t", s=nseg)
                            cv.append((co, co3, uv, ctp))
                        for (co, co3, uv, ctp) in cv:
                            ts('dve', co3, uv[:, :, 0:SL], vec(l, V_FCW + ctp * 3), ALU.mult, vec(l, V_FCB + ctp), ALU.add)
                        for kk_ in (1, 2):
                            for (co, co3, uv, ctp) in cv:
                                stt(co3, uv[:, :, kk_:kk_ + SL], vec(l, V_FCW + ctp * 3 + kk_), co3, ALU.mult, ALU.add)
                        val, gate = cv[0][0], cv[1][0]
                        act(gate[:, 0:NT], gate[:, 0:NT], AF.Silu)
                        tt('pool', RR(ACTB[j][:, 0:NT]), val[:, 0:NT], gate[:, 0:NT], ALU.mult)
                    for mt in range(8):
                        o = next(ffn_it)
                        ps = psum()
                        for k in range(11):
                            mm(ps[:, 0:NT], RR(WR[:, o + k * 128:o + k * 128 + 128]), RR(ACTB[k][:, 0:NT]),
                               start=(k == 0), stop=(k == 10))
                        for (seq, c0, ln) in segs:
                            stt(Xb[mt][:, c0:c0 + ln], ps[:, c0:c0 + ln], modc(l, 5, mt, seq), Xb[mt][:, c0:c0 + ln],
                                ALU.mult, ALU.add)
            rmsnorm_mod(0, 0, NT, segs, final=True)
            S.dma(dst.rearrange("c p t -> p c t"), V(o_R1, 8 * NTT).rearrange("p (c t) -> p c t", c=8)[:, :, 0:NT])

        import os as _os
        _kt = _os.environ.get('KTILES', 'ps')
        for ti in range(ntp if 'p' in _kt else 0):
            process(NTT, [(0, 0, NTT)], CH, xp[:, :, ti * NTT:(ti + 1) * NTT], y_d[:, :, ti * NTT:(ti + 1) * NTT])
        if 's' in _kt:
            process(32, [(1, 0, 16), (2, 16, 16)], 16, xs, ys_d)
        S.dma(st_out, V(o_st, L * NSTO))
        S.dma(sT_out, V(o_sT, L * 768))
        S.finish()
        S.emit(es)
    return nc


def _col(v):
    v = np.asarray(v, np.float32).reshape(-1, 128)
    return np.ascontiguousarray(v.T)


def _ffn_base(ctp):
    m, i = divmod(ctp, 2)
    return 128 * m if i == 0 else DFF + 128 * m


def _shared(inp):
    f = lambda k: np.asarray(inp[k], np.float32)
    vecs = np.zeros((L, 128, NV), np.float32)
    smallw = np.zeros((L, 128, NSW), np.float32)
    smallwB = np.zeros((L, 128, NSWB), np.float32)
    for l in range(L):
        v = vecs[l]
        v[:, V_NM:V_NM + 8] = _col(f('norm_mix')[l])
        v[:, V_NF:V_NF + 8] = _col(f('norm_ffn')[l])
        v[:, V_BADA:V_BADA + 48] = _col(f('b_ada')[l])
        for ct in range(2):
            for k in range(4):
                v[:, V_LCW + ct * 4 + k] = f('lru_conv_w')[l, k, ct * 128:(ct + 1) * 128]
        v[:, V_LCB:V_LCB + 2] = _col(f('lru_conv_b')[l])
        v[:, V_LBA:V_LBA + 2] = _col(f('lru_ba')[l])
        v[:, V_LBX:V_LBX + 2] = _col(f('lru_bx')[l])
        v[:, V_LLAM:V_LLAM + 2] = _col(f('lru_lambda')[l])
        v[:, V_MU:V_MU + 14] = _col(f('rwkv_mu')[l])
        for nm, o in (('rwkv_w0', V_W0), ('rwkv_a0', V_A0), ('rwkv_k_k', V_KK), ('rwkv_k_a', V_KA),
                      ('rwkv_r_k', V_RK), ('rwkv_ln_w', V_LNW), ('rwkv_ln_b', V_LNB)):
            v[:, o:o + 4] = _col(f(nm)[l].reshape(-1))
        v[:, V_S5AR:V_S5AR + 8] = _col(f('s5_a_re')[l].reshape(-1))
        v[:, V_S5AI:V_S5AI + 8] = _col(f('s5_a_im')[l].reshape(-1))
        v[:, V_S5DT:V_S5DT + 8] = _col(np.repeat(f('s5_log_dt')[l], 64))
        v[:, V_S5D:V_S5D + 2] = _col(f('s5_d')[l])
        v[:, V_GLUB:V_GLUB + 2] = _col(f('s5_glu_b')[l])
        for ctp in range(44):
            b = _ffn_base(ctp)
            for k in range(3):
                v[:, V_FCW + ctp * 3 + k] = f('ffn_conv_w')[l, k, b:b + 128]
            v[:, V_FCB + ctp] = f('ffn_conv_b')[l, b:b + 128]
        v[:, V_NFIN:V_NFIN + 8] = _col(f('norm_final'))
        w = smallw[l]
        for nm, o in (('lru_wa', W_WA), ('lru_wx', W_WX)):
            for ct in range(2):
                for hh in range(2):
                    w[hh * 64:(hh + 1) * 64, o + ct * 128 + hh * 64:o + ct * 128 + hh * 64 + 64] = f(nm)[l, 2 * ct + hh]
        smallwB[l][0:64, W_LORA:W_LORA + 512] = f('rwkv_w2')[l]
        smallwB[l][64:128, W_LORA:W_LORA + 512] = f('rwkv_a2')[l]
        smallwB[l][:, W_G2:W_G2 + 512] = f('rwkv_g2')[l]
        for nm, o in (('s5_b_re', W_BRE), ('s5_b_im', W_BIM)):
            bsrc = f(nm)[l]
            for st_ in range(8):
                kt, half, par = st_ // 4, (st_ % 4) // 2, st_ % 2
                for gg in (2 * st_, 2 * st_ + 1):
                    p0 = gg * 16 - kt * 128
                    s0 = (gg - 2 * st_) * 64
                    c0 = o + (kt * 2 + par) * 128 + s0
                    w[p0:p0 + 16, c0:c0 + 64] = bsrc[gg].T
        for nm, o in (('s5_c_re', W_CRE), ('s5_c_im', W_CIM)):
            csrc = f(nm)[l]
            for st_ in range(8):
                for gg in (2 * st_, 2 * st_ + 1):
                    p0 = (gg - 2 * st_) * 64
                    c0 = o + st_ * 64 + (gg - 4 * (st_ // 2)) * 16
                    w[p0:p0 + 64, c0:c0 + 16] = csrc[gg].T
        w[:, W_GLU:W_GLU + 512] = f('s5_glu_w')[l].reshape(2, 128, 256).transpose(1, 0, 2).reshape(128, 512)
    consts = np.zeros((128, NCONST), np.float32)
    ii, jj = np.meshgrid(np.arange(128), np.arange(128), indexing='ij')
    consts[:, C_ID:C_ID + 128] = np.eye(128)
    consts[:, C_BONES:C_BONES + 128] = (ii // 64 == jj // 64)
    consts[:, C_ONES:C_ONES + 128] = 1.0
    consts[:, C_MSU:C_MSU + 128] = (ii < jj)
    consts[:, C_MSL:C_MSL + 128] = (jj < ii)
    consts[:, C_MIU:C_MIU + 128] = (ii <= jj)
    consts[:, C_RESET:C_RESET + 512] = (np.arange(512) % 128 != 0)[None, :]
    consts[:, C_NMSU:C_NMSU + 128] = -1.0 * (ii < jj)
    consts[:, C_NMSL:C_NMSL + 128] = -1.0 * (jj < ii)

    def slab(wm, ncol):
        n = ncol // 256
        return np.ascontiguousarray(wm.reshape(8, 128, n, 256).transpose(2, 1, 0, 3).reshape(n, 128, 2048))
    colidx = np.concatenate([np.arange(_ffn_base(c), _ffn_base(c) + 128) for c in range(44)])
    sh = dict(vecs=vecs, smallw=smallw, smallwB=smallwB, consts=consts)
    sh['win'] = np.stack([slab(f('w_in')[l], 2560) for l in range(L)])
    sh['wout'] = np.stack([slab(f('w_out')[l], 1024) for l in range(L)])
    sh['wup'] = np.stack([slab(f('ffn_up')[l][:, colidx], 5632) for l in range(L)])
    sh['wdn'] = np.stack([np.ascontiguousarray(f('ffn_down')[l].reshape(2, 11, 128, 8, 128).transpose(0, 3, 2, 1, 4).reshape(2, 8, 128, 1408))
                          for l in range(L)])
    sh['wada'] = np.stack([slab(f('w_ada')[l], 6144) for l in range(L)])
    return sh


def _core_inputs(inp, c, TP):
    f = lambda k: np.asarray(inp[k], np.float32)
    b, s0, s1 = c // 2, 2 * c, 2 * c + 1
    m = {}
    m['xp'] = np.ascontiguousarray(f('x_prompt')[b].T).reshape(8, 128, TP)
    m['xs'] = np.ascontiguousarray(f('x_sample')[[s0, s1]].reshape(32, D).T).reshape(8, 128, 32)
    cl = np.stack([f('c_prompt')[b], f('c_sample')[s0], f('c_sample')[s1], np.zeros(D, np.float32)])
    m['cT'] = np.ascontiguousarray(cl.reshape(4, 8, 128).transpose(2, 1, 0).reshape(128, 32))
    st = np.zeros((128, L * NSTO), np.float32)
    sT = np.zeros((128, L * 768), np.float32)
    for l in range(L):
        o = l * NSTO
        for seq, s in ((1, s0), (2, s1)):
            for ct in range(2):
                for j in range(3):
                    st[:, o + O_LH + (ct * 3 + seq) * 3 + j] = f('state_lru_conv')[l, s, j, ct * 128:(ct + 1) * 128]
                st[:, o + O_Lh + ct * 3 + seq] = f('state_lru_h')[l, s, ct * 128:(ct + 1) * 128]
            for ct in range(14):
                st[:, o + O_SH + ct * 3 + seq] = f('state_rwkv_shift')[l, s, ct * 128:(ct + 1) * 128]
            for st_ in range(8):
                st[:, o + O_S5R + st_ * 3 + seq] = f('state_s5_re')[l, s].reshape(-1)[st_ * 128:(st_ + 1) * 128]
                st[:, o + O_S5I + st_ * 3 + seq] = f('state_s5_im')[l, s].reshape(-1)[st_ * 128:(st_ + 1) * 128]
            for ctp in range(44):
                bb = _ffn_base(ctp)
                for j in range(2):
                    st[:, o + O_FH + (ctp * 3 + seq) * 2 + j] = f('state_ffn_conv')[l, s, j, bb:bb + 128]
            Sm = f('state_rwkv_S')[l, s].reshape(4, 2, 64, 64).transpose(1, 3, 0, 2).reshape(128, 256)
            sT[:, (l * 3 + seq) * 256:(l * 3 + seq + 1) * 256] = Sm
    m['st_in'] = st
    m['sT_in'] = sT
    return m


_NC_CACHE = {}


def kernel(**inp):
    TP = int(np.asarray(inp['x_prompt']).shape[1])
    ntp = TP // NTT
    import os
    if ntp not in _NC_CACHE:
        _NC_CACHE[ntp] = build(ntp, stage=int(os.environ.get('KSTAGE', '99')), sub=int(os.environ.get('KSUB', '99')), bits=int(os.environ.get('KBITS', '255')))
    nc = _NC_CACHE[ntp]
    sh = _shared(inp)
    in_maps = []
    for c in range(8):
        m = dict(sh)
        m.update(_core_inputs(inp, c, TP))
        in_maps.append(m)
    res = run_bass_kernel_spmd(nc, in_maps, core_ids=list(range(8)))
    R = res.results
    B, SB_ = 4, 16
    y_prompt = np.zeros((B, TP, D), np.float32)
    y_sample = np.zeros((SB_, 16, D), np.float32)

    def mk(nb):
        return [np.zeros((L, nb, 3, 256), np.float32), np.zeros((L, nb, 256), np.float32),
                np.zeros((L, nb, 1792), np.float32), np.zeros((L, nb, 8, 64, 64), np.float32),
                np.zeros((L, nb, 16, 64), np.float32), np.zeros((L, nb, 16, 64), np.float32),
                np.zeros((L, nb, 2, 2 * DFF), np.float32)]
    P, Sg = mk(B), mk(SB_)

    def unpack(dst, bi, st, sT, seq):
        for l in range(L):
            o = l * NSTO
            for ct in range(2):
                for j in range(3):
                    dst[0][l, bi, j, ct * 128:(ct + 1) * 128] = st[:, o + O_LH + (ct * 3 + seq) * 3 + j]
                dst[1][l, bi, ct * 128:(ct + 1) * 128] = st[:, o + O_Lh + ct * 3 + seq]
            for ct in range(14):
                dst[2][l, bi, ct * 128:(ct + 1) * 128] = st[:, o + O_SH + ct * 3 + seq]
            re = np.zeros(1024, np.float32)
            im = np.zeros(1024, np.float32)
            for st_ in range(8):
                re[st_ * 128:(st_ + 1) * 128] = st[:, o + O_S5R + st_ * 3 + seq]
                im[st_ * 128:(st_ + 1) * 128] = st[:, o + O_S5I + st_ * 3 + seq]
            dst[4][l, bi] = re.reshape(16, 64)
            dst[5][l, bi] = im.reshape(16, 64)
            for ctp in range(44):
                bb = _ffn_base(ctp)
                for j in range(2):
                    dst[6][l, bi, j, bb:bb + 128] = st[:, o + O_FH + (ctp * 3 + seq) * 2 + j]
            Sm = sT[:, (l * 3 + seq) * 256:(l * 3 + seq + 1) * 256].reshape(2, 64, 4, 64)
            dst[3][l, bi] = Sm.transpose(2, 0, 3, 1).reshape(8, 64, 64)

    for c in range(8):
        r = R[c]
        b, s0, s1 = c // 2, 2 * c, 2 * c + 1
        ys = np.asarray(r['ys']).reshape(D, 32).T
        y_sample[s0] = ys[0:16]
        y_sample[s1] = ys[16:32]
        st, sT = np.asarray(r['st_out']), np.asarray(r['sT_out'])
        unpack(Sg, s0, st, sT, 1)
        unpack(Sg, s1, st, sT, 2)
        if c % 2 == 0:
            y_prompt[b] = np.asarray(r['y']).reshape(D, TP).T
            unpack(P, b, st, sT, 0)
    return (y_prompt, y_sample, *P, *Sg)
```

```python
import bisect
import contextlib
import math
import numpy as np
import concourse.bass as bass
import concourse.mybir as mybir
from concourse.bass_utils import run_bass_kernel_spmd

F32 = mybir.dt.float32
F32R = mybir.dt.float32r
USE_R = True


def RR(ap):
    return ap.bitcast(F32R) if USE_R else ap
ALU = mybir.AluOpType
AF = mybir.ActivationFunctionType

ENG_ATTR = {'pe': 'tensor', 'act': 'scalar', 'dve': 'vector', 'pool': 'gpsimd', 'sp': 'sync'}
ERA = 30000
NSLOT = 8


class _Rec:
    __slots__ = ('w', 'r')

    def __init__(self, w=None, r=None):
        self.w = w
        self.r = dict(r) if r else {}


class _IMap:
    def __init__(self):
        self.b = [0, 1 << 60]
        self.rec = [_Rec()]

    def _split(self, x):
        i = bisect.bisect_right(self.b, x) - 1
        if self.b[i] != x:
            self.b.insert(i + 1, x)
            old = self.rec[i]
            self.rec.insert(i + 1, _Rec(old.w, old.r))

    def rng(self, lo, hi):
        self._split(lo)
        self._split(hi)
        i0 = bisect.bisect_left(self.b, lo)
        i1 = bisect.bisect_left(self.b, hi)
        return self.rec[i0:i1]


def _is_dram(ap):
    try:
        return 'DRam' in type(ap.tensor).__name__
    except Exception:
        return False


def ap_key(ap):
    pat = ap.ap
    pstride = pat[0][0]
    off = ap.offset
    col = off % pstride if pstride > 0 else off
    ext = 0
    for st, cnt in pat[1:]:
        ext += abs(st) * (cnt - 1)
    if ap.tensor.name.startswith('P'):
        return ap.tensor.name, 0, 512
    return ap.tensor.name, col, col + ext + 1


class Sched:
    def __init__(self, nc):
        self.nc = nc
        self.prog = {e: [] for e in ENG_ATTR}
        self.cnt = {e: 0 for e in ENG_ATTR}
        self.seen = {e: {} for e in ENG_ATTR}
        self.maps = {}
        self.semkeys = []
        self.semset = set()
        self.dma_n = {e: 0 for e in ENG_ATTR}
        self.slot_val = {}

    def _sem(self, key):
        if key not in self.semset:
            self.semset.add(key)
            self.semkeys.append(key)
        return key

    def _wait(self, eng, tok):
        key, val = tok
        if self.seen[eng].get(key, 0) >= val:
            return
        self.seen[eng][key] = val
        self.prog[eng].append(('wait', key, val))

    def _deps(self, eng, reads, writes, tok, pseudo):
        for ap in reads:
            name, lo, hi = ap_key(ap)
            m = self.maps.setdefault(name, _IMap())
            for rec in m.rng(lo, hi):
                if rec.w is not None and not (rec.w[2] == pseudo and pseudo == 'pe'):
                    self._wait(eng, rec.w[:2])
        for ap in writes:
            name, lo, hi = ap_key(ap)
            m = self.maps.setdefault(name, _IMap())
            for rec in m.rng(lo, hi):
                if rec.w is not None and rec.w[2] != pseudo:
                    self._wait(eng, rec.w[:2])
                for (k, e2), v in rec.r.items():
                    if e2 != pseudo:
                        self._wait(eng, (k, v))
        for ap in reads:
            name, lo, hi = ap_key(ap)
            for rec in self.maps[name].rng(lo, hi):
                rec.r[(tok[0], pseudo)] = tok[1]
        for ap in writes:
            name, lo, hi = ap_key(ap)
            for rec in self.maps[name].rng(lo, hi):
                rec.w = (tok[0], tok[1], pseudo)
                rec.r = {}

    def op(self, eng, fn, reads=(), writes=()):
        reads = [r for r in reads if r is not None and hasattr(r, 'tensor') and not _is_dram(r)]
        writes = [w for w in writes if not _is_dram(w)]
        n = self.cnt[eng] + 1
        self.cnt[eng] = n
        era, v = divmod(n - 1, ERA)
        key = self._sem((eng, era))
        tok = (key, v + 1)
        self._deps(eng, reads, writes, tok, eng)
        self.prog[eng].append(('op', fn, key))
        return tok

    def dma(self, out, in_, eng='sp'):
        j = self.dma_n[eng]
        self.dma_n[eng] = j + 1
        slot = j % NSLOT
        key = self._sem(('dma', eng, slot))
        prev = self.slot_val.get(key, 0)
        if prev > 0:
            self._wait(eng, (key, prev))
        tok = (key, prev + 16)
        self.slot_val[key] = prev + 16
        reads = [] if _is_dram(in_) else [in_]
        writes = [] if _is_dram(out) else [out]
        self._deps(eng, reads, writes, tok, 'dma_%s_%d_%d' % (eng, slot, j))
        self.prog[eng].append(('dma', out, in_, key))
        return tok

    def finish(self, eng='sp'):
        for key, val in self.slot_val.items():
            self._wait(eng, (key, val))

    def emit(self, es):
        nc = self.nc
        sems = {}
        for key in self.semkeys:
            sems[key] = es.enter_context(nc.semaphore("s_" + "_".join(str(x) for x in key)))
        block = es.enter_context(nc.Block())

        def run(eng_name):
            def body(e):
                for item in self.prog[eng_name]:
                    if item[0] == 'wait':
                        e.wait_ge(sems[item[1]], item[2])
                    elif item[0] == 'op':
                        item[1](e).then_inc(sems[item[2]], 1)
                    else:
                        e.dma_start(out=item[1], in_=item[2]).then_inc(sems[item[3]], 16)
            return body

        block.tensor(run('pe'))
        block.scalar(run('act'))
        block.vector(run('dve'))
        block.gpsimd(run('pool'))
        block.sync(run('sp'))


D = 1024
L = 2
NTT = 512
CH = 128
SB = 64
DFF = 2816
O_LH, O_Lh, O_SH, O_S5R, O_S5I, O_FH, NSTO = 0, 18, 24, 66, 90, 114, 378
V_NM, V_NF, V_BADA, V_LCW, V_LCB, V_LBA, V_LBX, V_LLAM = 0, 8, 16, 64, 72, 74, 76, 78
V_MU, V_W0, V_A0, V_KK, V_KA, V_RK, V_LNW, V_LNB = 80, 94, 98, 102, 106, 110, 114, 118
V_S5AR, V_S5AI, V_S5DT, V_S5D, V_GLUB, V_FCW, V_FCB, V_NFIN, NV = 122, 130, 138, 146, 148, 150, 282, 326, 334
W_WA, W_WX, W_GLU, W_BRE, W_BIM, W_CRE, W_CIM, NSW = 0, 256, 512, 1024, 1536, 2048, 2560, 3072
W_LORA, W_G2, NSWB = 0, 512, 1024
C_ID, C_BONES, C_ONES, C_MSU, C_MSL, C_MIU, C_RESET, C_NMSU, C_NMSL, NCONST = 0, 128, 256, 384, 512, 640, 768, 1280, 1408, 1536


def build(ntp, dbg=False, stage=99, sub=99, bits=255):
    TP = ntp * NTT
    nc = bass.Bass("TRN2", target_bir_lowering=False)

    def din(name, shape):
        return nc.dram_tensor(name, list(shape), F32, kind="ExternalInput").ap()

    def dout(name, shape):
        return nc.dram_tensor(name, list(shape), F32, kind="ExternalOutput").ap()

    xp = din("xp", [8, 128, TP])
    xs = din("xs", [8, 128, 32])
    cT = din("cT", [128, 32])
    vecs_d = din("vecs", [L, 128, NV])
    smallw_d = din("smallw", [L, 128, NSW])
    smallwB_d = din("smallwB", [L, 128, NSWB])
    consts_d = din("consts", [128, NCONST])
    win_d = din("win", [L, 10, 128, 2048])
    wout_d = din("wout", [L, 4, 128, 2048])
    wup_d = din("wup", [L, 22, 128, 2048])
    wdn_d = din("wdn", [L, 2, 8, 128, 1408])
    wada_d = din("wada", [L, 24, 128, 2048])
    st_in = din("st_in", [128, L * NSTO])
    sT_in = din("sT_in", [128, L * 3 * 256])
    y_d = dout("y", [8, 128, TP])
    ys_d = dout("ys", [8, 128, 32])
    st_out = dout("st_out", [128, L * NSTO])
    sT_out = dout("sT_out", [128, L * 3 * 256])
    dbg_d = dout("dbg", [128, 8192]) if dbg else None

    S = Sched(nc)
    es = contextlib.ExitStack()
    with es:
        NA = 39100
        A = es.enter_context(nc.sbuf_tensor("A", [128, NA], F32))
        WR = es.enter_context(nc.sbuf_tensor("WR", [128, 4096], F32))
        HYR = es.enter_context(nc.sbuf_tensor("HYR", [128, 8 * NTT], F32))
        ACTR = es.enter_context(nc.sbuf_tensor("ACTR", [128, 11 * NTT], F32))
        PS = [es.enter_context(nc.psum_tensor("P%d" % i, [128, 512], F32)) for i in range(8)]
        cur = [0]

        def alloc(n):
            o = cur[0]
            cur[0] += n
            assert cur[0] <= NA, cur[0]
            return o

        def V(o, n):
            return A[:, o:o + n]

        psn = [0]

        def psum():
            b = PS[psn[0] % 6]
            psn[0] += 1
            return b

        def mm(out, lhsT, rhs, start=True, stop=True):
            S.op('pe', lambda e: e.matmul(out, lhsT=lhsT, rhs=rhs, start=start, stop=stop),
                 reads=[lhsT, rhs], writes=[out])

        def tr(out, in_, ident):
            S.op('pe', lambda e: e.transpose(out, in_, ident), reads=[in_, ident], writes=[out])

        def act(out, in_, func, bias=None, scale=1.0):
            kw = {}
            if bias is not None:
                kw['bias'] = bias
            S.op('act', lambda e: e.activation(out=out, in_=in_, func=func, scale=scale, **kw),
                 reads=[in_, bias, scale], writes=[out])

        def tt(eng, out, a, b, op):
            S.op(eng, lambda e: e.tensor_tensor(out=out, in0=a, in1=b, op=op), reads=[a, b], writes=[out])

        def ts(eng, out, a, s1, op0, s2=None, op1=None):
            if op1 is None:
                S.op(eng, lambda e: e.tensor_scalar(out=out, in0=a, scalar1=s1, scalar2=None, op0=op0),
                     reads=[a, s1], writes=[out])
            else:
                S.op(eng, lambda e: e.tensor_scalar(out=out, in0=a, scalar1=s1, scalar2=s2, op0=op0, op1=op1),
                     reads=[a, s1, s2], writes=[out])

        def stt(out, in0, scalar, in1, op0, op1):
            S.op('dve', lambda e: e.scalar_tensor_tensor(out=out, in0=in0, scalar=scalar, in1=in1, op0=op0, op1=op1),
                 reads=[in0, scalar, in1], writes=[out])

        def scan(out, d0, d1, init, op0=ALU.mult, op1=ALU.add):
            S.op('dve', lambda e: e.tensor_tensor_scan(out=out, data0=d0, data1=d1, initial=init, op0=op0, op1=op1),
                 reads=[d0, d1, init], writes=[out])

        def cp(eng, out, in_):
            if eng == 'act':
                act(out, in_, AF.Copy)
            else:
                S.op(eng, lambda e: e.tensor_copy(out=out, in_=in_), reads=[in_], writes=[out])

        def memset(eng, out, val):
            S.op(eng, lambda e: e.memset(out, val), writes=[out])

        def recip(out, in_):
            S.op('dve', lambda e: e.reciprocal(out=out, in_=in_), reads=[in_], writes=[out])

        dbgcol = [0]

        def dump(ap, n):
            if dbg_d is not None and dbgcol[0] + n <= 8192:
                S.dma(dbg_d[0:ap.shape[0], dbgcol[0]:dbgcol[0] + n], ap)
                dbgcol[0] += n

        o_const = alloc(NCONST)
        CON = V(o_const, NCONST)
        ident = A[:, o_const + C_ID:o_const + C_ID + 128]
        bones = A[:, o_const + C_BONES:o_const + C_BONES + 128]
        ones = A[:, o_const + C_ONES:o_const + C_ONES + 128]
        o_vec = alloc(L * NV)
        o_cT = alloc(32)
        o_mod = alloc(L * 48 * 4)
        o_sc = alloc(L * 2 * 8 * 4)
        o_st = alloc(L * NSTO)
        o_sT = alloc(L * 3 * 256)
        o_misc = alloc(64)
        o_lrusp = alloc(L * 2)
        o_lrusp2 = alloc(L * 2)
        o_omka = alloc(L * 4)
        o_s5rho = alloc(L * 8)
        o_s5t = alloc(L * 5 * 8 * SB)
        o_smw = alloc(NSW)
        o_stg = alloc(2048)
        o_X = alloc(8 * NTT)
        o_R1 = alloc(10400)
        o_R2 = alloc(9216)

        def vec(l, c, n=1):
            return A[:, o_vec + l * NV + c:o_vec + l * NV + c + n]

        def stc(l, c, n=1):
            return A[:, o_st + l * NSTO + c:o_st + l * NSTO + c + n]

        def s5tab(l, which, st_):
            o = o_s5t + ((l * 5 + which) * 8 + st_) * SB
            return A[:, o:o + SB]

        EPS = A[:, o_misc:o_misc + 1]
        GNEPS = A[:, o_misc + 1:o_misc + 2]
        ONEC = A[:, o_misc + 2:o_misc + 3]
        TINY = A[:, o_misc + 3:o_misc + 4]

        S.dma(CON, consts_d)
        S.dma(V(o_vec, L * NV).rearrange("p (l n) -> p l n", l=L), vecs_d.rearrange("l p n -> p l n"))
        S.dma(V(o_cT, 32), cT)
        S.dma(V(o_st, L * NSTO), st_in)
        S.dma(V(o_sT, L * 768), sT_in)
        memset('pool', EPS, 1e-6)
        memset('pool', GNEPS, 64e-5)
        memset('pool', ONEC, 1.0)
        memset('pool', TINY, 1e-24)

        slab_n = [0]

        def load_slab(src, n=2048, rnd=True):
            if not rnd:
                S.dma(V(o_stg, n), src)
                return None
            o = (slab_n[0] % 2) * 2048
            slab_n[0] += 1
            q = n // 4
            for i in range(4):
                S.dma(V(o_stg + i * q, q), src[:, i * q:(i + 1) * q])
                cp('act' if i % 2 == 0 else 'dve', RR(WR[:, o + i * q:o + (i + 1) * q]), V(o_stg + i * q, q))
            return o

        def slab_iter(srcs):
            nxt = load_slab(*srcs[0])
            for i in range(len(srcs)):
                cur = nxt
                if i + 1 < len(srcs):
                    nxt = load_slab(*srcs[i + 1])
                yield cur

        sc_ = V(o_cT, 32)
        sig = V(o_R2, 32)
        act(sig, sc_, AF.Sigmoid)
        tt('pool', sc_, sc_, sig, ALU.mult)
        for l in range(L):
            for sl in range(24):
                load_slab(wada_d[l, sl], rnd=False)
                o = o_stg
                ps = psum()
                for m in range(2):
                    for k in range(8):
                        mm(ps[:, m * 4:m * 4 + 4], A[:, o + k * 256 + m * 128:o + k * 256 + m * 128 + 128],
                           A[:, o_cT + k * 4:o_cT + k * 4 + 4], start=(k == 0), stop=(k == 7))
                for m in range(2):
                    mt = sl * 2 + m
                    act(A[:, o_mod + (l * 48 + mt) * 4:o_mod + (l * 48 + mt) * 4 + 4], ps[:, m * 4:m * 4 + 4],
                        AF.Identity, bias=vec(l, V_BADA + mt))

        def modc(l, j, ct, seq):
            o = o_mod + (l * 48 + j * 8 + ct) * 4 + seq
            return A[:, o:o + 1]

        def nsc(l, which, ct, seq):
            o = o_sc + ((l * 2 + which) * 8 + ct) * 4 + seq
            return A[:, o:o + 1]

        for l in range(L):
            for which, (jj, vv) in enumerate(((1, V_NM), (4, V_NF))):
                for ct in range(8):
                    o = o_sc + ((l * 2 + which) * 8 + ct) * 4
                    om = o_mod + (l * 48 + jj * 8 + ct) * 4
                    ts('pool', A[:, o:o + 4], A[:, om:om + 4], ONEC, ALU.add, vec(l, vv + ct), ALU.mult)

        for l in range(L):
            t = V(o_R2, 2)
            act(t, vec(l, V_LLAM, 2), AF.Exp, scale=-1.0)
            act(t, t, AF.Ln, bias=ONEC)
            ts('pool', A[:, o_lrusp + l * 2:o_lrusp + l * 2 + 2], t, -8.0, ALU.mult)
            ts('pool', A[:, o_lrusp2 + l * 2:o_lrusp2 + l * 2 + 2], t, -16.0, ALU.mult)
            ts('pool', A[:, o_omka + l * 4:o_omka + l * 4 + 4], vec(l, V_KA, 4), -1.0, ALU.mult, 1.0, ALU.add)
            W8 = [V(o_R2 + 16 + 8 * i, 8) for i in range(24)]
            dt, mag, th, tq, ti, fr_, cs_, sn_ = W8[0:8]
            act(dt, vec(l, V_S5DT, 8), AF.Exp)
            tt('pool', mag, dt, vec(l, V_S5AR, 8), ALU.mult)
            act(mag, mag, AF.Exp)
            cp('pool', A[:, o_s5rho + l * 8:o_s5rho + l * 8 + 8], mag)
            tt('pool', th, dt, vec(l, V_S5AI, 8), ALU.mult)
            for (dst, offs) in ((sn_, 0.5), (cs_, 0.75)):
                ts('dve', tq, th, 1.0 / (2 * math.pi), ALU.mult, offs, ALU.add)
                tq_i = tq.bitcast(mybir.dt.int32)
                S.op('dve', lambda e, a=ti.bitcast(mybir.dt.int32), b=tq: e.tensor_copy(out=a, in_=b),
                     reads=[tq], writes=[ti])
                S.op('dve', lambda e, a=fr_, b=ti.bitcast(mybir.dt.int32): e.tensor_copy(out=a, in_=b),
                     reads=[ti], writes=[fr_])
                tt('dve', fr_, tq, fr_, ALU.subtract)
                ts('dve', ti, fr_, 0.0, ALU.is_lt)
                tt('dve', fr_, fr_, ti, ALU.add)
                ts('dve', fr_, fr_, 2 * math.pi, ALU.mult, -math.pi, ALU.add)
                ts('dve', fr_, fr_, math.pi, ALU.min, -math.pi, ALU.max)
                act(dst, fr_, AF.Sin)
            abr, abi, den, frr, fii, t1, t2, t3 = W8[8:16]
            tt('pool', abr, mag, cs_, ALU.mult)
            tt('pool', abi, mag, sn_, ALU.mult)
            are, aim = vec(l, V_S5AR, 8), vec(l, V_S5AI, 8)
            tt('pool', t1, are, are, ALU.mult)
            tt('pool', t2, aim, aim, ALU.mult)
            tt('pool', den, t1, t2, ALU.add)
            recip(den, den)
            ts('pool', t3, abr, -1.0, ALU.add)
            tt('pool', t1, t3, are, ALU.mult)
            tt('pool', t2, abi, aim, ALU.mult)
            tt('pool', t1, t1, t2, ALU.add)
            tt('pool', frr, t1, den, ALU.mult)
            tt('pool', t1, abi, are, ALU.mult)
            tt('pool', t2, t3, aim, ALU.mult)
            tt('pool', t1, t1, t2, ALU.subtract)
            tt('pool', fii, t1, den, ALU.mult)
            for st_ in range(8):
                Er, Ei = s5tab(l, 2, st_), s5tab(l, 3, st_)
                cp('pool', Er[:, 0:1], cs_[:, st_:st_ + 1])
                cp('pool', Ei[:, 0:1], sn_[:, st_:st_ + 1])
                n = 1
                tmpa, tmpb = V(o_R2 + 256, SB), V(o_R2 + 256 + SB, SB)
                while n < SB:
                    cr, ci = Er[:, n - 1:n], Ei[:, n - 1:n]
                    ts('dve', tmpa[:, 0:n], Er[:, 0:n], cr, ALU.mult)
                    ts('dve', tmpb[:, 0:n], Ei[:, 0:n], ci, ALU.mult)
                    tt('dve', Er[:, n:2 * n], tmpa[:, 0:n], tmpb[:, 0:n], ALU.subtract)
                    ts('dve', tmpa[:, 0:n], Er[:, 0:n], ci, ALU.mult)
                    ts('dve', tmpb[:, 0:n], Ei[:, 0:n], cr, ALU.mult)
                    tt('dve', Ei[:, n:2 * n], tmpa[:, 0:n], tmpb[:, 0:n], ALU.add)
                    n *= 2
                Epr, Epi = s5tab(l, 0, st_), s5tab(l, 1, st_)
                fr1, fi1 = frr[:, st_:st_ + 1], fii[:, st_:st_ + 1]
                ts('dve', tmpa, Er, fr1, ALU.mult)
                ts('dve', tmpb, Ei, fi1, ALU.mult)
                tt('dve', Epr, tmpa, tmpb, ALU.add)
                ts('dve', tmpa, Er, fi1, ALU.mult)
                ts('dve', tmpb, Ei, fr1, ALU.mult)
                tt('dve', Epi, tmpa, tmpb, ALU.subtract)
                cp('pool', s5tab(l, 4, st_), mag[:, st_:st_ + 1].to_broadcast([128, SB]))

        Xb = [V(o_X + ct * NTT, NTT) for ct in range(8)]
        HYb = [HYR[:, ct * NTT:(ct + 1) * NTT] for ct in range(8)]
        FINb = [V(o_R1 + ct * NTT, NTT) for ct in range(8)]

        def rmsnorm_mod(l, which, NT, segs, final=False):
            ps = psum()
            for ct in range(8):
                sq = V(o_R2 + (ct % 2) * NTT, NTT)
                act(sq[:, 0:NT], Xb[ct][:, 0:NT], AF.Square)
                mm(ps[:, 0:NT], ones, sq[:, 0:NT], start=(ct == 0), stop=(ct == 7))
            rstd = V(o_R2 + 2 * NTT, NTT)
            act(rstd[:, 0:NT], ps[:, 0:NT], AF.Sqrt, bias=EPS, scale=1.0 / D)
            recip(rstd[:, 0:NT], rstd[:, 0:NT])
            for ct in range(8):
                ntmp = V(o_R2 + (3 + ct % 2) * NTT, NTT)
                tt('pool', ntmp[:, 0:NT], Xb[ct][:, 0:NT], rstd[:, 0:NT], ALU.mult)
                for (seq, c0, ln) in segs:
                    if final:
                        ts('dve', FINb[ct][:, c0:c0 + ln], ntmp[:, c0:c0 + ln], vec(0, V_NFIN + ct), ALU.mult)
                    else:
                        act(RR(HYb[ct][:, c0:c0 + ln]), ntmp[:, c0:c0 + ln], AF.Identity,
                            bias=modc(l, 0 if which == 0 else 3, ct, seq), scale=nsc(l, which, ct, seq))

        def process(NT, segs, C, src, dst):
            if stage < 1:
                return
            nseg = len(segs)
            SL = segs[0][2]
            S.dma(V(o_X, 8 * NTT).rearrange("p (c t) -> p c t", c=8)[:, :, 0:NT], src.rearrange("c p t -> p c t"))
            for l in range(L):
                S.dma(V(o_smw, NSW), smallw_d[l])
                ts('pool', V(o_smw + W_CIM, 512), V(o_smw + W_CIM, 512), -1.0, ALU.mult)
                smw = lambda c, n: A[:, o_smw + c:o_smw + c + n]
                rmsnorm_mod(l, 0, NT, segs)
                o_G = o_R1
                o_LX = o_G + 2 * NTT
                o_PR = o_LX + 2 * 520
                o_U5 = o_PR + 14 * 520
                assert o_U5 + 2 * NTT <= o_R1 + 10400
                Gb = [V(o_G + i * NTT, NTT) for i in range(2)]
                LXf = [V(o_LX + i * 520, 520) for i in range(2)]
                PRf = [V(o_PR + i * 520, 520) for i in range(14)]
                U5 = [V(o_U5 + i * NTT, NTT) for i in range(2)]

                def segview(buf, H):
                    return buf[:, 0:nseg * (H + SL)].rearrange("p (s t) -> p s t", s=nseg)

                for si, (seq, c0, ln) in enumerate(segs):
                    for ct in range(2):
                        cp('pool', segview(LXf[ct], 3)[:, si, 0:3], stc(l, O_LH + (ct * 3 + seq) * 3, 3))
                    for ct in range(14):
                        cp('pool', segview(PRf[ct], 2)[:, si, 1:2], stc(l, O_SH + ct * 3 + seq))
                sl_it = slab_iter([(win_d[l, sl], 2048) for sl in range(10)])
                for sl in range(10):
                    o = next(sl_it)
                    pss = [psum() for _ in range(2)]
                    for k in range(8):
                        for m in range(2):
                            mm(pss[m][:, 0:NT], RR(WR[:, o + k * 256 + m * 128:o + k * 256 + m * 128 + 128]),
                               RR(HYb[k][:, 0:NT]), start=(k == 0), stop=(k == 7))
                    for m in range(2):
                        mt = sl * 2 + m
                        src_ps = pss[m][:, 0:NT]
                        if mt < 2:
                            cp('act', Gb[mt][:, 0:NT], src_ps)
                        elif mt < 4:
                            cp('act', segview(LXf[mt - 2], 3)[:, :, 3:3 + SL], src_ps.rearrange("p (s t) -> p s t", s=nseg))
                        elif mt < 18:
                            cp('act' if mt % 2 else 'dve', segview(PRf[mt - 4], 2)[:, :, 2:2 + SL],
                               src_ps.rearrange("p (s t) -> p s t", s=nseg))
                        else:
                            cp('dve', U5[mt - 18][:, 0:NT], src_ps)
                for si, (seq, c0, ln) in enumerate(segs):
                    for ct in range(2):
                        cp('pool', stc(l, O_LH + (ct * 3 + seq) * 3, 3), segview(LXf[ct], 3)[:, si, SL:SL + 3])
                    for ct in range(14):
                        cp('pool', stc(l, O_SH + ct * 3 + seq), segview(PRf[ct], 2)[:, si, SL + 1:SL + 2])

                if stage < 2:
                    continue
                for ct in range(2):
                    xc = V(o_R2, NTT)
                    xv = segview(LXf[ct], 3)
                    xc3 = xc[:, 0:NT].rearrange("p (s t) -> p s t", s=nseg)
                    cw = lambda k: vec(l, V_LCW + ct * 4 + k)
                    ts('dve', xc3, xv[:, :, 0:SL], cw(0), ALU.mult, vec(l, V_LCB + ct), ALU.add)
                    for k in range(1, 4):
                        stt(xc3, xv[:, :, k:k + SL], cw(k), xc3, ALU.mult, ALU.add)
                    ps1, ps2 = psum(), psum()
                    mm(ps1[:, 0:NT], smw(W_WA + ct * 128, 128), xc[:, 0:NT])
                    mm(ps2[:, 0:NT], smw(W_WX + ct * 128, 128), xc[:, 0:NT])
                    rg, ig = V(o_R2 + NTT, NTT), V(o_R2 + 2 * NTT, NTT)
                    act(rg[:, 0:NT], ps1[:, 0:NT], AF.Sigmoid, bias=vec(l, V_LBA + ct))
                    act(ig[:, 0:NT], ps2[:, 0:NT], AF.Sigmoid, bias=vec(l, V_LBX + ct))
                    aa, gn = V(o_R2 + 3 * NTT, NTT), V(o_R2 + 4 * NTT, NTT)
                    act(aa[:, 0:NT], rg[:, 0:NT], AF.Exp, scale=A[:, o_lrusp + l * 2 + ct:o_lrusp + l * 2 + ct + 1])
                    act(gn[:, 0:NT], rg[:, 0:NT], AF.Exp, scale=A[:, o_lrusp2 + l * 2 + ct:o_lrusp2 + l * 2 + ct + 1])
                    ts('pool', gn[:, 0:NT], gn[:, 0:NT], -1.0, ALU.mult, 1.0, ALU.add)
                    ts('pool', gn[:, 0:NT], gn[:, 0:NT], 0.0, ALU.max)
                    act(gn[:, 0:NT], gn[:, 0:NT], AF.Sqrt)
                    tt('pool', ig[:, 0:NT], ig[:, 0:NT], xc[:, 0:NT], ALU.mult)
                    tt('pool', ig[:, 0:NT], ig[:, 0:NT], gn[:, 0:NT], ALU.mult)
                    hh = V(o_R2 + 5 * NTT, NTT)
                    for (seq, c0, ln) in segs:
                        hst = stc(l, O_Lh + ct * 3 + seq)
                        scan(hh[:, c0:c0 + ln], aa[:, c0:c0 + ln], ig[:, c0:c0 + ln], hst)
                        cp('pool', hst, hh[:, c0 + ln - 1:c0 + ln])
                    act(rg[:, 0:NT], Gb[ct][:, 0:NT], AF.Gelu_apprx_tanh)
                    tt('pool', RR(HYb[ct][:, 0:NT]), hh[:, 0:NT], rg[:, 0:NT], ALU.mult)

                if stage < 3:
                    continue
                o_s = o_R2
                nsb_list = []
                for (seq, c0, ln) in segs:
                    for s0 in range(0, ln, SB):
                        nsb_list.append((seq, c0 + s0, min(SB, ln - s0), s0 + SB >= ln, s0 == 0))
                SBL = nsb_list[0][2]
                nsb = len(nsb_list)
                ypss = [PS[6], PS[7]]
                for st_ in range(8):
                    kt, half, par = st_ // 4, (st_ % 4) // 2, st_ % 2
                    pr = slice(64 * half, 64 * half + 64)
                    pbr, pbi = psum(), psum()
                    mm(pbr[:, 0:NT], A[pr, o_smw + W_BRE + (kt * 2 + par) * 128:o_smw + W_BRE + (kt * 2 + par) * 128 + 128],
                       U5[kt][pr, 0:NT])
                    mm(pbi[:, 0:NT], A[pr, o_smw + W_BIM + (kt * 2 + par) * 128:o_smw + W_BIM + (kt * 2 + par) * 128 + 128],
                       U5[kt][pr, 0:NT])
                    base = o_s + (st_ % 2) * 4352
                    cre, cim, t1, t2, gre, gim, hre, him = [V(base + i * NTT, NTT) for i in range(8)]
                    def bt(tab):
                        return tab[:, 0:SBL].unsqueeze(1).to_broadcast([128, nsb, SBL])
                    v3 = lambda b: b[:, 0:NT].rearrange("p (s t) -> p s t", s=nsb)
                    Epr, Epi, Er, Ei, Rh = [s5tab(l, i, st_) for i in range(5)]
                    tt('dve', v3(t1), v3(pbr), bt(Epr), ALU.mult)
                    tt('dve', v3(t2), v3(pbi), bt(Epi), ALU.mult)
                    tt('pool', cre[:, 0:NT], t1[:, 0:NT], t2[:, 0:NT], ALU.subtract)
                    tt('dve', v3(t1), v3(pbi), bt(Epr), ALU.mult)
                    tt('dve', v3(t2), v3(pbr), bt(Epi), ALU.mult)
                    tt('pool', cim[:, 0:NT], t1[:, 0:NT], t2[:, 0:NT], ALU.add)
                    for (seq, c0, ln, last, first_sb) in nsb_list:
                        hr0, hi0 = stc(l, O_S5R + st_ * 3 + seq), stc(l, O_S5I + st_ * 3 + seq)
                        ini_r = hr0 if first_sb else hre[:, c0 - 1:c0]
                        ini_i = hi0 if first_sb else him[:, c0 - 1:c0]
                        cs = slice(c0, c0 + ln)
                        scan(gre[:, cs], Rh[:, 0:ln], cre[:, cs], ini_r)
                        scan(gim[:, cs], Rh[:, 0:ln], cim[:, cs], ini_i)
                        q1, q2, q3, q4 = [V(base + 4096 + i * 64, 64) for i in range(4)]
                        tt('pool', q1[:, 0:ln], gre[:, cs], Er[:, 0:ln], ALU.mult)
                        tt('pool', q2[:, 0:ln], gim[:, cs], Ei[:, 0:ln], ALU.mult)
                        tt('pool', hre[:, cs], q1[:, 0:ln], q2[:, 0:ln], ALU.subtract)
                        tt('dve', q3[:, 0:ln], gim[:, cs], Er[:, 0:ln], ALU.mult)
                        tt('dve', q4[:, 0:ln], gre[:, cs], Ei[:, 0:ln], ALU.mult)
                        tt('dve', him[:, cs], q3[:, 0:ln], q4[:, 0:ln], ALU.add)
                        if last:
                            cp('act', hr0, hre[:, c0 + ln - 1:c0 + ln])
                            cp('act', hi0, him[:, c0 + ln - 1:c0 + ln])
                    j, hf = st_ // 4, (st_ % 4) // 2
                    po = slice(64 * hf, 64 * hf + 64)
                    first = (st_ % 2 == 0)
                    mm(ypss[j][po, 0:NT], smw(W_CRE + st_ * 64, 64), hre[:, 0:NT], start=first, stop=False)
                    mm(ypss[j][po, 0:NT], smw(W_CIM + st_ * 64, 64), him[:, 0:NT], start=False, stop=(not first))
                zb = [V(o_s + i * NTT, NTT) for i in range(2)]
                for j in range(2):
                    stt(zb[j][:, 0:NT], U5[j][:, 0:NT], vec(l, V_S5D + j), ypss[j][:, 0:NT], ALU.mult, ALU.add)
                    act(zb[j][:, 0:NT], zb[j][:, 0:NT], AF.Gelu_apprx_tanh)
                for j in range(2):
                    ps = psum()
                    for k in range(2):
                        mm(ps[:, 0:NT], smw(W_GLU + k * 256 + j * 128, 128), zb[k][:, 0:NT], start=(k == 0), stop=(k == 1))
                    gt = V(o_s + 2 * NTT, NTT)
                    act(gt[:, 0:NT], ps[:, 0:NT], AF.Sigmoid, bias=vec(l, V_GLUB + j))
                    tt('pool', RR(HYb[6 + j][:, 0:NT]), zb[j][:, 0:NT], gt[:, 0:NT], ALU.mult)

                if stage < 4:
                    continue
                S.dma(V(o_smw, NSWB), smallwB_d[l])
                for ct in range(14):
                    pv = segview(PRf[ct], 2)
                    dtmp = V(o_R2 + (ct % 2) * NTT, NTT)
                    d3 = dtmp[:, 0:NT].rearrange("p (s t) -> p s t", s=nseg)
                    tt('pool', d3, pv[:, :, 1:1 + SL], pv[:, :, 2:2 + SL], ALU.subtract)
                    stt(pv[:, :, 2:2 + SL], d3, vec(l, V_MU + ct), pv[:, :, 2:2 + SL], ALU.mult, ALU.add)
                o_c = o_R2 + 2 * NTT

                def xm(ct):
                    if nseg == 1:
                        return PRf[ct][:, 2:2 + NT]
                    return None
                if nseg > 1:
                    for ct in range(14):
                        tmpc = V(o_c, NT)
                        cp('pool', tmpc.rearrange("p (s t) -> p s t", s=nseg), segview(PRf[ct], 2)[:, :, 2:2 + SL])
                        cp('pool', PRf[ct][:, 0:NT], tmpc)
                    xm = lambda ct: PRf[ct][:, 0:NT]
                lo_t = V(o_R2, NTT)
                act(lo_t[0:64, 0:NT], xm(12)[0:64, :], AF.Tanh)
                cp('pool', lo_t[64:128, 0:NT], xm(12)[64:128, :])
                gs_t = V(o_R2 + NTT, NTT)
                act(gs_t[:, 0:NT], xm(13), AF.Sigmoid)
                YR = [HYb[2 + hp] for hp in range(4)]
                for hp in range(4):
                    o_f = o_R2 + 2 * NTT
                    F = [V(o_f + i * NTT, NTT) for i in range(9)]
                    lw, aa, kkn, bb, csb, t1, t2, t3, t4 = F
                    r_, k_, v_ = xm(hp), xm(4 + hp), xm(8 + hp)
                    gg, bonus, gex, at_f, ig_, gC, bh_f = t4, t3, t1, t1, kkn, lw, aa
                    kh_f, kt_f, gamC, rt_f, bt_f = gC, ig_, csb, t2, bb
                    rmask = A[:, o_const + C_RESET:o_const + C_RESET + NTT]

                    def prep_gen(a_, b_, hp=hp, r_=r_, k_=k_, v_=v_):
                        ps1, ps2, ps3 = psum(), psum(), psum()
                        mm(ps1[:, a_:b_], A[0:64, o_smw + W_LORA + hp * 128:o_smw + W_LORA + hp * 128 + 128], lo_t[0:64, a_:b_])
                        yield
                        mm(ps2[:, a_:b_], A[64:128, o_smw + W_LORA + hp * 128:o_smw + W_LORA + hp * 128 + 128], lo_t[64:128, a_:b_])
                        yield
                        mm(ps3[:, a_:b_], smw(W_G2 + hp * 128, 128), gs_t[:, a_:b_])
                        yield
                        act(lw[:, a_:b_], ps1[:, a_:b_], AF.Sigmoid, bias=vec(l, V_W0 + hp))
                        yield
                        ts('pool', lw[:, a_:b_], lw[:, a_:b_], -0.6065306597126334, ALU.mult)
                        yield
                        act(aa[:, a_:b_], ps2[:, a_:b_], AF.Sigmoid, bias=vec(l, V_A0 + hp))
                        yield
                        cp('act', gg[:, a_:b_], ps3[:, a_:b_])
                        yield
                        ts('pool', kkn[:, a_:b_], k_[:, a_:b_], vec(l, V_KK + hp), ALU.mult)
                        yield
                        tt('dve', t1[:, a_:b_], kkn[:, a_:b_], kkn[:, a_:b_], ALU.mult)
                        yield
                        ps = psum()
                        mm(ps[:, a_:b_], bones, t1[:, a_:b_])
                        yield
                        ts('dve', t1[:, a_:b_], ps[:, a_:b_], TINY, ALU.max)
                        yield
                        act(t1[:, a_:b_], t1[:, a_:b_], AF.Sqrt)
                        yield
                        recip(t1[:, a_:b_], t1[:, a_:b_])
                        yield
                        tt('pool', kkn[:, a_:b_], kkn[:, a_:b_], t1[:, a_:b_], ALU.mult)
                        yield
                        ts('pool', t1[:, a_:b_], aa[:, a_:b_], vec(l, V_KA + hp), ALU.mult, A[:, o_omka + l * 4 + hp:o_omka + l * 4 + hp + 1], ALU.add)
                        yield
                        tt('pool', k_[:, a_:b_], k_[:, a_:b_], t1[:, a_:b_], ALU.mult)
                        yield
                        tt('dve', bb[:, a_:b_], kkn[:, a_:b_], aa[:, a_:b_], ALU.mult)
                        yield
                        stt(t1[:, a_:b_], r_[:, a_:b_], vec(l, V_RK + hp), k_[:, a_:b_], ALU.mult, ALU.mult)
                        yield
                        ps = psum()
                        mm(ps[:, a_:b_], bones, t1[:, a_:b_])
                        yield
                        tt('dve', bonus[:, a_:b_], ps[:, a_:b_], v_[:, a_:b_], ALU.mult)
                        yield
                        if C == CH:
                            scan(csb[:, a_:b_], rmask[:, a_:b_], lw[:, a_:b_], 0.0)
                            yield
                        else:
                            for (seq, c0, ln) in segs:
                                scan(csb[:, c0:c0 + ln], rmask[:, 1:1 + ln], lw[:, c0:c0 + ln], 0.0)
                                yield
                        nch = (b_ - a_) // C
                        c3 = lambda bf: bf[:, a_:b_].rearrange("p (c t) -> p c t", c=nch)
                        csC = c3(csb)[:, :, C - 1:C].to_broadcast([128, nch, C])
                        tt('pool', gex[:, a_:b_], csb[:, a_:b_], lw[:, a_:b_], ALU.subtract)
                        yield
                        act(gex[:, a_:b_], gex[:, a_:b_], AF.Exp)
                        yield
                        tt('dve', at_f[:, a_:b_], kkn[:, a_:b_], gex[:, a_:b_], ALU.mult)
                        yield
                        act(ig_[:, a_:b_], csb[:, a_:b_], AF.Exp, scale=-1.0)
                        yield
                        tt('pool', c3(gC), csC, c3(csb), ALU.subtract)
                        yield
                        act(gC[:, a_:b_], gC[:, a_:b_], AF.Exp)
                        yield
                        tt('dve', bh_f[:, a_:b_], bb[:, a_:b_], gC[:, a_:b_], ALU.mult)
                        yield
                        tt('pool', bb[:, a_:b_], bb[:, a_:b_], ig_[:, a_:b_], ALU.mult)
                        yield
                        tt('dve', kh_f[:, a_:b_], k_[:, a_:b_], gC[:, a_:b_], ALU.mult)
                        yield
                        tt('pool', kt_f[:, a_:b_], k_[:, a_:b_], ig_[:, a_:b_], ALU.mult)
                        yield
                        act(csb[:, a_:b_], csb[:, a_:b_], AF.Exp)
                        yield
                        tt('dve', rt_f[:, a_:b_], r_[:, a_:b_], csb[:, a_:b_], ALU.mult)
                        yield
                    HS = [(0, 256), (256, 512)] if NT == NTT else [(0, NT)]
                    gens = [prep_gen(a_, b_) for (a_, b_) in HS]
                    while gens:
                        for g_ in list(gens):
                            try:
                                next(g_)
                            except StopIteration:
                                gens.remove(g_)
                    o_m = o_R2 + 11 * NTT
                    chunks = []
                    for (seq, c0, ln) in segs:
                        for s0 in range(0, ln, C):
                            chunks.append((seq, c0 + s0))
                    for (seq, c0) in chunks:
                        if sub < 1:
                            break
                        cc = slice(c0, c0 + C)
                        pst = psum()
                        for i, srcf in enumerate((at_f, None, bh_f, kh_f)):
                            sap = v_[:, c0:c0 + C] if srcf is None else srcf[:, cc]
                            tr(pst[0:C, i * 128:i * 128 + 128], sap, ident)
                        TM = V(o_m, 512)
                        cp('act', TM[0:C, :], pst[0:C, :])
                        At_t, V_t, Bh_t, Kh_t = [TM[0:C, i * 128:i * 128 + 128] for i in range(4)]
                        U = []
                        for hl in range(2):
                            pr = slice(64 * hl, 64 * hl + 64)
                            ob = o_m + 512 + hl * 1280
                            NTm, MTm, Nm, PBm, QKm, Pa, PaT, Pb, PbT, Tm = [A[0:C, ob + i * 128:ob + i * 128 + C] for i in range(10)]
                            U.append(dict(pr=pr, S1=A[0:C, ob:ob + 128], NT=NTm, MT=MTm, N=Nm, PB=PBm, QK=QKm, Pa=Pa, PaT=PaT, Pb=Pb, PbT=PbT, T=Tm))
                        msu = A[0:C, o_const + C_MSU:o_const + C_MSU + C]
                        msl = A[0:C, o_const + C_MSL:o_const + C_MSL + C]
                        miu = A[0:C, o_const + C_MIU:o_const + C_MIU + C]
                        nmsu = A[0:C, o_const + C_NMSU:o_const + C_NMSU + C]
                        nmsl = A[0:C, o_const + C_NMSL:o_const + C_NMSL + C]
                        for u in U:
                            if not (bits & 2):
                                break
                            pr = u['pr']
                            pA, pB = psum(), psum()
                            mm(pA[0:C, 0:C], at_f[pr, cc], bt_f[pr, cc])
                            mm(pA[0:C, 128:128 + C], at_f[pr, cc], kt_f[pr, cc])
                            mm(pB[0:C, 0:C], bt_f[pr, cc], at_f[pr, cc])
                            mm(pB[0:C, 128:128 + C], bt_f[pr, cc], rt_f[pr, cc])
                            mm(pB[0:C, 256:256 + C], kt_f[pr, cc], rt_f[pr, cc])
                            if not (bits & 4):
                                continue
                            cp('act', u['PaT'], pA[0:C, 0:C]); tt('pool', u['PaT'], u['PaT'], nmsl, ALU.mult)
                            cp('act', u['MT'], pA[0:C, 128:128 + C]); tt('pool', u['MT'], u['MT'], msl, ALU.mult)
                            if not (bits & 16):
                                continue
                            cp('act', u['Pa'], pB[0:C, 0:C]); tt('pool', u['Pa'], u['Pa'], nmsu, ALU.mult)
                            cp('act', u['PB'], pB[0:C, 128:128 + C]); tt('pool', u['PB'], u['PB'], miu, ALU.mult)
                            cp('act', u['QK'], pB[0:C, 256:256 + C]); tt('pool', u['QK'], u['QK'], miu, ALU.mult)
                            if bits & 8:
                                tt('pool', u['T'], u['Pa'], ident[0:C, 0:C], ALU.add)
                        if sub < 2:
                            continue
                        nlev = int(round(math.log2(C))) - 1
                        for lev in range(1, nlev + 1):
                            for u in U:
                                Pp, PpT = (u['Pa'], u['PaT']) if lev % 2 == 1 else (u['Pb'], u['PbT'])
                                Pn, PnT = (u['Pb'], u['PbT']) if lev % 2 == 1 else (u['Pa'], u['PaT'])
                                pq = psum()
                                mm(pq[0:C, 0:C], Pp, PpT)
                                if lev < nlev:
                                    mm(pq[0:C, 128:128 + C], PpT, Pp)
                                cp('act', PnT, pq[0:C, 0:C])
                                if lev < nlev:
                                    cp('act', Pn, pq[0:C, 128:128 + C])
                                mm(pq[0:C, 256:256 + C], PnT, u['T'])
                                tt('dve', u['T'], pq[0:C, 256:256 + C], u['T'], ALU.add)
                        if sub < 3:
                            continue
                        for u in U:
                            pr = u['pr']
                            hc = slice(pr.start, pr.stop)
                            pq = psum()
                            mm(pq[0:C, 0:64], u['T'], At_t[:, hc])
                            mm(pq[0:C, 128:128 + C], u['MT'], u['T'])
                            Ab = u['S1'][:, 0:64]
                            MTT = u['N']
                            cp('act', Ab, pq[0:C, 0:64])
                            cp('act', MTT, pq[0:C, 128:128 + C])
                            mm(pq[0:C, 256:320], MTT, V_t[:, hc])
                            nUt = u['S1'][:, 64:128]
                            ts('dve', nUt, pq[0:C, 256:320], -1.0, ALU.mult)
                            pg = psum()
                            mm(pg[pr, 0:64], Ab, Bh_t[:, hc])
                            mm(pg[pr, 128:128 + C], Ab, u['PB'])
                            Gm = A[pr, o_m + 3072 + 0:o_m + 3072 + 64]
                            Rh = A[pr, o_m + 3072 + 64:o_m + 3072 + 64 + C]
                            stt(Gm, ident[pr, hc], gamC[pr, c0 + C - 1:c0 + C], pg[pr, 0:64], ALU.mult, ALU.subtract)
                            tt('dve', Rh, rt_f[pr, cc], pg[pr, 128:128 + C], ALU.subtract)
                            STm = A[pr, o_sT + (l * 3 + seq) * 256 + hp * 64:o_sT + (l * 3 + seq) * 256 + hp * 64 + 64]
                            ph = psum()
                            mm(ph[pr, 128:128 + C], nUt, u['PB'], start=True, stop=False)
                            mm(ph[pr, 128:128 + C], V_t[:, hc], u['QK'], start=False, stop=False)
                            mm(ph[pr, 128:128 + C], STm, Rh, start=False, stop=True)
                            cp('act', RR(YR[hp][pr, cc]), ph[pr, 128:128 + C])
                            ph2 = psum()
                            mm(ph2[pr, 0:64], Bh_t[:, hc], nUt, start=True, stop=False)
                            mm(ph2[pr, 0:64], Kh_t[:, hc], V_t[:, hc], start=False, stop=False)
                            mm(ph2[pr, 0:64], Gm, STm, start=False, stop=True)
                            cp('dve', STm, ph2[pr, 0:64])
                    y = YR[hp]
                    ps = psum()
                    mm(ps[:, 0:NT], bones, y[:, 0:NT])
                    yc = t1
                    stt(yc[:, 0:NT], ps[:, 0:NT], -1.0 / 64, y[:, 0:NT], ALU.mult, ALU.add)
                    tt('dve', t2[:, 0:NT], yc[:, 0:NT], yc[:, 0:NT], ALU.mult)
                    ps = psum()
                    mm(ps[:, 0:NT], bones, t2[:, 0:NT])
                    act(t2[:, 0:NT], ps[:, 0:NT], AF.Sqrt, bias=GNEPS, scale=1.0 / 64)
                    recip(t2[:, 0:NT], t2[:, 0:NT])
                    tt('pool', yc[:, 0:NT], yc[:, 0:NT], t2[:, 0:NT], ALU.mult)
                    ts('pool', yc[:, 0:NT], yc[:, 0:NT], vec(l, V_LNW + hp), ALU.mult, vec(l, V_LNB + hp), ALU.add)
                    tt('dve', yc[:, 0:NT], yc[:, 0:NT], bonus[:, 0:NT], ALU.add)
                    tt('pool', RR(y[:, 0:NT]), yc[:, 0:NT], gg[:, 0:NT], ALU.mult)

                if stage < 5:
                    continue
                sl_it = slab_iter([(wout_d[l, sl], 2048) for sl in range(4)])
                for sl in range(4):
                    o = next(sl_it)
                    pss = [psum() for _ in range(2)]
                    for k in range(8):
                        for m in range(2):
                            mm(pss[m][:, 0:NT], RR(WR[:, o + k * 256 + m * 128:o + k * 256 + m * 128 + 128]),
                               RR(HYb[k][:, 0:NT]), start=(k == 0), stop=(k == 7))
                    for m in range(2):
                        ct = sl * 2 + m
                        for (seq, c0, ln) in segs:
                            stt(Xb[ct][:, c0:c0 + ln], pss[m][:, c0:c0 + ln], modc(l, 2, ct, seq), Xb[ct][:, c0:c0 + ln],
                                ALU.mult, ALU.add)
                if stage < 6:
                    continue
                rmsnorm_mod(l, 1, NT, segs)
                ACTB = [ACTR[:, i * NTT:(i + 1) * NTT] for i in range(11)]
                ffn_srcs = []
                for half in range(2):
                    ffn_srcs += [(wup_d[l, half * 11 + j], 2048) for j in range(11)]
                    ffn_srcs += [(wdn_d[l, half, mt], 1408) for mt in range(8)]
                ffn_it = slab_iter(ffn_srcs)
                for half in range(2):
                    for j in range(11):
                        mg = half * 11 + j
                        o = next(ffn_it)
                        pss = [psum() for _ in range(2)]
                        for k in range(8):
                            for m in range(2):
                                mm(pss[m][:, 0:NT], RR(WR[:, o + k * 256 + m * 128:o + k * 256 + m * 128 + 128]),
                                   RR(HYb[k][:, 0:NT]), start=(k == 0), stop=(k == 7))
                        cv = []
                        for m in range(2):
                            ctp = mg * 2 + m
                            bi = (j % 2) * 2 + m
                            ub = V(o_R2 + bi * 520, 520)
                            uv = segview(ub, 2)
                            for si, (seq, c0, ln) in enumerate(segs):
                                cp('pool', uv[:, si, 0:2], stc(l, O_FH + (ctp * 3 + seq) * 2, 2))
                            cp('act', uv[:, :, 2:2 + SL], pss[m][:, 0:NT].rearrange("p (s t) -> p s t", s=nseg))
                            for si, (seq, c0, ln) in enumerate(segs):
                                cp('pool', stc(l, O_FH + (ctp * 3 + seq) * 2, 2), uv[:, si, SL:SL + 2])
                            co = V(o_R2 + 4 * 520 + bi * NTT, NTT)
                            co3 = co[:, 0:NT].rearrange("p (s t) -> p s t", s=nseg)
                            cv.append((co, co3, uv, ctp))
                        for (co, co3, uv, ctp) in cv:
                            ts('dve', co3, uv[:, :, 0:SL], vec(l, V_FCW + ctp * 3), ALU.mult, vec(l, V_FCB + ctp), ALU.add)
                        for kk_ in (1, 2):
                            for (co, co3, uv, ctp) in cv:
                                stt(co3, uv[:, :, kk_:kk_ + SL], vec(l, V_FCW + ctp * 3 + kk_), co3, ALU.mult, ALU.add)
                        val, gate = cv[0][0], cv[1][0]
                        act(gate[:, 0:NT], gate[:, 0:NT], AF.Silu)
                        tt('pool', RR(ACTB[j][:, 0:NT]), val[:, 0:NT], gate[:, 0:NT], ALU.mult)
                    for mt in range(8):
                        o = next(ffn_it)
                        ps = psum()
                        for k in range(11):
                            mm(ps[:, 0:NT], RR(WR[:, o + k * 128:o + k * 128 + 128]), RR(ACTB[k][:, 0:NT]),
                               start=(k == 0), stop=(k == 10))
                        for (seq, c0, ln) in segs:
                            stt(Xb[mt][:, c0:c0 + ln], ps[:, c0:c0 + ln], modc(l, 5, mt, seq), Xb[mt][:, c0:c0 + ln],
                                ALU.mult, ALU.add)
            rmsnorm_mod(0, 0, NT, segs, final=True)
            S.dma(dst.rearrange("c p t -> p c t"), V(o_R1, 8 * NTT).rearrange("p (c t) -> p c t", c=8)[:, :, 0:NT])

        import os as _os
        _kt = _os.environ.get('KTILES', 'ps')
        for ti in range(ntp if 'p' in _kt else 0):
            process(NTT, [(0, 0, NTT)], CH, xp[:, :, ti * NTT:(ti + 1) * NTT], y_d[:, :, ti * NTT:(ti + 1) * NTT])
        if 's' in _kt:
            process(32, [(1, 0, 16), (2, 16, 16)], 16, xs, ys_d)
        S.dma(st_out, V(o_st, L * NSTO))
        S.dma(sT_out, V(o_sT, L * 768))
        S.finish()
        S.emit(es)
    return nc


def _col(v):
    v = np.asarray(v, np.float32).reshape(-1, 128)
    return np.ascontiguousarray(v.T)


def _ffn_base(ctp):
    m, i = divmod(ctp, 2)
    return 128 * m if i == 0 else DFF + 128 * m


def _shared(inp):
    f = lambda k: np.asarray(inp[k], np.float32)
    vecs = np.zeros((L, 128, NV), np.float32)
    smallw = np.zeros((L, 128, NSW), np.float32)
    smallwB = np.zeros((L, 128, NSWB), np.float32)
    for l in range(L):
        v = vecs[l]
        v[:, V_NM:V_NM + 8] = _col(f('norm_mix')[l])
        v[:, V_NF:V_NF + 8] = _col(f('norm_ffn')[l])
        v[:, V_BADA:V_BADA + 48] = _col(f('b_ada')[l])
        for ct in range(2):
            for k in range(4):
                v[:, V_LCW + ct * 4 + k] = f('lru_conv_w')[l, k, ct * 128:(ct + 1) * 128]
        v[:, V_LCB:V_LCB + 2] = _col(f('lru_conv_b')[l])
        v[:, V_LBA:V_LBA + 2] = _col(f('lru_ba')[l])
        v[:, V_LBX:V_LBX + 2] = _col(f('lru_bx')[l])
        v[:, V_LLAM:V_LLAM + 2] = _col(f('lru_lambda')[l])
        v[:, V_MU:V_MU + 14] = _col(f('rwkv_mu')[l])
        for nm, o in (('rwkv_w0', V_W0), ('rwkv_a0', V_A0), ('rwkv_k_k', V_KK), ('rwkv_k_a', V_KA),
                      ('rwkv_r_k', V_RK), ('rwkv_ln_w', V_LNW), ('rwkv_ln_b', V_LNB)):
            v[:, o:o + 4] = _col(f(nm)[l].reshape(-1))
        v[:, V_S5AR:V_S5AR + 8] = _col(f('s5_a_re')[l].reshape(-1))
        v[:, V_S5AI:V_S5AI + 8] = _col(f('s5_a_im')[l].reshape(-1))
        v[:, V_S5DT:V_S5DT + 8] = _col(np.repeat(f('s5_log_dt')[l], 64))
        v[:, V_S5D:V_S5D + 2] = _col(f('s5_d')[l])
        v[:, V_GLUB:V_GLUB + 2] = _col(f('s5_glu_b')[l])
        for ctp in range(44):
            b = _ffn_base(ctp)
            for k in range(3):
                v[:, V_FCW + ctp * 3 + k] = f('ffn_conv_w')[l, k, b:b + 128]
            v[:, V_FCB + ctp] = f('ffn_conv_b')[l, b:b + 128]
        v[:, V_NFIN:V_NFIN + 8] = _col(f('norm_final'))
        w = smallw[l]
        for nm, o in (('lru_wa', W_WA), ('lru_wx', W_WX)):
            for ct in range(2):
                for hh in range(2):
                    w[hh * 64:(hh + 1) * 64, o + ct * 128 + hh * 64:o + ct * 128 + hh * 64 + 64] = f(nm)[l, 2 * ct + hh]
        smallwB[l][0:64, W_LORA:W_LORA + 512] = f('rwkv_w2')[l]
        smallwB[l][64:128, W_LORA:W_LORA + 512] = f('rwkv_a2')[l]
        smallwB[l][:, W_G2:W_G2 + 512] = f('rwkv_g2')[l]
        for nm, o in (('s5_b_re', W_BRE), ('s5_b_im', W_BIM)):
            bsrc = f(nm)[l]
            for st_ in range(8):
                kt, half, par = st_ // 4, (st_ % 4) // 2, st_ % 2
                for gg in (2 * st_, 2 * st_ + 1):
                    p0 = gg * 16 - kt * 128
                    s0 = (gg - 2 * st_) * 64
                    c0 = o + (kt * 2 + par) * 128 + s0
                    w[p0:p0 + 16, c0:c0 + 64] = bsrc[gg].T
        for nm, o in (('s5_c_re', W_CRE), ('s5_c_im', W_CIM)):
            csrc = f(nm)[l]
            for st_ in range(8):
                for gg in (2 * st_, 2 * st_ + 1):
                    p0 = (gg - 2 * st_) * 64
                    c0 = o + st_ * 64 + (gg - 4 * (st_ // 2)) * 16
                    w[p0:p0 + 64, c0:c0 + 16] = csrc[gg].T
        w[:, W_GLU:W_GLU + 512] = f('s5_glu_w')[l].reshape(2, 128, 256).transpose(1, 0, 2).reshape(128, 512)
    consts = np.zeros((128, NCONST), np.float32)
    ii, jj = np.meshgrid(np.arange(128), np.arange(128), indexing='ij')
    consts[:, C_ID:C_ID + 128] = np.eye(128)
    consts[:, C_BONES:C_BONES + 128] = (ii // 64 == jj // 64)
    consts[:, C_ONES:C_ONES + 128] = 1.0
    consts[:, C_MSU:C_MSU + 128] = (ii < jj)
    consts[:, C_MSL:C_MSL + 128] = (jj < ii)
    consts[:, C_MIU:C_MIU + 128] = (ii <= jj)
    consts[:, C_RESET:C_RESET + 512] = (np.arange(512) % 128 != 0)[None, :]
    consts[:, C_NMSU:C_NMSU + 128] = -1.0 * (ii < jj)
    consts[:, C_NMSL:C_NMSL + 128] = -1.0 * (jj < ii)

    def slab(wm, ncol):
        n = ncol // 256
        return np.ascontiguousarray(wm.reshape(8, 128, n, 256).transpose(2, 1, 0, 3).reshape(n, 128, 2048))
    colidx = np.concatenate([np.arange(_ffn_base(c), _ffn_base(c) + 128) for c in range(44)])
    sh = dict(vecs=vecs, smallw=smallw, smallwB=smallwB, consts=consts)
    sh['win'] = np.stack([slab(f('w_in')[l], 2560) for l in range(L)])
    sh['wout'] = np.stack([slab(f('w_out')[l], 1024) for l in range(L)])
    sh['wup'] = np.stack([slab(f('ffn_up')[l][:, colidx], 5632) for l in range(L)])
    sh['wdn'] = np.stack([np.ascontiguousarray(f('ffn_down')[l].reshape(2, 11, 128, 8, 128).transpose(0, 3, 2, 1, 4).reshape(2, 8, 128, 1408))
                          for l in range(L)])
    sh['wada'] = np.stack([slab(f('w_ada')[l], 6144) for l in range(L)])
    return sh


def _core_inputs(inp, c, TP):
    f = lambda k: np.asarray(inp[k], np.float32)
    b, s0, s1 = c // 2, 2 * c, 2 * c + 1
    m = {}
    m['xp'] = np.ascontiguousarray(f('x_prompt')[b].T).reshape(8, 128, TP)
    m['xs'] = np.ascontiguousarray(f('x_sample')[[s0, s1]].reshape(32, D).T).reshape(8, 128, 32)
    cl = np.stack([f('c_prompt')[b], f('c_sample')[s0], f('c_sample')[s1], np.zeros(D, np.float32)])
    m['cT'] = np.ascontiguousarray(cl.reshape(4, 8, 128).transpose(2, 1, 0).reshape(128, 32))
    st = np.zeros((128, L * NSTO), np.float32)
    sT = np.zeros((128, L * 768), np.float32)
    for l in range(L):
        o = l * NSTO
        for seq, s in ((1, s0), (2, s1)):
            for ct in range(2):
                for j in range(3):
                    st[:, o + O_LH + (ct * 3 + seq) * 3 + j] = f('state_lru_conv')[l, s, j, ct * 128:(ct + 1) * 128]
                st[:, o + O_Lh + ct * 3 + seq] = f('state_lru_h')[l, s, ct * 128:(ct + 1) * 128]
            for ct in range(14):
                st[:, o + O_SH + ct * 3 + seq] = f('state_rwkv_shift')[l, s, ct * 128:(ct + 1) * 128]
            for st_ in range(8):
                st[:, o + O_S5R + st_ * 3 + seq] = f('state_s5_re')[l, s].reshape(-1)[st_ * 128:(st_ + 1) * 128]
                st[:, o + O_S5I + st_ * 3 + seq] = f('state_s5_im')[l, s].reshape(-1)[st_ * 128:(st_ + 1) * 128]
            for ctp in range(44):
                bb = _ffn_base(ctp)
                for j in range(2):
                    st[:, o + O_FH + (ctp * 3 + seq) * 2 + j] = f('state_ffn_conv')[l, s, j, bb:bb + 128]
            Sm = f('state_rwkv_S')[l, s].reshape(4, 2, 64, 64).transpose(1, 3, 0, 2).reshape(128, 256)
            sT[:, (l * 3 + seq) * 256:(l * 3 + seq + 1) * 256] = Sm
    m['st_in'] = st
    m['sT_in'] = sT
    return m


_NC_CACHE = {}


def kernel(**inp):
    TP = int(np.asarray(inp['x_prompt']).shape[1])
    ntp = TP // NTT
    import os
    if ntp not in _NC_CACHE:
        _NC_CACHE[ntp] = build(ntp, stage=int(os.environ.get('KSTAGE', '99')), sub=int(os.environ.get('KSUB', '99')), bits=int(os.environ.get('KBITS', '255')))
    nc = _NC_CACHE[ntp]
    sh = _shared(inp)
    in_maps = []
    for c in range(8):
        m = dict(sh)
        m.update(_core_inputs(inp, c, TP))
        in_maps.append(m)
    res = run_bass_kernel_spmd(nc, in_maps, core_ids=list(range(8)))
    R = res.results
    B, SB_ = 4, 16
    y_prompt = np.zeros((B, TP, D), np.float32)
    y_sample = np.zeros((SB_, 16, D), np.float32)

    def mk(nb):
        return [np.zeros((L, nb, 3, 256), np.float32), np.zeros((L, nb, 256), np.float32),
                np.zeros((L, nb, 1792), np.float32), np.zeros((L, nb, 8, 64, 64), np.float32),
                np.zeros((L, nb, 16, 64), np.float32), np.zeros((L, nb, 16, 64), np.float32),
                np.zeros((L, nb, 2, 2 * DFF), np.float32)]
    P, Sg = mk(B), mk(SB_)

    def unpack(dst, bi, st, sT, seq):
        for l in range(L):
            o = l * NSTO
            for ct in range(2):
                for j in range(3):
                    dst[0][l, bi, j, ct * 128:(ct + 1) * 128] = st[:, o + O_LH + (ct * 3 + seq) * 3 + j]
                dst[1][l, bi, ct * 128:(ct + 1) * 128] = st[:, o + O_Lh + ct * 3 + seq]
            for ct in range(14):
                dst[2][l, bi, ct * 128:(ct + 1) * 128] = st[:, o + O_SH + ct * 3 + seq]
            re = np.zeros(1024, np.float32)
            im = np.zeros(1024, np.float32)
            for st_ in range(8):
                re[st_ * 128:(st_ + 1) * 128] = st[:, o + O_S5R + st_ * 3 + seq]
                im[st_ * 128:(st_ + 1) * 128] = st[:, o + O_S5I + st_ * 3 + seq]
            dst[4][l, bi] = re.reshape(16, 64)
            dst[5][l, bi] = im.reshape(16, 64)
            for ctp in range(44):
                bb = _ffn_base(ctp)
                for j in range(2):
                    dst[6][l, bi, j, bb:bb + 128] = st[:, o + O_FH + (ctp * 3 + seq) * 2 + j]
            Sm = sT[:, (l * 3 + seq) * 256:(l * 3 + seq + 1) * 256].reshape(2, 64, 4, 64)
            dst[3][l, bi] = Sm.transpose(2, 0, 3, 1).reshape(8, 64, 64)

    for c in range(8):
        r = R[c]
        b, s0, s1 = c // 2, 2 * c, 2 * c + 1
        ys = np.asarray(r['ys']).reshape(D, 32).T
        y_sample[s0] = ys[0:16]
        y_sample[s1] = ys[16:32]
        st, sT = np.asarray(r['st_out']), np.asarray(r['sT_out'])
        unpack(Sg, s0, st, sT, 1)
        unpack(Sg, s1, st, sT, 2)
        if c % 2 == 0:
            y_prompt[b] = np.asarray(r['y']).reshape(D, TP).T
            unpack(P, b, st, sT, 0)
    return (y_prompt, y_sample, *P, *Sg)
```
